# Optimizing a Trainium2 kernel written in Bass

```python
import math
import jax
import jax.numpy as jnp
from jax import lax
import numpy as np

D_MODEL = 2048
BATCH = 8
SEQ = 2048
DEPTH = 4

RMS_EPS = 1e-6
NEG_INF = -1e30
N_BRANCH = 3
SSM_WIDTH = D_MODEL // 4
SSM_GROUP = 16
SSM_GROUPS = SSM_WIDTH // SSM_GROUP
SSM_STATE = 64
DT_MIN = 1e-3
DT_MAX = 1e-1
HEAD_DIM = 64
DSWA_PATTERNS = ((128, 1), (512, 4), (2048, 16))
ATTN_WIDTH = D_MODEL // 4
HEADS_PER_PATTERN = ATTN_WIDTH // HEAD_DIM
N_ATTN_HEADS = HEADS_PER_PATTERN * len(DSWA_PATTERNS)
QKV_WIDTH = N_ATTN_HEADS * HEAD_DIM
CONV_WIDTH = D_MODEL // 4
CONV_K = 3
D_FF = 256 * ((8 * D_MODEL // 3 + 255) // 256)
FFN_CONV_K = 3
OFF_Q = SSM_WIDTH
OFF_K = OFF_Q + QKV_WIDTH
OFF_V = OFF_K + QKV_WIDTH
OFF_CONV = OFF_V + QKV_WIDTH
OFF_GATE = OFF_CONV + 3 * CONV_WIDTH
N_IN = OFF_GATE + N_BRANCH * D_MODEL

kernel_name = 'hybrid_ssm_dilated_attn_conv_trunk'


def alibi_slopes(n_heads):
    return np.array([2.0 ** (-8.0 * (h + 1) / n_heads) for h in range(n_heads)], dtype=np.float32)


def rms_norm(x, g):
    x32 = x.astype(jnp.float32)
    y = x32 * lax.rsqrt(jnp.mean(x32 * x32, axis=-1, keepdims=True) + RMS_EPS)
    return y.astype(x.dtype) * g


def causal_dwconv(z, w):
    k_width = w.shape[0]
    seq = z.shape[1]
    zp = jnp.pad(z, ((0, 0), (k_width - 1, 0), (0, 0)))
    return sum(w[k] * zp[:, k_width - 1 - k:k_width - 1 - k + seq] for k in range(k_width))


def s5_mixer(u, log_dt, a_re, a_im, b_re, b_im, c_re, c_im, d_skip, w_glu, b_glu):
    f32 = jnp.float32
    bsz, seq, _ = u.shape
    u32 = u.astype(f32)
    ug = u32.reshape(bsz, seq, SSM_GROUPS, SSM_GROUP)
    dt = jnp.exp(log_dt.astype(f32))[:, None]
    ar, ai = a_re.astype(f32), a_im.astype(f32)
    mag = jnp.exp(ar * dt)
    lr, li = mag * jnp.cos(ai * dt), mag * jnp.sin(ai * dt)
    den = ar * ar + ai * ai
    fr = ((lr - 1.0) * ar + li * ai) / den
    fi = (li * ar - (lr - 1.0) * ai) / den
    br, bi = b_re.astype(f32), b_im.astype(f32)
    bbr = fr[..., None] * br - fi[..., None] * bi
    bbi = fr[..., None] * bi + fi[..., None] * br
    xr = jnp.einsum('blgc,gnc->blgn', ug, bbr)
    xi = jnp.einsum('blgc,gnc->blgn', ug, bbi)
    lam_r = jnp.broadcast_to(lr, xr.shape)
    lam_i = jnp.broadcast_to(li, xr.shape)

    def combine(e1, e2):
        a1r, a1i, b1r, b1i = e1
        a2r, a2i, b2r, b2i = e2
        return (a2r * a1r - a2i * a1i,
                a2r * a1i + a2i * a1r,
                a2r * b1r - a2i * b1i + b2r,
                a2r * b1i + a2i * b1r + b2i)

    _, _, hr, hi = lax.associative_scan(combine, (lam_r, lam_i, xr, xi), axis=1)
    y = (jnp.einsum('blgn,gcn->blgc', hr, c_re.astype(f32))
         - jnp.einsum('blgn,gcn->blgc', hi, c_im.astype(f32)))
    y = y.reshape(bsz, seq, SSM_WIDTH) + d_skip.astype(f32) * u32
    g = jax.nn.gelu(y).astype(u.dtype)
    return g * jax.nn.sigmoid(g @ w_glu + b_glu)


def dilated_window_attention(q, k, v, slopes, window, dilation):
    f32 = jnp.float32
    bsz, seq, n_h, e = q.shape
    w_steps = window // dilation
    qb_len = w_steps
    unit = dilation * qb_len
    seq_p = -(-seq // unit) * unit
    pad = seq_p - seq
    m_len = seq_p // dilation
    nb = m_len // qb_len

    def to_blocks(t):
        t = jnp.pad(t, ((0, 0), (0, pad), (0, 0), (0, 0)))
        t = t.reshape(bsz, m_len, dilation, n_h, e).transpose(0, 2, 1, 3, 4)
        return t.reshape(bsz, dilation, nb, qb_len, n_h, e)

    def with_prev(t):
        prev = jnp.pad(t[:, :, :-1], ((0, 0), (0, 0), (1, 0), (0, 0), (0, 0), (0, 0)))
        return jnp.concatenate([prev, t], axis=3)

    qb = to_blocks(q)
    kk = with_prev(to_blocks(k))
    vv = with_prev(to_blocks(v))
    s = jnp.einsum('brnqhe,brnkhe->brnhqk', qb, kk).astype(f32) * (e ** -0.5)
    qi = jnp.arange(qb_len)[:, None]
    kj = jnp.arange(2 * qb_len)[None, :]
    dist = qb_len + qi - kj
    blk = jnp.arange(nb)[:, None, None]
    valid = (dist >= 0) & (dist <= w_steps) & ((blk > 0) | (kj >= qb_len))
    bias = -slopes[:, None, None] * (dist * dilation).astype(f32)
    s = jnp.where(valid[:, None], s + bias, NEG_INF)
    m = jnp.max(s, axis=-1, keepdims=True)
    p = jnp.exp(s - m)
    l = jnp.sum(p, axis=-1, keepdims=True)
    o = jnp.einsum('brnhqk,brnkhe->brnqhe', p / l, vv.astype(f32))
    lse = (m + jnp.log(l))[..., 0]
    o = o.reshape(bsz, dilation, m_len, n_h, e).transpose(0, 2, 1, 3, 4).reshape(bsz, seq_p, n_h, e)
    lse = lse.transpose(0, 1, 2, 4, 3).reshape(bsz, dilation, m_len, n_h)
    lse = lse.transpose(0, 2, 1, 3).reshape(bsz, seq_p, n_h)
    return o[:, :seq], lse[:, :seq]


def dilated_attention_mixer(q, k, v):
    bsz, seq = q.shape[0], q.shape[1]
    slopes = jnp.asarray(alibi_slopes(N_ATTN_HEADS))
    outs, lses = [], []
    for g, (window, dilation) in enumerate(DSWA_PATTERNS):
        hs = slice(g * HEADS_PER_PATTERN, (g + 1) * HEADS_PER_PATTERN)
        o, lse = dilated_window_attention(q[:, :, hs], k[:, :, hs], v[:, :, hs], slopes[hs], window, dilation)
        outs.append(o)
        lses.append(lse)
    alpha = jax.nn.softmax(jnp.stack(lses, axis=0), axis=0)
    o = jnp.sum(alpha[..., None] * jnp.stack(outs, axis=0), axis=0)
    return o.reshape(bsz, seq, ATTN_WIDTH).astype(q.dtype)


def hybrid_mixer(h, w_in, ssm_log_dt, ssm_a_re, ssm_a_im, ssm_b_re, ssm_b_im, ssm_c_re, ssm_c_im,
                 ssm_d, w_glu, b_glu, conv_mix_w, w_ssm_out, w_attn_out, w_conv_out, b_gate, w_o):
    bsz, seq, _ = h.shape
    proj = h @ w_in
    u = proj[..., :OFF_Q]
    q = proj[..., OFF_Q:OFF_K].reshape(bsz, seq, N_ATTN_HEADS, HEAD_DIM)
    k = proj[..., OFF_K:OFF_V].reshape(bsz, seq, N_ATTN_HEADS, HEAD_DIM)
    v = proj[..., OFF_V:OFF_CONV].reshape(bsz, seq, N_ATTN_HEADS, HEAD_DIM)
    conv_b, conv_c, conv_h = jnp.split(proj[..., OFF_CONV:OFF_GATE], 3, axis=-1)
    gates = jax.nn.sigmoid(proj[..., OFF_GATE:] + b_gate).reshape(bsz, seq, N_BRANCH, D_MODEL)
    y_ssm = s5_mixer(u, ssm_log_dt, ssm_a_re, ssm_a_im, ssm_b_re, ssm_b_im, ssm_c_re, ssm_c_im,
                     ssm_d, w_glu, b_glu) @ w_ssm_out
    y_attn = dilated_attention_mixer(q, k, v) @ w_attn_out
    y_conv = (conv_b * causal_dwconv(conv_c * conv_h, conv_mix_w)) @ w_conv_out
    merged = gates[:, :, 0] * y_ssm + gates[:, :, 1] * y_attn + gates[:, :, 2] * y_conv
    return merged @ w_o


def conv_ffn(h, w_up, ffn_conv_w, w_down):
    up = causal_dwconv(h @ w_up, ffn_conv_w)
    a, b = jnp.split(up, 2, axis=-1)
    return (jax.nn.silu(a) * b) @ w_down


def setup_inputs(seed: int = 0) -> dict:
    key = jax.random.key(seed)
    ks = jax.random.split(key, 32)
    f32 = jnp.float32

    def nrm(k, shape, scale):
        return scale * jax.random.normal(k, shape, f32)

    nl = DEPTH
    n_idx = jnp.arange(SSM_STATE, dtype=f32)
    return {
        'x': nrm(ks[0], (BATCH, SEQ, D_MODEL), 1.0),
        'c': nrm(ks[1], (BATCH, D_MODEL), 1.0),
        'w_mod': nrm(ks[2], (nl, D_MODEL, 6 * D_MODEL), 0.5 * D_MODEL ** -0.5),
        'b_mod': nrm(ks[3], (nl, 6 * D_MODEL), 0.01),
        'g_pre_mix': 1.0 + nrm(ks[4], (nl, D_MODEL), 0.02),
        'g_post_mix': 1.0 + nrm(ks[5], (nl, D_MODEL), 0.02),
        'g_pre_ffn': 1.0 + nrm(ks[6], (nl, D_MODEL), 0.02),
        'g_post_ffn': 1.0 + nrm(ks[7], (nl, D_MODEL), 0.02),
        'w_in': nrm(ks[8], (nl, D_MODEL, N_IN), D_MODEL ** -0.5),
        'ssm_log_dt': jax.random.uniform(ks[9], (nl, SSM_GROUPS), f32, math.log(DT_MIN), math.log(DT_MAX)),
        'ssm_a_re': -0.5 + nrm(ks[10], (nl, SSM_GROUPS, SSM_STATE), 0.01),
        'ssm_a_im': jnp.pi * n_idx + nrm(ks[11], (nl, SSM_GROUPS, SSM_STATE), 0.01),
        'ssm_b_re': nrm(ks[12], (nl, SSM_GROUPS, SSM_STATE, SSM_GROUP), (2 * SSM_GROUP) ** -0.5),
        'ssm_b_im': nrm(ks[13], (nl, SSM_GROUPS, SSM_STATE, SSM_GROUP), (2 * SSM_GROUP) ** -0.5),
        'ssm_c_re': nrm(ks[14], (nl, SSM_GROUPS, SSM_GROUP, SSM_STATE), 0.5),
        'ssm_c_im': nrm(ks[15], (nl, SSM_GROUPS, SSM_GROUP, SSM_STATE), 0.5),
        'ssm_d': nrm(ks[16], (nl, SSM_WIDTH), 1.0),
        'w_glu': nrm(ks[17], (nl, SSM_WIDTH, SSM_WIDTH), SSM_WIDTH ** -0.5),
        'b_glu': nrm(ks[18], (nl, SSM_WIDTH), 0.01),
        'conv_mix_w': nrm(ks[19], (nl, CONV_K, CONV_WIDTH), CONV_K ** -0.5),
        'w_ssm_out': nrm(ks[20], (nl, SSM_WIDTH, D_MODEL), SSM_WIDTH ** -0.5),
        'w_attn_out': nrm(ks[21], (nl, ATTN_WIDTH, D_MODEL), ATTN_WIDTH ** -0.5),
        'w_conv_out': nrm(ks[22], (nl, CONV_WIDTH, D_MODEL), CONV_WIDTH ** -0.5),
        'b_gate': nrm(ks[23], (nl, N_BRANCH * D_MODEL), 0.01),
        'w_o': nrm(ks[24], (nl, D_MODEL, D_MODEL), D_MODEL ** -0.5),
        'w_up': nrm(ks[25], (nl, D_MODEL, 2 * D_FF), D_MODEL ** -0.5),
        'ffn_conv_w': nrm(ks[26], (nl, FFN_CONV_K, 2 * D_FF), FFN_CONV_K ** -0.5),
        'w_down': nrm(ks[27], (nl, D_FF, D_MODEL), D_FF ** -0.5),
    }


def reference(x, c, w_mod, b_mod, g_pre_mix, g_post_mix, g_pre_ffn, g_post_ffn, w_in,
              ssm_log_dt, ssm_a_re, ssm_a_im, ssm_b_re, ssm_b_im, ssm_c_re, ssm_c_im, ssm_d,
              w_glu, b_glu, conv_mix_w, w_ssm_out, w_attn_out, w_conv_out, b_gate, w_o,
              w_up, ffn_conv_w, w_down):
    cond = jax.nn.silu(c)
    for l in range(DEPTH):
        mod = cond @ w_mod[l] + b_mod[l]
        sh1, sc1, gt1, sh2, sc2, gt2 = jnp.split(mod, 6, axis=-1)
        h = rms_norm(x, g_pre_mix[l]) * (1.0 + sc1[:, None]) + sh1[:, None]
        y = hybrid_mixer(h, w_in[l], ssm_log_dt[l], ssm_a_re[l], ssm_a_im[l], ssm_b_re[l], ssm_b_im[l],
                         ssm_c_re[l], ssm_c_im[l], ssm_d[l], w_glu[l], b_glu[l], conv_mix_w[l],
                         w_ssm_out[l], w_attn_out[l], w_conv_out[l], b_gate[l], w_o[l])
        x = x + gt1[:, None] * rms_norm(y, g_post_mix[l])
        h = rms_norm(x, g_pre_ffn[l]) * (1.0 + sc2[:, None]) + sh2[:, None]
        y = conv_ffn(h, w_up[l], ffn_conv_w[l], w_down[l])
        x = x + gt2[:, None] * rms_norm(y, g_post_ffn[l])
    return x
```

```python
import os
import numpy as np
import concourse.bass as bass
import concourse.mybir as mybir
from concourse.bass_utils import run_bass_kernel_spmd
from contextlib import ExitStack

F32 = mybir.dt.float32
BF16 = mybir.dt.bfloat16
AF = mybir.ActivationFunctionType
ALU = mybir.AluOpType

D = 2048
L = 2048
KT = 16
NL = 4
N_IN = 12800
DFF = 5632
NCH = 4
TC = 512
PI = float(np.pi)

ENGS = ("pe", "act", "dve", "pool", "sp")
DMA_K = 8


class Op:
    __slots__ = ("eng", "fn", "deps", "is_dma", "signal", "sigval", "dsem", "dval", "dprev")

    def __init__(self, eng, fn, is_dma):
        self.eng = eng
        self.fn = fn
        self.is_dma = is_dma
        self.deps = ()
        self.signal = False
        self.sigval = 0
        self.dsem = None
        self.dval = 0
        self.dprev = None


class Sched:
    def __init__(self, same_engine_sync=True):
        self.ops = []
        self.lw = {}
        self.rd = {}
        self.same = same_engine_sync
        self.last_on = {e: None for e in ENGS}
        self.last_dmas = {e: [] for e in ENGS}
        self.barrier_deps = ()
        self._fresh = {e: False for e in ENGS}
        self.marks = []
        self.npe = 0

    def mark(self, name):
        self.marks.append((name, self.npe))

    def add(self, eng, fn, reads=(), writes=(), dma=False):
        op = Op(eng, fn, dma)
        if eng == "pe":
            self.npe += 1
        deps = set(self.barrier_deps) if self._fresh[eng] else set()
        self._fresh[eng] = False
        lw, rd = self.lw, self.rd
        for k in reads:
            w = lw.get(k)
            if w is not None:
                deps.add(w)
        for k in writes:
            w = lw.get(k)
            if w is not None:
                deps.add(w)
            r = rd.get(k)
            if r:
                deps.update(r)
        for k in reads:
            lst = rd.get(k)
            if lst is None:
                rd[k] = [op]
            elif (not dma) and lst and lst[-1].eng == eng and not lst[-1].is_dma:
                lst[-1] = op
            else:
                lst.append(op)
        for k in writes:
            lw[k] = op
            rd[k] = []
        deps.discard(op)
        op.deps = deps
        self.ops.append(op)
        if dma:
            ld = self.last_dmas[eng]
            ld.append(op)
            if len(ld) > DMA_K:
                ld.pop(0)
        else:
            self.last_on[eng] = op
        return op

    def barrier(self):
        deps = [o for o in self.last_on.values() if o is not None]
        for e in ENGS:
            deps.extend(self.last_dmas[e])
        self.barrier_deps = tuple(deps)
        self._fresh = {e: True for e in ENGS}
        self.lw = {}
        self.rd = {}

    def prepare(self, sems, dma_sems):
        ops = self.ops
        same = self.same
        for op in ops:
            for d in op.deps:
                if d.is_dma:
                    continue
                if d.eng == op.eng and not op.is_dma and (d.eng == "pe" or not same):
                    continue
                d.signal = True
        cnt = {e: 0 for e in ENGS}
        dcnt = {e: 0 for e in ENGS}
        hist = {e: [] for e in ENGS}
        for op in ops:
            if op.is_dma:
                n = dcnt[op.eng]
                dcnt[op.eng] = n + 1
                op.dsem = dma_sems[op.eng][n % DMA_K]
                op.dval = 16 * (n // DMA_K + 1)
                h = hist[op.eng]
                op.dprev = h[n - DMA_K] if n >= DMA_K else None
                h.append(op)
            elif op.signal:
                cnt[op.eng] += 1
                op.sigval = cnt[op.eng]
        self.by_eng = {e: [] for e in ENGS}
        for op in ops:
            self.by_eng[op.eng].append(op)
        self.sems = sems

    def run(self, eng_name, e):
        waited = {}
        sems = self.sems
        same = self.same
        for op in self.by_eng[eng_name]:
            need = {}
            deps = list(op.deps)
            if op.is_dma and op.dprev is not None:
                deps.append(op.dprev)
            for d in deps:
                if d.is_dma:
                    s, v = d.dsem, d.dval
                else:
                    if d.eng == eng_name and not op.is_dma and (eng_name == "pe" or not same):
                        continue
                    if not d.signal:
                        continue
                    s, v = sems[d.eng], d.sigval
                if waited.get(s, 0) >= v:
                    continue
                if need.get(s, 0) < v:
                    need[s] = v
            for s, v in need.items():
                e.wait_ge(s, v)
                waited[s] = v
            ins = op.fn(e)
            if op.is_dma:
                ins.then_inc(op.dsem, 16)
            elif op.signal:
                ins.then_inc(sems[eng_name], 1)
        last = {}
        for op in self.by_eng[eng_name]:
            if op.is_dma:
                last[op.dsem] = op.dval
        for s, v in last.items():
            e.wait_ge(s, v)


def alibi_slopes(n_heads):
    return np.array([2.0 ** (-8.0 * (h + 1) / n_heads) for h in range(n_heads)], dtype=np.float64)


def _bf16_round(x):
    u = np.asarray(x, np.float32).view(np.uint32).astype(np.uint64)
    r = ((u + 0x7FFF + ((u >> 16) & 1)) & 0xFFFF0000).astype(np.uint32)
    return r.view(np.float32)


def make_bias_tables():
    slopes = alibi_slopes(24)
    dil = [1, 4, 16]
    kj = np.arange(128)[:, None]
    qi = np.arange(128)[None, :]
    tab = np.zeros((128, 24, 256), np.float64)
    NEG = -240000.0
    for h in range(24):
        a = slopes[h] * dil[h // 8] * 8.0
        prev = np.where(qi <= kj, -a * (128 + qi - kj), NEG)
        cur = np.where(qi >= kj, -a * (qi - kj), NEG)
        tab[:, h, 0:128] = prev
        tab[:, h, 128:256] = cur
    hi = _bf16_round(tab.astype(np.float32))
    lo = _bf16_round((tab - hi.astype(np.float64)).astype(np.float32))
    return hi.reshape(128, 24 * 256), lo.reshape(128, 24 * 256)


def build_program(n_layers=NL, debug=False):
    nc = bass.Bass("TRN2", target_bir_lowering=False)
    S = Sched()

    def din(name, shape, dt=F32):
        return nc.dram_tensor(name, list(shape), dt, kind="ExternalInput").ap()

    def dscr(name, shape, dt):
        return nc.dram_tensor(name, list(shape), dt, kind=("ExternalOutput" if debug else "Internal")).ap()

    xT_in = din("xT", [D, L])
    cT_in = din("cT", [128, 16])
    w_mod = din("w_mod", [NL, D, 6 * D])
    w_in = din("w_in", [NL, D, N_IN])
    w_glu = din("w_glu", [NL, 512, 512])
    w_ssm_out = din("w_ssm_out", [NL, 512, D])
    w_attn_out = din("w_attn_out", [NL, 512, D])
    w_conv_out = din("w_conv_out", [NL, 512, D])
    w_o = din("w_o", [NL, D, D])
    w_up = din("w_up", [NL, D, 2 * DFF])
    w_down = din("w_down", [NL, DFF, D])
    bmodT_in = din("bmodT", [NL, 128, 96])
    gains_in = din("gains", [NL, 128, 64])
    bgateT_in = din("bgateT", [NL, 128, 48])
    ssm_sm_in = din("ssm_sm", [NL, 128, 48])
    ssm_flat_in = din("ssm_flat", [NL, 3, 2048])
    BTre_in = din("BTre", [NL, 128, 2048])
    BTim_in = din("BTim", [NL, 128, 2048])
    CTre_in = din("CTre", [NL, 128, 2048])
    CTim_in = din("CTim", [NL, 128, 2048])
    dT_in = din("dT", [NL, 128, 4])
    bgluT_in = din("bgluT", [NL, 128, 4])
    convwT_in = din("convwT", [NL, 128, 12])
    ffnwT_in = din("ffnwT", [NL, 128, 264])
    iota_in = din("iota", [128, 512])
    ones_in = din("ones", [128, 128])
    bias_hi_in = din("bias_hi", [128, 24 * 256])
    bias_lo_in = din("bias_lo", [128, 24 * 256])

    outT = nc.dram_tensor("outT", [D, L], F32, kind="ExternalOutput").ap()
    XT = dscr("XT", [D, L], F32)
    YT = dscr("YT", [D, L], F32)
    PROJ = dscr("PROJ", [N_IN, L], BF16)
    VTOK = dscr("VTOK", [L, 1536], BF16)
    GS = dscr("GS", [DFF, L], BF16)

    st = ExitStack()
    with st:
        def sbt(name, shape, dt):
            return st.enter_context(nc.sbuf_tensor("sb_" + name, list(shape), dt))

        R1 = sbt("R1", [128, 32768], BF16)
        WSall = sbt("WS", [128, 24576], BF16)
        WSa = [WSall[:, 8192 * i:8192 * (i + 1)] for i in range(3)]
        BRa = sbt("BR", [128, 24576], BF16)
        Ta = sbt("T", [128, 12288], BF16)
        ACC = sbt("ACC", [128, 2048], F32)
        RSTDY = sbt("RSTDY", [128, 2048], F32)
        PS = st.enter_context(nc.psum_tensor("PS", [128, 4096], F32))
        iota = sbt("iota", [128, 512], F32)
        ones32 = sbt("ones32", [128, 128], F32)
        onesbf = sbt("onesbf", [128, 128], BF16)
        epsT = sbt("epsT", [128, 1], F32)
        condT = sbt("condT", [128, 16], BF16)
        cTs = sbt("cTs", [128, 16], F32)
        modT = sbt("modT", [128, 96], F32)
        bmodT = sbt("bmodT", [128, 96], F32)
        gains = sbt("gains", [128, 64], F32)
        AG = sbt("AG", [128, 64], F32)
        bgateT = sbt("bgateT", [128, 48], F32)
        ssm_sm = sbt("ssm_sm", [128, 48], F32)
        sm_r = sbt("sm_r", [128, 16], F32)
        sm_th = sbt("sm_th", [128, 16], F32)
        sm_cT = sbt("sm_cT", [128, 16], F32)
        sm_sT = sbt("sm_sT", [128, 16], F32)
        sm_tmp = sbt("sm_tmp", [128, 96], F32)
        carry = sbt("carry", [128, 8], F32)
        dTt = sbt("dTt", [128, 4], F32)
        bgluT = sbt("bgluT", [128, 4], F32)
        convwT = sbt("convwT", [128, 12], F32)
        ffnwT = sbt("ffnwT", [128, 264], F32)

        def psb(b, n=1):
            return PS[:, 512 * b:512 * (b + n)]

        def pk(b, n=1):
            return ["ps%d" % i for i in range(b, b + n)]

        class Arena:
            def __init__(self, t, size, name):
                self.t = t
                self.size = size
                self.off = 0
                self.name = name

            def reset(self):
                self.off = 0

            def bf(self, n):
                assert self.off + n <= self.size, (self.name, self.off, n, self.size)
                a = self.t[:, self.off:self.off + n]
                self.off += n
                return a

            def f32(self, n):
                return self.bf(2 * n).bitcast(F32)

        aR1 = Arena(R1, 32768, "R1")
        aBR = Arena(BRa, 24576, "BR")
        aT = Arena(Ta, 12288, "T")

        def reset_arenas():
            aR1.reset()
            aBR.reset()
            aT.reset()

        def A(eng, fn, r=(), w=(), dma=False):
            S.add(eng, fn, r, w, dma)

        def dma_sp(out, in_, r=(), w=()):
            A("sp", lambda e: e.dma_start(out=out, in_=in_), r, w, True)

        def dma_pool(out, in_, r=(), w=()):
            A("pool", lambda e: e.dma_start(out=out, in_=in_), r, w, True)

        dma_sp(iota[:], iota_in[:, :], w=["iota"])
        dma_sp(ones32[:], ones_in[:, :], w=["ones32"])
        dma_pool(onesbf[:], ones_in[:, :], w=["onesbf"])
        dma_sp(cTs[:], cT_in[:, :], w=["cTs"])
        A("dve", lambda e: e.memset(epsT[:], 1e-6), w=["epsT"])
        A("act", lambda e: e.activation(out=condT[:], in_=cTs[:], func=AF.Silu), r=["cTs"], w=["condT"])
        S.barrier()

        def sincos(x_ap, n, out_sin, out_cos, tmpk, tmpp, keyp):
            M = 12582912.0
            C1 = 6.28125
            C2 = float(2 * np.pi - 6.28125)
            for which, outp, shift in (("s", out_sin, 0.0), ("c", out_cos, PI / 2)):
                kk = keyp + which
                A("dve", lambda e, shift=shift: e.tensor_scalar(out=tmpp, in0=x_ap, scalar1=shift, scalar2=None, op0=ALU.add), r=[keyp + "x"], w=[keyp + "p"])
                A("dve", lambda e: e.tensor_scalar(out=tmpk, in0=tmpp, scalar1=float(1 / (2 * np.pi)), scalar2=M, op0=ALU.mult, op1=ALU.add), r=[keyp + "p"], w=[keyp + "k"])
                A("dve", lambda e: e.tensor_scalar(out=tmpk, in0=tmpk, scalar1=-M, scalar2=None, op0=ALU.add), r=[keyp + "k"], w=[keyp + "k"])
                A("dve", lambda e: e.scalar_tensor_tensor(out=tmpp, in0=tmpk, scalar=-C1, in1=tmpp, op0=ALU.mult, op1=ALU.add), r=[keyp + "k", keyp + "p"], w=[keyp + "p"])
                A("dve", lambda e: e.scalar_tensor_tensor(out=tmpp, in0=tmpk, scalar=-C2, in1=tmpp, op0=ALU.mult, op1=ALU.add), r=[keyp + "k", keyp + "p"], w=[keyp + "p"])
                A("dve", lambda e: e.tensor_scalar(out=tmpp, in0=tmpp, scalar1=PI, scalar2=-PI, op0=ALU.min, op1=ALU.max), r=[keyp + "p"], w=[keyp + "p"])
                A("act", lambda e, outp=outp: e.activation(out=outp, in_=tmpp, func=AF.Sin), r=[keyp + "p"], w=[kk])

        def load_w(slot, dram_ap3, nkt, ncol, key):
            v = WSa[slot][:, 0:nkt * ncol].rearrange("p (k n) -> p k n", k=nkt)
            dma_pool(v, dram_ap3, w=[key])
            return v

        def phase_mod(l):
            S.mark("phase_mod")
            reset_arenas()
            dma_sp(bmodT[:], bmodT_in[l], w=["bmodT"])
            dma_sp(gains[:], gains_in[l], w=["gains"])
            dma_sp(bgateT[:], bgateT_in[l], w=["bgateT"])
            wv = w_mod[l].rearrange("(kt p) n -> p kt n", p=128)
            for cg in range(24):
                slot = cg % 3
                W = load_w(slot, wv[:, :, 512 * cg:512 * (cg + 1)], 16, 512, "WS%d" % slot)
                for j in range(4):
                    ct = 4 * cg + j
                    for kt in range(16):
                        A("pe", lambda e, W=W, j=j, kt=kt, ct=ct: e.matmul(PS[:, 1024 + ct:1025 + ct], W[:, kt, 128 * j:128 * (j + 1)], condT[:, kt:kt + 1], start=(kt == 0), stop=(kt == 15)),
                          r=["WS%d" % slot, "condT"], w=["ps2"])
            A("dve", lambda e: e.tensor_tensor(out=modT[:], in0=PS[:, 1024:1120], in1=bmodT[:], op=ALU.add), r=["ps2", "bmodT"], w=["modT"])
            A("dve", lambda e: e.scalar_tensor_tensor(out=AG[:, 0:16], in0=modT[:, 16:32], scalar=1.0, in1=gains[:, 0:16], op0=ALU.add, op1=ALU.mult), r=["modT", "gains"], w=["AG"])
            A("dve", lambda e: e.tensor_tensor(out=AG[:, 16:32], in0=modT[:, 32:48], in1=gains[:, 16:32], op=ALU.mult), r=["modT", "gains"], w=["AG"])
            A("dve", lambda e: e.scalar_tensor_tensor(out=AG[:, 32:48], in0=modT[:, 64:80], scalar=1.0, in1=gains[:, 32:48], op0=ALU.add, op1=ALU.mult), r=["modT", "gains"], w=["AG"])
            A("dve", lambda e: e.tensor_tensor(out=AG[:, 48:64], in0=modT[:, 80:96], in1=gains[:, 48:64], op=ALU.mult), r=["modT", "gains"], w=["AG"])
            S.barrier()

        def phase_norm(src_x, has_y, Gap, Aap, Bap, dst_x, make_h):
            S.mark("phase_norm")
            reset_arenas()
            hT = aR1.bf(32768).rearrange("p (k t) -> p k t", k=16)
            XN = aBR.f32(8192).rearrange("p (k t) -> p k t", k=16)
            xt = [aT.f32(512) for _ in range(2)]
            yt = [aT.f32(512) for _ in range(2)]
            tmp = [aT.f32(512) for _ in range(2)]
            sq = [aT.bf(512) for _ in range(2)]
            rt = aT.f32(512)
            rstd = aT.f32(512)
            for c in range(NCH):
                cs = slice(TC * c, TC * (c + 1))
                pb = c % 2
                for ft in range(16):
                    b = ft % 2
                    rows = slice(128 * ft, 128 * (ft + 1))
                    kx = "XN%d" % ft
                    if has_y:
                        dma_sp(xt[b], src_x[rows, cs], w=["xt%d" % b])
                        dma_sp(yt[b], YT[rows, cs], w=["yt%d" % b])
                        A("dve", lambda e, b=b, cs=cs: e.tensor_tensor(out=tmp[b], in0=yt[b], in1=RSTDY[:, cs], op=ALU.mult), r=["yt%d" % b, "RSTDY"], w=["tmp%d" % b])
                        A("dve", lambda e, b=b, ft=ft: e.scalar_tensor_tensor(out=XN[:, ft, :], in0=tmp[b], scalar=Gap[:, ft:ft + 1], in1=xt[b], op0=ALU.mult, op1=ALU.add),
                          r=["tmp%d" % b, "xt%d" % b, "AG"], w=[kx])
                    else:
                        dma_sp(XN[:, ft, :], src_x[rows, cs], w=[kx])
                    if dst_x is not None:
                        dma_sp(dst_x[rows, cs], XN[:, ft, :], r=[kx])
                    if make_h:
                        A("act", lambda e, b=b, ft=ft: e.activation(out=sq[b], in_=XN[:, ft, :], func=AF.Square), r=[kx], w=["sq%d" % b])
                        A("pe", lambda e, b=b, ft=ft, pb=pb: e.matmul(psb(pb), onesbf[:], sq[b], start=(ft == 0), stop=(ft == 15)), r=["sq%d" % b, "onesbf"], w=pk(pb))
                if make_h:
                    A("act", lambda e, pb=pb: e.activation(out=rt, in_=psb(pb), func=AF.Sqrt, bias=epsT[:, 0:1], scale=1.0 / D), r=pk(pb) + ["epsT"], w=["rt"])
                    A("dve", lambda e: e.reciprocal(out=rstd, in_=rt), r=["rt"], w=["rstd"])
                    for ft in range(16):
                        b = ft % 2
                        A("dve", lambda e, b=b, ft=ft: e.tensor_tensor(out=tmp[b], in0=XN[:, ft, :], in1=rstd, op=ALU.mult), r=["XN%d" % ft, "rstd"], w=["tmp%d" % b])
                        A("act", lambda e, b=b, ft=ft, cs=cs: e.activation(out=hT[:, ft, cs], in_=tmp[b], func=AF.Identity, bias=Bap[:, ft:ft + 1], scale=Aap[:, ft:ft + 1]),
                          r=["tmp%d" % b, "AG", "modT"], w=["hT%d" % ft])
            S.barrier()

        half_ctr = [0]

        def phase_inproj(l):
            S.mark("phase_inproj")
            reset_arenas()
            hT = aR1.bf(32768).rearrange("p (k t) -> p k t", k=16)
            stg = [aT.bf(2048) for _ in range(2)]
            vst = [aT.bf(512) for _ in range(2)]
            wv = w_in[l].rearrange("(kt p) n -> p kt n", p=128)
            hkeys = ["hT%d" % k for k in range(16)]
            vb = 0
            ev = 0
            for cg in range(25):
                slot = cg % 3
                wk = "WS%d" % slot
                W = load_w(slot, wv[:, :, 512 * cg:512 * (cg + 1)], 16, 512, wk)
                if 7 <= cg <= 9:
                    for tt in range(16):
                        bank = vb % 8
                        vb += 1
                        for kt in range(16):
                            A("pe", lambda e, W=W, kt=kt, tt=tt, bank=bank: e.matmul(psb(bank), hT[:, kt, 128 * tt:128 * (tt + 1)], W[:, kt, :], start=(kt == 0), stop=(kt == 15)),
                              r=[wk, "hT%d" % kt], w=pk(bank))
                        b = tt % 2
                        if tt % 2 == 0:
                            A("act", lambda e, b=b, bank=bank: e.activation(out=vst[b], in_=psb(bank), func=AF.Copy), r=pk(bank), w=["vst%d" % b])
                        else:
                            A("dve", lambda e, b=b, bank=bank: e.tensor_copy(out=vst[b], in_=psb(bank)), r=pk(bank), w=["vst%d" % b])
                        dma_sp(VTOK[128 * tt:128 * (tt + 1), 512 * (cg - 7):512 * (cg - 6)], vst[b], r=["vst%d" % b])
                    continue
                for j in range(4):
                    pt = 4 * cg + j
                    hb = 4 * (half_ctr[0] % 2)
                    half_ctr[0] += 1
                    for c in range(NCH):
                        for kt in range(16):
                            A("pe", lambda e, W=W, kt=kt, j=j, c=c, hb=hb: e.matmul(psb(hb + c), W[:, kt, 128 * j:128 * (j + 1)], hT[:, kt, TC * c:TC * (c + 1)], start=(kt == 0), stop=(kt == 15)),
                              r=[wk, "hT%d" % kt], w=pk(hb + c))
                    b = ev % 2
                    ev += 1
                    src = psb(hb, 4)
                    dst = stg[b]
                    sk = "stg%d" % b
                    if cg >= 13:
                        gi_ = pt - 52
                        A("act", lambda e, src=src, dst=dst, gi_=gi_: e.activation(out=dst, in_=src, func=AF.Sigmoid, bias=bgateT[:, gi_:gi_ + 1]), r=pk(hb, 4) + ["bgateT"], w=[sk])
                    elif cg in (2, 5, 3, 6):
                        dil = 4 if cg in (2, 5) else 16
                        A("dve", lambda e, src=src, dst=dst, dil=dil: e.tensor_copy(out=dst.rearrange("p (r m) -> p r m", r=dil), in_=src.rearrange("p (m r) -> p r m", r=dil)), r=pk(hb, 4), w=[sk])
                    elif ev % 2 == 0:
                        A("act", lambda e, src=src, dst=dst: e.activation(out=dst, in_=src, func=AF.Copy), r=pk(hb, 4), w=[sk])
                    else:
                        A("dve", lambda e, src=src, dst=dst: e.tensor_copy(out=dst, in_=src), r=pk(hb, 4), w=[sk])
                    dma_sp(PROJ[128 * pt:128 * (pt + 1), :], dst, r=[sk], w=["PROJ%d" % pt])
            S.barrier()

        def phase_ssm(l):
            S.mark("phase_ssm")
            reset_arenas()
            dma_sp(ssm_sm[:], ssm_sm_in[l], w=["ssm_sm"])
            dma_sp(dTt[:], dT_in[l], w=["dTt"])
            dma_sp(bgluT[:], bgluT_in[l], w=["bgluT"])
            dt_sm = sm_tmp[:, 0:16]
            A("act", lambda e: e.activation(out=dt_sm, in_=ssm_sm[:, 0:16], func=AF.Exp), r=["ssm_sm"], w=["dt_sm"])
            A("dve", lambda e: e.tensor_tensor(out=sm_tmp[:, 16:32], in0=ssm_sm[:, 16:32], in1=dt_sm, op=ALU.mult), r=["dt_sm", "ssm_sm"], w=["ardt"])
            A("act", lambda e: e.activation(out=sm_r[:], in_=sm_tmp[:, 16:32], func=AF.Exp), r=["ardt"], w=["sm_r"])
            A("dve", lambda e: e.tensor_tensor(out=sm_th[:], in0=ssm_sm[:, 32:48], in1=dt_sm, op=ALU.mult), r=["dt_sm", "ssm_sm"], w=["sm_th"])
            A("dve", lambda e: e.tensor_scalar(out=sm_tmp[:, 32:48], in0=sm_th[:], scalar1=float(TC), scalar2=None, op0=ALU.mult), r=["sm_th"], w=["smTx"])
            sincos(sm_tmp[:, 32:48], 16, sm_sT[:], sm_cT[:], sm_tmp[:, 48:64], sm_tmp[:, 64:80], "smT")
            BbrT = aR1.bf(2048).rearrange("p (s n) -> p s n", s=16)
            BbiT = aR1.bf(2048).rearrange("p (s n) -> p s n", s=16)
            CrT = aR1.bf(2048).rearrange("p (s n) -> p s n", s=16)
            CiT = aR1.bf(2048).rearrange("p (s n) -> p s n", s=16)
            Wglu = aR1.bf(2048).rearrange("p (k n) -> p k n", k=4)
            uT = aR1.bf(8192).rearrange("p (k t) -> p k t", k=4)
            GL = aR1.bf(8192).rearrange("p (k t) -> p k t", k=4)
            dma_pool(CrT.rearrange("p s n -> p (s n)"), CTre_in[l], w=["CrT"])
            dma_pool(CiT.rearrange("p s n -> p (s n)"), CTim_in[l], w=["CiT"])
            dma_pool(Wglu, w_glu[l].rearrange("(k p) n -> p k n", p=128), w=["Wglu"])
            for k in range(4):
                dma_sp(uT[:, k, :], PROJ[128 * k:128 * (k + 1), :], w=["uT%d" % k])
            fl = [aBR.f32(512) for _ in range(14)]
            (f_ld, f_ar, f_ai, f_dt, f_mag, f_th, f_sin, f_cos, f_k, f_p, f_a, f_b, f_fr, f_fi) = fl
            f_bre = aBR.f32(512)
            f_bim = aBR.f32(512)
            for q in range(4):
                qs = slice(512 * q, 512 * (q + 1))
                dma_sp(f_ld, ssm_flat_in[l, 0:1, qs].to_broadcast([128, 512]), w=["f_ld"])
                dma_sp(f_ar, ssm_flat_in[l, 1:2, qs].to_broadcast([128, 512]), w=["f_ar"])
                dma_sp(f_ai, ssm_flat_in[l, 2:3, qs].to_broadcast([128, 512]), w=["f_ai"])
                dma_sp(f_bre, BTre_in[l, :, qs], w=["f_bre"])
                dma_sp(f_bim, BTim_in[l, :, qs], w=["f_bim"])
                A("act", lambda e: e.activation(out=f_dt, in_=f_ld, func=AF.Exp), r=["f_ld"], w=["f_dt"])
                A("dve", lambda e: e.tensor_tensor(out=f_a, in0=f_ar, in1=f_dt, op=ALU.mult), r=["f_ar", "f_dt"], w=["f_a"])
                A("act", lambda e: e.activation(out=f_mag, in_=f_a, func=AF.Exp), r=["f_a"], w=["f_mag"])
                A("dve", lambda e: e.tensor_tensor(out=f_th, in0=f_ai, in1=f_dt, op=ALU.mult), r=["f_ai", "f_dt"], w=["flx"])
                sincos(f_th, 512, f_sin, f_cos, f_k, f_p, "fl")
                A("dve", lambda e: e.tensor_tensor(out=f_cos, in0=f_cos, in1=f_mag, op=ALU.mult), r=["flc", "f_mag"], w=["flc"])
                A("dve", lambda e: e.tensor_tensor(out=f_sin, in0=f_sin, in1=f_mag, op=ALU.mult), r=["fls", "f_mag"], w=["fls"])
                A("dve", lambda e: e.tensor_scalar(out=f_cos, in0=f_cos, scalar1=-1.0, scalar2=None, op0=ALU.add), r=["flc"], w=["flc"])
                A("dve", lambda e: e.tensor_tensor(out=f_a, in0=f_ar, in1=f_ar, op=ALU.mult), r=["f_ar"], w=["f_a"])
                A("dve", lambda e: e.tensor_tensor(out=f_b, in0=f_ai, in1=f_ai, op=ALU.mult), r=["f_ai"], w=["f_b"])
                A("dve", lambda e: e.tensor_tensor(out=f_a, in0=f_a, in1=f_b, op=ALU.add), r=["f_a", "f_b"], w=["f_a"])
                A("dve", lambda e: e.reciprocal(out=f_a, in_=f_a), r=["f_a"], w=["f_a"])
                A("dve", lambda e: e.tensor_tensor(out=f_fr, in0=f_cos, in1=f_ar, op=ALU.mult), r=["flc", "f_ar"], w=["f_fr"])
                A("dve", lambda e: e.tensor_tensor(out=f_b, in0=f_sin, in1=f_ai, op=ALU.mult), r=["fls", "f_ai"], w=["f_b"])
                A("dve", lambda e: e.tensor_tensor(out=f_fr, in0=f_fr, in1=f_b, op=ALU.add), r=["f_fr", "f_b"], w=["f_fr"])
                A("dve", lambda e: e.tensor_tensor(out=f_fr, in0=f_fr, in1=f_a, op=ALU.mult), r=["f_fr", "f_a"], w=["f_fr"])
                A("dve", lambda e: e.tensor_tensor(out=f_fi, in0=f_sin, in1=f_ar, op=ALU.mult), r=["fls", "f_ar"], w=["f_fi"])
                A("dve", lambda e: e.tensor_tensor(out=f_b, in0=f_cos, in1=f_ai, op=ALU.mult), r=["flc", "f_ai"], w=["f_b"])
                A("dve", lambda e: e.tensor_tensor(out=f_fi, in0=f_fi, in1=f_b, op=ALU.subtract), r=["f_fi", "f_b"], w=["f_fi"])
                A("dve", lambda e: e.tensor_tensor(out=f_fi, in0=f_fi, in1=f_a, op=ALU.mult), r=["f_fi", "f_a"], w=["f_fi"])
                brv = BbrT.rearrange("p s n -> p (s n)")[:, qs]
                biv = BbiT.rearrange("p s n -> p (s n)")[:, qs]
                A("dve", lambda e: e.tensor_tensor(out=f_k, in0=f_fr, in1=f_bre, op=ALU.mult), r=["f_fr", "f_bre"], w=["flk"])
                A("dve", lambda e: e.tensor_tensor(out=f_p, in0=f_fi, in1=f_bim, op=ALU.mult), r=["f_fi", "f_bim"], w=["flp"])
                A("dve", lambda e, brv=brv: e.tensor_tensor(out=brv, in0=f_k, in1=f_p, op=ALU.subtract), r=["flk", "flp"], w=["BbrT"])
                A("dve", lambda e: e.tensor_tensor(out=f_k, in0=f_fr, in1=f_bim, op=ALU.mult), r=["f_fr", "f_bim"], w=["flk"])
                A("dve", lambda e: e.tensor_tensor(out=f_p, in0=f_fi, in1=f_bre, op=ALU.mult), r=["f_fi", "f_bre"], w=["flp"])
                A("dve", lambda e, biv=biv: e.tensor_tensor(out=biv, in0=f_k, in1=f_p, op=ALU.add), r=["flk", "flp"], w=["BbiT"])
            aBR.reset()
            cosT = aBR.f32(512)
            sinT = aBR.f32(512)
            tk = aBR.f32(512)
            tp = aBR.f32(512)
            thx = aBR.f32(512)
            xr_s = [aBR.f32(512) for _ in range(2)]
            xi_s = [aBR.f32(512) for _ in range(2)]
            t1 = aBR.f32(512)
            t2 = aBR.f32(512)
            t3 = aBR.f32(512)
            t4 = aBR.f32(512)
            zr = aBR.f32(512)
            zi = aBR.f32(512)
            gr = [aBR.f32(512) for _ in range(2)]
            gi = [aBR.f32(512) for _ in range(2)]
            hr = [aBR.bf(512) for _ in range(2)]
            nhi = [aBR.bf(512) for _ in range(2)]
            yv = aBR.f32(512)
            y2 = aBR.f32(512)
            ctr = 0
            for stt in range(16):
                ut = stt // 4
                A("dve", lambda e, stt=stt: e.tensor_scalar(out=thx, in0=iota[:], scalar1=sm_th[:, stt:stt + 1], scalar2=None, op0=ALU.mult), r=["iota", "sm_th"], w=["tbx"])
                sincos(thx, 512, sinT, cosT, tk, tp, "tb")
                for c in range(NCH):
                    cs = slice(TC * c, TC * (c + 1))
                    b = ctr % 2
                    ctr += 1
                    bx = 4 + 2 * b
                    A("pe", lambda e, stt=stt, ut=ut, cs=cs, bx=bx: e.matmul(psb(bx), BbrT[:, stt, :], uT[:, ut, cs], start=True, stop=True), r=["BbrT", "uT%d" % ut], w=pk(bx))
                    A("pe", lambda e, stt=stt, ut=ut, cs=cs, bx=bx: e.matmul(psb(bx + 1), BbiT[:, stt, :], uT[:, ut, cs], start=True, stop=True), r=["BbiT", "uT%d" % ut], w=pk(bx + 1))
                    A("act", lambda e, b=b, bx=bx: e.activation(out=xr_s[b], in_=psb(bx), func=AF.Copy), r=pk(bx), w=["xr%d" % b])
                    A("act", lambda e, b=b, bx=bx: e.activation(out=xi_s[b], in_=psb(bx + 1), func=AF.Copy), r=pk(bx + 1), w=["xi%d" % b])
                    A("dve", lambda e, b=b: e.tensor_tensor(out=t1, in0=xr_s[b], in1=cosT, op=ALU.mult), r=["xr%d" % b, "tbc"], w=["t1"])
                    A("pool", lambda e, b=b: e.tensor_tensor(out=t2, in0=xi_s[b], in1=sinT, op=ALU.mult), r=["xi%d" % b, "tbs"], w=["t2"])
                    A("dve", lambda e: e.tensor_tensor(out=zr, in0=t1, in1=t2, op=ALU.add), r=["t1", "t2"], w=["zr"])
                    A("pool", lambda e, b=b: e.tensor_tensor(out=t3, in0=xi_s[b], in1=cosT, op=ALU.mult), r=["xi%d" % b, "tbc"], w=["t3"])
                    A("dve", lambda e, b=b: e.tensor_tensor(out=t4, in0=xr_s[b], in1=sinT, op=ALU.mult), r=["xr%d" % b, "tbs"], w=["t4"])
                    A("pool", lambda e: e.tensor_tensor(out=zi, in0=t3, in1=t4, op=ALU.subtract), r=["t3", "t4"], w=["zi"])
                    rbc = sm_r[:, stt:stt + 1].to_broadcast([128, 512])
                    if c == 0:
                        A("dve", lambda e, b=b, rbc=rbc: e.tensor_tensor_scan(out=gr[b], data0=rbc, data1=zr, initial=0.0, op0=ALU.mult, op1=ALU.add), r=["zr", "sm_r"], w=["gr%d" % b])
                        A("dve", lambda e, b=b, rbc=rbc: e.tensor_tensor_scan(out=gi[b], data0=rbc, data1=zi, initial=0.0, op0=ALU.mult, op1=ALU.add), r=["zi", "sm_r"], w=["gi%d" % b])
                    else:
                        A("dve", lambda e, b=b, rbc=rbc: e.tensor_tensor_scan(out=gr[b], data0=rbc, data1=zr, initial=carry[:, 0:1], op0=ALU.mult, op1=ALU.add), r=["zr", "sm_r", "carry"], w=["gr%d" % b])
                        A("dve", lambda e, b=b, rbc=rbc: e.tensor_tensor_scan(out=gi[b], data0=rbc, data1=zi, initial=carry[:, 1:2], op0=ALU.mult, op1=ALU.add), r=["zi", "sm_r", "carry"], w=["gi%d" % b])
                    if c < NCH - 1:
                        cT_ = sm_cT[:, stt:stt + 1]
                        sT_ = sm_sT[:, stt:stt + 1]
                        A("dve", lambda e, b=b, cT_=cT_: e.tensor_tensor(out=carry[:, 2:3], in0=gr[b][:, 511:512], in1=cT_, op=ALU.mult), r=["gr%d" % b, "smTc"], w=["cy2"])
                        A("dve", lambda e, b=b, sT_=sT_: e.tensor_tensor(out=carry[:, 3:4], in0=gi[b][:, 511:512], in1=sT_, op=ALU.mult), r=["gi%d" % b, "smTs"], w=["cy3"])
                        A("dve", lambda e, b=b, sT_=sT_: e.tensor_tensor(out=carry[:, 4:5], in0=gr[b][:, 511:512], in1=sT_, op=ALU.mult), r=["gr%d" % b, "smTs"], w=["cy4"])
                        A("dve", lambda e, b=b, cT_=cT_: e.tensor_tensor(out=carry[:, 5:6], in0=gi[b][:, 511:512], in1=cT_, op=ALU.mult), r=["gi%d" % b, "smTc"], w=["cy5"])
                        A("dve", lambda e: e.tensor_tensor(out=carry[:, 0:1], in0=carry[:, 2:3], in1=carry[:, 3:4], op=ALU.subtract), r=["cy2", "cy3"], w=["carry"])
                        A("dve", lambda e: e.tensor_tensor(out=carry[:, 1:2], in0=carry[:, 4:5], in1=carry[:, 5:6], op=ALU.add), r=["cy4", "cy5", "carry"], w=["carry"])
                    A("dve", lambda e, b=b: e.tensor_tensor(out=t1, in0=gr[b], in1=cosT, op=ALU.mult), r=["gr%d" % b, "tbc"], w=["t1"])
                    A("pool", lambda e, b=b: e.tensor_tensor(out=t2, in0=gi[b], in1=sinT, op=ALU.mult), r=["gi%d" % b, "tbs"], w=["t2"])
                    A("dve", lambda e, b=b: e.tensor_tensor(out=hr[b], in0=t1, in1=t2, op=ALU.subtract), r=["t1", "t2"], w=["hr%d" % b])
                    A("pool", lambda e, b=b: e.tensor_tensor(out=t3, in0=gr[b], in1=sinT, op=ALU.mult), r=["gr%d" % b, "tbs"], w=["t3"])
                    A("dve", lambda e, b=b: e.tensor_tensor(out=t4, in0=gi[b], in1=cosT, op=ALU.mult), r=["gi%d" % b, "tbc"], w=["t4"])
                    A("dve", lambda e, b=b: e.scalar_tensor_tensor(out=nhi[b], in0=t3, scalar=-1.0, in1=t4, op0=ALU.mult, op1=ALU.subtract), r=["t3", "t4"], w=["nhi%d" % b])
                    first = (stt % 4 == 0)
                    last = (stt % 4 == 3)
                    A("pe", lambda e, stt=stt, b=b, c=c, first=first: e.matmul(psb(c), CrT[:, stt, :], hr[b], start=first, stop=False), r=["CrT", "hr%d" % b], w=pk(c))
                    A("pe", lambda e, stt=stt, b=b, c=c, last=last: e.matmul(psb(c), CiT[:, stt, :], nhi[b], start=False, stop=last), r=["CiT", "nhi%d" % b], w=pk(c))
                if stt % 4 == 3:
                    for c in range(NCH):
                        cs = slice(TC * c, TC * (c + 1))
                        A("dve", lambda e, ut=ut, cs=cs, c=c: e.scalar_tensor_tensor(out=yv, in0=uT[:, ut, cs], scalar=dTt[:, ut:ut + 1], in1=psb(c), op0=ALU.mult, op1=ALU.add), r=["uT%d" % ut, "dTt"] + pk(c), w=["yv"])
                        A("dve", lambda e: e.tensor_tensor(out=y2, in0=yv, in1=yv, op=ALU.mult), r=["yv"], w=["y2"])
                        A("dve", lambda e: e.tensor_scalar(out=y2, in0=y2, scalar1=0.044715, scalar2=1.0, op0=ALU.mult, op1=ALU.add), r=["y2"], w=["y2"])
                        A("dve", lambda e: e.tensor_tensor(out=y2, in0=y2, in1=yv, op=ALU.mult), r=["y2", "yv"], w=["y2"])
                        A("act", lambda e: e.activation(out=y2, in_=y2, func=AF.Sigmoid, scale=float(2.0 * np.sqrt(2.0 / np.pi))), r=["y2"], w=["y2"])
                        A("dve", lambda e, ut=ut, cs=cs: e.tensor_tensor(out=GL[:, ut, cs], in0=y2, in1=yv, op=ALU.mult), r=["y2", "yv"], w=["GL%d" % ut])
            S.barrier()
            BR = BRa[:, :].rearrange("p (k t) -> p k t", k=12)
            sg = aT.f32(512)
            n = 0
            for fo in range(4):
                for c in range(NCH):
                    cs = slice(TC * c, TC * (c + 1))
                    bank = 4 + (n % 4)
                    n += 1
                    for kt in range(4):
                        A("pe", lambda e, kt=kt, fo=fo, cs=cs, bank=bank: e.matmul(psb(bank), Wglu[:, kt, 128 * fo:128 * (fo + 1)], GL[:, kt, cs], start=(kt == 0), stop=(kt == 3)), r=["Wglu", "GL%d" % kt], w=pk(bank))
                    A("act", lambda e, fo=fo, bank=bank: e.activation(out=sg, in_=psb(bank), func=AF.Sigmoid, bias=bgluT[:, fo:fo + 1]), r=pk(bank) + ["bgluT"], w=["sg"])
                    A("dve", lambda e, fo=fo, cs=cs: e.tensor_tensor(out=BR[:, fo, cs], in0=GL[:, fo, cs], in1=sg, op=ALU.mult), r=["sg", "GL%d" % fo], w=["BR%d" % fo])
            S.barrier()

        def phase_conv(l):
            S.mark("phase_conv")
            reset_arenas()
            BR = BRa[:, :].rearrange("p (k t) -> p k t", k=12)
            dma_sp(convwT[:], convwT_in[l], w=["convwT"])
            cb = [aR1.bf(2048) for _ in range(2)]
            cc = [aR1.bf(2048) for _ in range(2)]
            ch = [aR1.bf(2048) for _ in range(2)]
            zb = aR1.f32(2064)
            acc = aR1.f32(2048)
            A("dve", lambda e: e.memset(zb[:, 0:16], 0.0), w=["zb"])
            for j in range(4):
                b = j % 2
                dma_sp(cb[b], PROJ[128 * (40 + j):128 * (41 + j), :], w=["cb%d" % b])
                dma_sp(cc[b], PROJ[128 * (44 + j):128 * (45 + j), :], w=["cc%d" % b])
                dma_sp(ch[b], PROJ[128 * (48 + j):128 * (49 + j), :], w=["ch%d" % b])
                A("dve", lambda e, b=b: e.tensor_tensor(out=zb[:, 16:2064], in0=cc[b], in1=ch[b], op=ALU.mult), r=["cc%d" % b, "ch%d" % b], w=["zb"])
                A("dve", lambda e, j=j: e.tensor_scalar(out=acc, in0=zb[:, 16:2064], scalar1=convwT[:, 3 * j:3 * j + 1], scalar2=None, op0=ALU.mult), r=["zb", "convwT"], w=["cacc"])
                A("dve", lambda e, j=j: e.scalar_tensor_tensor(out=acc, in0=zb[:, 15:2063], scalar=convwT[:, 3 * j + 1:3 * j + 2], in1=acc, op0=ALU.mult, op1=ALU.add), r=["zb", "convwT", "cacc"], w=["cacc"])
                A("dve", lambda e, j=j: e.scalar_tensor_tensor(out=acc, in0=zb[:, 14:2062], scalar=convwT[:, 3 * j + 2:3 * j + 3], in1=acc, op0=ALU.mult, op1=ALU.add), r=["zb", "convwT", "cacc"], w=["cacc"])
                A("dve", lambda e, j=j, b=b: e.tensor_tensor(out=BR[:, 8 + j, :], in0=acc, in1=cb[b], op=ALU.mult), r=["cacc", "cb%d" % b], w=["BR%d" % (8 + j)])
            S.barrier()

        def phase_attn(l):
            S.mark("phase_attn")
            reset_arenas()
            BR = BRa[:, :].rearrange("p (k t) -> p k t", k=12)
            bhi = aR1.bf(24 * 256).rearrange("p (h n) -> p h n", h=24)
            blo = aR1.bf(24 * 256).rearrange("p (h n) -> p h n", h=24)
            qT = [aR1.bf(2048) for _ in range(2)]
            kT = [aR1.bf(2048) for _ in range(2)]
            Vb = [aR1.bf(2048).rearrange("p (b f) -> p b f", b=16) for _ in range(2)]
            identb = aR1.bf(128)
            onesv = aR1.bf(64)
            Uacc = aT.f32(2048)
            Lacc = aT.f32(2048)
            pT = [aT.bf(256) for _ in range(2)]
            dma_pool(bhi.rearrange("p h n -> p (h n)"), bias_hi_in[:, :], w=["bhi"])
            dma_pool(blo.rearrange("p h n -> p (h n)"), bias_lo_in[:, :], w=["blo"])
            dma_pool(identb, ident_in[:, :], w=["identb"])
            A("dve", lambda e: e.memset(onesv, 1.0), w=["onesv"])
            dils = [1, 4, 16]
            n = 0
            sctr = 0
            uctr = 0
            for hp in range(4):
                for g in range(3):
                    dil = dils[g]
                    nb = 16 // dil
                    b = n % 2
                    n += 1
                    qt = 4 + 4 * g + hp
                    kt_ = 16 + 4 * g + hp
                    dma_sp(qT[b], PROJ[128 * qt:128 * (qt + 1), :], w=["qT%d" % b])
                    dma_sp(kT[b], PROJ[128 * kt_:128 * (kt_ + 1), :], w=["kT%d" % b])
                    vsrc = VTOK.rearrange("(bb i r) f -> i r bb f", i=128, r=dil)
                    for r in range(dil):
                        dma_sp(Vb[b][:, r * nb:(r + 1) * nb, :], vsrc[:, r, :, 512 * g + 128 * hp:512 * g + 128 * (hp + 1)], w=["Vb%d" % b])
                    for q4 in range(4):
                        ub = 2 + (uctr % 2)
                        lb = 4 + (uctr % 2)
                        uctr += 1
                        for bi4 in range(4):
                            bi = 4 * q4 + bi4
                            bblk = bi % nb
                            for hh in range(2):
                                head = 8 * g + 2 * hp + hh
                                prt = slice(64 * hh, 64 * (hh + 1))
                                sb_ = sctr % 2
                                sctr += 1
                                sps = PS[:, 256 * sb_:256 * (sb_ + 1)]
                                skey = ["pss%d" % sb_]
                                qv = qT[b][prt, 128 * bi:128 * (bi + 1)]
                                if bblk > 0:
                                    A("pe", lambda e, sps=sps, head=head: e.matmul(sps, identb, bhi[:, head, :], start=True, stop=False), r=["identb", "bhi"], w=skey)
                                    A("pe", lambda e, sps=sps, head=head: e.matmul(sps, identb, blo[:, head, :], start=False, stop=False), r=["identb", "blo"], w=skey)
                                    A("pe", lambda e, sps=sps, b=b, prt=prt, bi=bi, qv=qv: e.matmul(sps[:, 0:128], kT[b][prt, 128 * (bi - 1):128 * bi], qv, start=False, stop=False), r=["kT%d" % b, "qT%d" % b], w=skey)
                                    A("pe", lambda e, sps=sps, b=b, prt=prt, bi=bi, qv=qv: e.matmul(sps[:, 128:256], kT[b][prt, 128 * bi:128 * (bi + 1)], qv, start=False, stop=True), r=["kT%d" % b, "qT%d" % b], w=skey)
                                    A("act", lambda e, sps=sps, sb_=sb_: e.activation(out=pT[sb_], in_=sps, func=AF.Exp, scale=0.125), r=skey, w=["pT%d" % sb_])
                                    ucol = slice(128 * bi4, 128 * (bi4 + 1))
                                    A("pe", lambda e, ub=ub, prt=prt, ucol=ucol, b=b, bi=bi, hh=hh, sb_=sb_: e.matmul(psb(ub)[prt, ucol], Vb[b][:, bi - 1, 64 * hh:64 * (hh + 1)], pT[sb_][:, 0:128], start=True, stop=False), r=["Vb%d" % b, "pT%d" % sb_], w=pk(ub))
                                    A("pe", lambda e, ub=ub, prt=prt, ucol=ucol, b=b, bi=bi, hh=hh, sb_=sb_: e.matmul(psb(ub)[prt, ucol], Vb[b][:, bi, 64 * hh:64 * (hh + 1)], pT[sb_][:, 128:256], start=False, stop=True), r=["Vb%d" % b, "pT%d" % sb_], w=pk(ub))
                                    A("pe", lambda e, lb=lb, prt=prt, ucol=ucol, sb_=sb_: e.matmul(psb(lb)[prt, ucol], onesv, pT[sb_][:, 0:128], start=True, stop=False), r=["onesv", "pT%d" % sb_], w=pk(lb))
                                    A("pe", lambda e, lb=lb, prt=prt, ucol=ucol, sb_=sb_: e.matmul(psb(lb)[prt, ucol], onesv, pT[sb_][:, 128:256], start=False, stop=True), r=["onesv", "pT%d" % sb_], w=pk(lb))
                                else:
                                    sc_ = sps[:, 128:256]
                                    A("pe", lambda e, sc_=sc_, head=head: e.matmul(sc_, identb, bhi[:, head, 128:256], start=True, stop=False), r=["identb", "bhi"], w=skey)
                                    A("pe", lambda e, sc_=sc_, head=head: e.matmul(sc_, identb, blo[:, head, 128:256], start=False, stop=False), r=["identb", "blo"], w=skey)
                                    A("pe", lambda e, sc_=sc_, b=b, prt=prt, bi=bi, qv=qv: e.matmul(sc_, kT[b][prt, 128 * bi:128 * (bi + 1)], qv, start=False, stop=True), r=["kT%d" % b, "qT%d" % b], w=skey)
                                    A("act", lambda e, sc_=sc_, sb_=sb_: e.activation(out=pT[sb_][:, 128:256], in_=sc_, func=AF.Exp, scale=0.125), r=skey, w=["pT%d" % sb_])
                                    ucol = slice(128 * bi4, 128 * (bi4 + 1))
                                    A("pe", lambda e, ub=ub, prt=prt, ucol=ucol, b=b, bi=bi, hh=hh, sb_=sb_: e.matmul(psb(ub)[prt, ucol], Vb[b][:, bi, 64 * hh:64 * (hh + 1)], pT[sb_][:, 128:256], start=True, stop=True), r=["Vb%d" % b, "pT%d" % sb_], w=pk(ub))
                                    A("pe", lambda e, lb=lb, prt=prt, ucol=ucol, sb_=sb_: e.matmul(psb(lb)[prt, ucol], onesv, pT[sb_][:, 128:256], start=True, stop=True), r=["onesv", "pT%d" % sb_], w=pk(lb))
                        if dil == 1:
                            uo = Uacc[:, 512 * q4:512 * (q4 + 1)]
                            lo_ = Lacc[:, 512 * q4:512 * (q4 + 1)]
                            ui = psb(ub)
                            li = psb(lb)
                        elif dil == 4:
                            uo = Uacc.rearrange("p (m r) -> p r m", r=4)[:, q4, :]
                            lo_ = Lacc.rearrange("p (m r) -> p r m", r=4)[:, q4, :]
                            ui = psb(ub)
                            li = psb(lb)
                        else:
                            uo = Uacc.rearrange("p (i r) -> p i r", r=16)[:, :, 4 * q4:4 * (q4 + 1)]
                            lo_ = Lacc.rearrange("p (i r) -> p i r", r=16)[:, :, 4 * q4:4 * (q4 + 1)]
                            ui = psb(ub).rearrange("p (rr i) -> p i rr", rr=4)
                            li = psb(lb).rearrange("p (rr i) -> p i rr", rr=4)
                        if g == 0:
                            A("dve", lambda e, uo=uo, ui=ui: e.tensor_copy(out=uo, in_=ui), r=pk(ub), w=["Uacc"])
                            A("dve", lambda e, lo_=lo_, li=li: e.tensor_copy(out=lo_, in_=li), r=pk(lb), w=["Lacc"])
                        else:
                            A("dve", lambda e, uo=uo, ui=ui: e.tensor_tensor(out=uo, in0=uo, in1=ui, op=ALU.add), r=pk(ub) + ["Uacc"], w=["Uacc"])
                            A("dve", lambda e, lo_=lo_, li=li: e.tensor_tensor(out=lo_, in0=lo_, in1=li, op=ALU.add), r=pk(lb) + ["Lacc"], w=["Lacc"])
                A("dve", lambda e: e.reciprocal(out=Lacc, in_=Lacc), r=["Lacc"], w=["Lacc"])
                A("dve", lambda e, hp=hp: e.tensor_tensor(out=BR[:, 4 + hp, :], in0=Uacc, in1=Lacc, op=ALU.mult), r=["Uacc", "Lacc"], w=["BR%d" % (4 + hp)])
            S.barrier()

        def phase_merge(l):
            S.mark("phase_merge")
            reset_arenas()
            BR = BRa[:, :].rearrange("p (k t) -> p k t", k=12)
            mT = aR1.bf(32768).rearrange("p (k t) -> p k t", k=16)
            Wb = []
            for br, wd in enumerate((w_ssm_out, w_attn_out, w_conv_out)):
                Wb.append(load_w(br, wd[l].rearrange("(k p) n -> p k n", p=128), 4, 2048, "WS%d" % br))
            gt = [[aT.bf(512) for _ in range(3)] for _ in range(2)]
            m0 = aT.f32(512)
            m1 = aT.f32(512)
            m2 = aT.f32(512)
            n = 0
            for fo in range(16):
                for c in range(NCH):
                    cs = slice(TC * c, TC * (c + 1))
                    b = n % 2
                    n += 1
                    pb = 3 * b
                    for br in range(3):
                        gtile = 52 + 16 * br + fo
                        dma_sp(gt[b][br], PROJ[128 * gtile:128 * (gtile + 1), cs], w=["gt%d%d" % (b, br)])
                        for kt in range(4):
                            A("pe", lambda e, br=br, kt=kt, fo=fo, cs=cs, pb=pb: e.matmul(psb(pb + br), Wb[br][:, kt, 128 * fo:128 * (fo + 1)], BR[:, 4 * br + kt, cs], start=(kt == 0), stop=(kt == 3)),
                              r=["WS%d" % br, "BR%d" % (4 * br + kt)], w=pk(pb + br))
                    A("dve", lambda e, b=b, pb=pb: e.tensor_tensor(out=m0, in0=psb(pb), in1=gt[b][0], op=ALU.mult), r=pk(pb) + ["gt%d0" % b], w=["m0"])
                    A("dve", lambda e, b=b, pb=pb: e.tensor_tensor(out=m1, in0=psb(pb + 1), in1=gt[b][1], op=ALU.mult), r=pk(pb + 1) + ["gt%d1" % b], w=["m1"])
                    A("dve", lambda e, b=b, pb=pb: e.tensor_tensor(out=m2, in0=psb(pb + 2), in1=gt[b][2], op=ALU.mult), r=pk(pb + 2) + ["gt%d2" % b], w=["m2"])
                    A("pool", lambda e: e.tensor_tensor(out=m0, in0=m0, in1=m1, op=ALU.add), r=["m0", "m1"], w=["m0"])
                    A("pool", lambda e, fo=fo, cs=cs: e.tensor_tensor(out=mT[:, fo, cs], in0=m0, in1=m2, op=ALU.add), r=["m0", "m2"], w=["mT%d" % fo])
            S.barrier()
            aT.reset()
            aBR.reset()
            yst = [aBR.f32(2048) for _ in range(2)]
            sqf = aBR.f32(2048)
            wv = w_o[l].rearrange("(kt p) n -> p kt n", p=128)
            mkeys = ["mT%d" % k for k in range(16)]
            n = 0
            for cg in range(4):
                slot = cg % 3
                wk = "WS%d" % slot
                W = load_w(slot, wv[:, :, 512 * cg:512 * (cg + 1)], 16, 512, wk)
                for j in range(4):
                    fo = 4 * cg + j
                    hb = 4 * (half_ctr[0] % 2)
                    half_ctr[0] += 1
                    for c in range(NCH):
                        for kt in range(16):
                            A("pe", lambda e, W=W, kt=kt, j=j, c=c, hb=hb: e.matmul(psb(hb + c), W[:, kt, 128 * j:128 * (j + 1)], mT[:, kt, TC * c:TC * (c + 1)], start=(kt == 0), stop=(kt == 15)),
                              r=[wk, "mT%d" % kt], w=pk(hb + c))
                    b = n % 2
                    n += 1
                    A("act", lambda e, b=b, hb=hb: e.activation(out=yst[b], in_=psb(hb, 4), func=AF.Copy), r=pk(hb, 4), w=["yst%d" % b])
                    dma_sp(YT[128 * fo:128 * (fo + 1), :], yst[b], r=["yst%d" % b])
                    if fo == 0:
                        A("act", lambda e, hb=hb: e.activation(out=ACC[:], in_=psb(hb, 4), func=AF.Square), r=pk(hb, 4), w=["ACC"])
                    else:
                        A("act", lambda e, hb=hb: e.activation(out=sqf, in_=psb(hb, 4), func=AF.Square), r=pk(hb, 4), w=["sqf"])
                        A("pool", lambda e: e.tensor_tensor(out=ACC[:], in0=ACC[:], in1=sqf, op=ALU.add), r=["sqf", "ACC"], w=["ACC"])
            finish_rstd()
            S.barrier()

        def finish_rstd():
            for c in range(NCH):
                cs = slice(TC * c, TC * (c + 1))
                A("pe", lambda e, c=c, cs=cs: e.matmul(psb(c), ones32[:], ACC[:, cs], start=True, stop=True), r=["ones32", "ACC"], w=pk(c))
            A("act", lambda e: e.activation(out=RSTDY[:], in_=psb(0, 4), func=AF.Sqrt, bias=epsT[:, 0:1], scale=1.0 / D), r=pk(0, 4) + ["epsT"], w=["RSTDY"])
            A("dve", lambda e: e.reciprocal(out=RSTDY[:], in_=RSTDY[:]), r=["RSTDY"], w=["RSTDY"])

        def phase_ffn(l):
            S.mark("phase_ffn")
            reset_arenas()
            hT = aR1.bf(32768).rearrange("p (k t) -> p k t", k=16)
            dma_sp(ffnwT[:], ffnwT_in[l], w=["ffnwT"])
            abuf = [aBR.f32(2064) for _ in range(2)]
            bbuf = [aBR.f32(2064) for _ in range(2)]
            cb = aBR.f32(2048)
            ca = aT.f32(2048)
            sa = aT.f32(2048)
            gst = [aT.bf(2048) for _ in range(2)]
            for b in range(2):
                A("dve", lambda e, b=b: e.memset(abuf[b][:, 0:16], 0.0), w=["abuf%d" % b])
                A("dve", lambda e, b=b: e.memset(bbuf[b][:, 0:16], 0.0), w=["bbuf%d" % b])
            wv = w_up[l].rearrange("(kt p) n -> p kt n", p=128)
            n = 0
            hn = 0

            def conv3(dst, src, w0, dkey, skey):
                A("dve", lambda e: e.tensor_scalar(out=dst, in0=src[:, 16:2064], scalar1=ffnwT[:, w0:w0 + 1], scalar2=None, op0=ALU.mult), r=[skey, "ffnwT"], w=[dkey])
                A("dve", lambda e: e.scalar_tensor_tensor(out=dst, in0=src[:, 15:2063], scalar=ffnwT[:, w0 + 1:w0 + 2], in1=dst, op0=ALU.mult, op1=ALU.add), r=[skey, "ffnwT", dkey], w=[dkey])
                A("dve", lambda e: e.scalar_tensor_tensor(out=dst, in0=src[:, 14:2062], scalar=ffnwT[:, w0 + 2:w0 + 3], in1=dst, op0=ALU.mult, op1=ALU.add), r=[skey, "ffnwT", dkey], w=[dkey])

            for pg in range(11):
                sla = (2 * pg) % 3
                slb = (2 * pg + 1) % 3
                Wa = load_w(sla, wv[:, :, 512 * pg:512 * (pg + 1)], 16, 512, "WS%d" % sla)
                Wb_ = load_w(slb, wv[:, :, DFF + 512 * pg:DFF + 512 * (pg + 1)], 16, 512, "WS%d" % slb)
                for j in range(4):
                    fa = 4 * pg + j
                    b = n % 2
                    n += 1
                    for th in range(2):
                        hb = 4 * (hn % 2)
                        hn += 1
                        for (W, wk, boff) in ((Wa, "WS%d" % sla, 0), (Wb_, "WS%d" % slb, 2)):
                            for kt in range(16):
                                for c2 in range(2):
                                    tok = slice(1024 * th + 512 * c2, 1024 * th + 512 * (c2 + 1))
                                    A("pe", lambda e, W=W, kt=kt, j=j, tok=tok, bank=hb + boff + c2: e.matmul(psb(bank), W[:, kt, 128 * j:128 * (j + 1)], hT[:, kt, tok], start=(kt == 0), stop=(kt == 15)),
                                      r=[wk, "hT%d" % kt], w=pk(hb + boff + c2))
                        A("act", lambda e, b=b, th=th, hb=hb: e.activation(out=abuf[b][:, 16 + 1024 * th:16 + 1024 * (th + 1)], in_=psb(hb, 2), func=AF.Copy), r=pk(hb, 2), w=["abuf%d" % b])
                        A("act", lambda e, b=b, th=th, hb=hb: e.activation(out=bbuf[b][:, 16 + 1024 * th:16 + 1024 * (th + 1)], in_=psb(hb + 2, 2), func=AF.Copy), r=pk(hb + 2, 2), w=["bbuf%d" % b])
                    conv3(ca, abuf[b], 3 * fa, "ca", "abuf%d" % b)
                    A("act", lambda e: e.activation(out=sa, in_=ca, func=AF.Silu), r=["ca"], w=["sa"])
                    conv3(cb, bbuf[b], 3 * (44 + fa), "cb", "bbuf%d" % b)
                    A("dve", lambda e, b=b: e.tensor_tensor(out=gst[b], in0=sa, in1=cb, op=ALU.mult), r=["sa", "cb"], w=["gst%d" % b])
                    dma_sp(GS[128 * fa:128 * (fa + 1), :], gst[b], r=["gst%d" % b], w=["GS%d" % fa])
            S.barrier()
            reset_arenas()
            g1 = aR1.bf(32768).rearrange("p (k t) -> p k t", k=32)
            g2 = aBR.bf(12288).rearrange("p (k t) -> p k t", k=12)
            yst = [aBR.f32(1024) for _ in range(2)]
            sqf = [aT.f32(1024) for _ in range(2)]
            gv = GS.rearrange("(k p) t -> p k t", p=128)
            wdv = w_down[l].rearrange("(k p) n -> p k n", p=128)
            WD = [WSall[:, 11264 * i:11264 * (i + 1)].rearrange("p (k n) -> p k n", k=44) for i in range(2)]
            n = 0
            for th in range(2):
                ts_ = slice(1024 * th, 1024 * (th + 1))
                for q in range(4):
                    dma_sp(g1[:, 8 * q:8 * (q + 1), :], gv[:, 8 * q:8 * (q + 1), ts_], r=["GS%d" % i for i in range(8 * q, 8 * q + 8)], w=["g1_%d" % q])
                dma_sp(g2[:, 0:6, :], gv[:, 32:38, ts_], r=["GS%d" % i for i in range(32, 38)], w=["g2_0"])
                dma_sp(g2[:, 6:12, :], gv[:, 38:44, ts_], r=["GS%d" % i for i in range(38, 44)], w=["g2_1"])

                def gsrc(kt, c2):
                    if kt < 32:
                        return g1[:, kt, 512 * c2:512 * (c2 + 1)], "g1_%d" % (kt // 8)
                    return g2[:, kt - 32, 512 * c2:512 * (c2 + 1)], "g2_%d" % ((kt - 32) // 6)

                for cgp in range(8):
                    slot = cgp % 2
                    wk = "WD%d" % slot
                    dma_pool(WD[slot], wdv[:, :, 256 * cgp:256 * (cgp + 1)], w=[wk])
                    for j in range(2):
                        fo = 2 * cgp + j
                        bank = 2 * (n % 4)
                        b = n % 2
                        n += 1
                        for kt in range(44):
                            for c2 in range(2):
                                gs_, gk = gsrc(kt, c2)
                                A("pe", lambda e, slot=slot, kt=kt, j=j, gs_=gs_, bank=bank + c2: e.matmul(psb(bank), WD[slot][:, kt, 128 * j:128 * (j + 1)], gs_, start=(kt == 0), stop=(kt == 43)), r=[wk, gk], w=pk(bank + c2))
                        A("act", lambda e, b=b, bank=bank: e.activation(out=yst[b], in_=psb(bank, 2), func=AF.Copy), r=pk(bank, 2), w=["yst%d" % b])
                        dma_sp(YT[128 * fo:128 * (fo + 1), ts_], yst[b], r=["yst%d" % b])
                        if fo == 0:
                            A("act", lambda e, bank=bank, ts_=ts_: e.activation(out=ACC[:, ts_], in_=psb(bank, 2), func=AF.Square), r=pk(bank, 2), w=["ACC"])
                        else:
                            A("act", lambda e, b=b, bank=bank: e.activation(out=sqf[b], in_=psb(bank, 2), func=AF.Square), r=pk(bank, 2), w=["sqf%d" % b])
                            A("pool", lambda e, b=b, ts_=ts_: e.tensor_tensor(out=ACC[:, ts_], in0=ACC[:, ts_], in1=sqf[b], op=ALU.add), r=["sqf%d" % b, "ACC"], w=["ACC"])
            S.barrier()
            finish_rstd()
            S.barrier()

        ident_in = din("ident", [128, 128])

        for l in range(n_layers):
            if l == 0:
                phase_mod(0)
                phase_norm(xT_in, False, None, AG[:, 0:16], modT[:, 0:16], XT, True)
            phase_inproj(l)
            phase_ssm(l)
            phase_conv(l)
            phase_attn(l)
            phase_merge(l)
            phase_norm(XT, True, AG[:, 16:32], AG[:, 32:48], modT[:, 48:64], XT, True)
            phase_ffn(l)
            if l + 1 < n_layers:
                G2 = sbt("G2_%d" % l, [128, 16], F32)
                A("dve", lambda e, G2=G2: e.tensor_copy(out=G2[:], in_=AG[:, 48:64]), w=["G2s"])
                S.barrier()
                phase_mod(l + 1)
                phase_norm(XT, True, G2[:], AG[:, 0:16], modT[:, 0:16], XT, True)
            else:
                phase_norm(XT, True, AG[:, 48:64], None, None, outT, False)

        sems = {e: st.enter_context(nc.semaphore("s_" + e)) for e in ENGS}
        dma_sems = {e: [st.enter_context(nc.semaphore("d_%s%d" % (e, i))) for i in range(DMA_K)] for e in ("sp", "pool")}
        S.mark("end")
        build_program.marks = S.marks
        S.prepare(sems, dma_sems)
        block = st.enter_context(nc.Block())

        @block.tensor
        def _(e):
            S.run("pe", e)

        @block.scalar
        def _(e):
            S.run("act", e)

        @block.vector
        def _(e):
            S.run("dve", e)

        @block.gpsimd
        def _(e):
            S.run("pool", e)

        @block.sync
        def _(e):
            S.run("sp", e)
    return nc


def prep_shared(inp):
    f = lambda a: np.ascontiguousarray(np.asarray(a, dtype=np.float32))
    sh = {}
    for k in ("w_mod", "w_in", "w_glu", "w_ssm_out", "w_attn_out", "w_conv_out", "w_o", "w_up", "w_down"):
        sh[k] = f(inp[k])
    sh["bmodT"] = f(np.asarray(inp["b_mod"]).reshape(NL, 96, 128).transpose(0, 2, 1))
    g = np.stack([np.asarray(inp[k]).reshape(NL, 16, 128).transpose(0, 2, 1) for k in ("g_pre_mix", "g_post_mix", "g_pre_ffn", "g_post_ffn")], axis=2)
    sh["gains"] = f(g.reshape(NL, 128, 64))
    sh["bgateT"] = f(np.asarray(inp["b_gate"]).reshape(NL, 48, 128).transpose(0, 2, 1))
    ld = np.repeat(np.asarray(inp["ssm_log_dt"]), 64, axis=1)
    ar = np.asarray(inp["ssm_a_re"]).reshape(NL, 2048)
    ai = np.asarray(inp["ssm_a_im"]).reshape(NL, 2048)
    sh["ssm_flat"] = f(np.stack([ld, ar, ai], axis=1))
    sm = np.stack([a.reshape(NL, 16, 128).transpose(0, 2, 1) for a in (ld, ar, ai)], axis=2)
    sh["ssm_sm"] = f(sm.reshape(NL, 128, 48))
    bre = np.asarray(inp["ssm_b_re"])
    bim = np.asarray(inp["ssm_b_im"])
    cre = np.asarray(inp["ssm_c_re"])
    cim = np.asarray(inp["ssm_c_im"])
    BTre = np.zeros((NL, 128, 16, 128), np.float32)
    BTim = np.zeros((NL, 128, 16, 128), np.float32)
    CTre = np.zeros((NL, 128, 16, 128), np.float32)
    CTim = np.zeros((NL, 128, 16, 128), np.float32)
    for g_ in range(32):
        cs = slice(16 * (g_ % 8), 16 * (g_ % 8) + 16)
        ss = slice(64 * (g_ % 2), 64 * (g_ % 2) + 64)
        BTre[:, cs, g_ // 2, ss] = bre[:, g_].transpose(0, 2, 1)
        BTim[:, cs, g_ // 2, ss] = bim[:, g_].transpose(0, 2, 1)
        CTre[:, ss, g_ // 2, cs] = cre[:, g_].transpose(0, 2, 1)
        CTim[:, ss, g_ // 2, cs] = cim[:, g_].transpose(0, 2, 1)
    sh["BTre"] = BTre.reshape(NL, 128, 2048)
    sh["BTim"] = BTim.reshape(NL, 128, 2048)
    sh["CTre"] = CTre.reshape(NL, 128, 2048)
    sh["CTim"] = CTim.reshape(NL, 128, 2048)
    sh["dT"] = f(np.asarray(inp["ssm_d"]).reshape(NL, 4, 128).transpose(0, 2, 1))
    sh["bgluT"] = f(np.asarray(inp["b_glu"]).reshape(NL, 4, 128).transpose(0, 2, 1))
    sh["convwT"] = f(np.asarray(inp["conv_mix_w"]).reshape(NL, 3, 4, 128).transpose(0, 3, 2, 1).reshape(NL, 128, 12))
    sh["ffnwT"] = f(np.asarray(inp["ffn_conv_w"]).reshape(NL, 3, 88, 128).transpose(0, 3, 2, 1).reshape(NL, 128, 264))
    sh["iota"] = f(np.tile(np.arange(512, dtype=np.float32)[None, :], (128, 1)))
    sh["ones"] = np.ones((128, 128), np.float32)
    sh["ident"] = np.eye(128, dtype=np.float32)
    hi, lo = make_bias_tables()
    sh["bias_hi"] = f(hi)
    sh["bias_lo"] = f(lo)
    return sh


def kernel(**inputs):
    n_layers = int(os.environ.get("K_NLAYERS", NL))
    debug = bool(int(os.environ.get("K_DEBUG", "0")))
    ncores = int(os.environ.get("K_NCORES", 8))
    sh = prep_shared(inputs)
    x = np.asarray(inputs["x"], dtype=np.float32)
    c = np.asarray(inputs["c"], dtype=np.float32)
    in_maps = []
    for b in range(ncores):
        m = dict(sh)
        m["xT"] = np.ascontiguousarray(x[b].T)
        m["cT"] = np.ascontiguousarray(c[b].reshape(16, 128).T)
        in_maps.append(m)
    nc = build_program(n_layers=n_layers, debug=debug)
    res = run_bass_kernel_spmd(nc, in_maps, core_ids=list(range(ncores)))
    if debug:
        kernel.last_results = res.results
    out = np.stack([np.ascontiguousarray(r["outT"].T) for r in res.results], axis=0)
    return out.astype(np.float32)
```

```python
import os
import numpy as np
import concourse.bass as bass
import concourse.mybir as mybir
from concourse.bass_utils import run_bass_kernel_spmd
from contextlib import ExitStack

F32 = mybir.dt.float32
BF16 = mybir.dt.bfloat16
AF = mybir.ActivationFunctionType
ALU = mybir.AluOpType

D = 2048
L = 2048
KT = 16
NL = 4
N_IN = 12800
DFF = 5632
NCH = 4
TC = 512
PI = float(np.pi)
USE_ACT_TABLES = 0

ENGS = ("pe", "act", "dve", "pool", "sp")
DMA_K = 8


class Op:
    __slots__ = ("eng", "fn", "deps", "is_dma", "signal", "sigval", "dsem", "dval", "dprev")

    def __init__(self, eng, fn, is_dma):
        self.eng = eng
        self.fn = fn
        self.is_dma = is_dma
        self.deps = ()
        self.signal = False
        self.sigval = 0
        self.dsem = None
        self.dval = 0
        self.dprev = None


class Sched:
    def __init__(self, same_engine_sync=True):
        self.ops = []
        self.lw = {}
        self.rd = {}
        self.same = same_engine_sync
        self.last_on = {e: None for e in ENGS}
        self.last_dmas = {e: [] for e in ENGS}
        self.barrier_deps = ()
        self._fresh = {e: False for e in ENGS}
        self.marks = []
        self.npe = 0

    def mark(self, name):
        self.marks.append((name, self.npe))

    def add(self, eng, fn, reads=(), writes=(), dma=False):
        op = Op(eng, fn, dma)
        if eng == "pe":
            self.npe += 1
        deps = set(self.barrier_deps) if self._fresh[eng] else set()
        self._fresh[eng] = False
        lw, rd = self.lw, self.rd
        for k in reads:
            w = lw.get(k)
            if w is not None:
                deps.add(w)
        for k in writes:
            w = lw.get(k)
            if w is not None:
                deps.add(w)
            r = rd.get(k)
            if r:
                deps.update(r)
        for k in reads:
            lst = rd.get(k)
            if lst is None:
                rd[k] = [op]
            elif (not dma) and lst and lst[-1].eng == eng and not lst[-1].is_dma:
                lst[-1] = op
            else:
                lst.append(op)
        for k in writes:
            lw[k] = op
            rd[k] = []
        deps.discard(op)
        op.deps = deps
        self.ops.append(op)
        if dma:
            ld = self.last_dmas[eng]
            ld.append(op)
            if len(ld) > DMA_K:
                ld.pop(0)
        else:
            self.last_on[eng] = op
        return op

    def barrier(self):
        deps = [o for o in self.last_on.values() if o is not None]
        for e in ENGS:
            deps.extend(self.last_dmas[e])
        self.barrier_deps = tuple(deps)
        self._fresh = {e: True for e in ENGS}
        self.lw = {}
        self.rd = {}

    def prepare(self, sems, dma_sems):
        ops = self.ops
        same = self.same
        for op in ops:
            for d in op.deps:
                if d.is_dma:
                    continue
                if d.eng == op.eng and not op.is_dma and (d.eng == "pe" or not same):
                    continue
                d.signal = True
        cnt = {e: 0 for e in ENGS}
        dcnt = {e: 0 for e in ENGS}
        hist = {e: [] for e in ENGS}
        for op in ops:
            if op.is_dma:
                n = dcnt[op.eng]
                dcnt[op.eng] = n + 1
                op.dsem = dma_sems[op.eng][n % DMA_K]
                op.dval = 16 * (n // DMA_K + 1)
                h = hist[op.eng]
                op.dprev = h[n - DMA_K] if n >= DMA_K else None
                h.append(op)
            elif op.signal:
                cnt[op.eng] += 1
                op.sigval = cnt[op.eng]
        self.by_eng = {e: [] for e in ENGS}
        for op in ops:
            self.by_eng[op.eng].append(op)
        self.sems = sems

    def run(self, eng_name, e):
        waited = {}
        sems = self.sems
        same = self.same
        for op in self.by_eng[eng_name]:
            need = {}
            deps = list(op.deps)
            if op.is_dma and op.dprev is not None:
                deps.append(op.dprev)
            for d in deps:
                if d.is_dma:
                    s, v = d.dsem, d.dval
                else:
                    if d.eng == eng_name and not op.is_dma and (eng_name == "pe" or not same):
                        continue
                    if not d.signal:
                        continue
                    s, v = sems[d.eng], d.sigval
                if waited.get(s, 0) >= v:
                    continue
                if need.get(s, 0) < v:
                    need[s] = v
            for s, v in need.items():
                e.wait_ge(s, v)
                waited[s] = v
            ins = op.fn(e)
            if op.is_dma:
                ins.then_inc(op.dsem, 16)
            elif op.signal:
                ins.then_inc(sems[eng_name], 1)
        last = {}
        for op in self.by_eng[eng_name]:
            if op.is_dma:
                last[op.dsem] = op.dval
        for s, v in last.items():
            e.wait_ge(s, v)


def alibi_slopes(n_heads):
    return np.array([2.0 ** (-8.0 * (h + 1) / n_heads) for h in range(n_heads)], dtype=np.float64)


def _bf16_round(x):
    u = np.asarray(x, np.float32).view(np.uint32).astype(np.uint64)
    r = ((u + 0x7FFF + ((u >> 16) & 1)) & 0xFFFF0000).astype(np.uint32)
    return r.view(np.float32)


def make_bias_tables():
    slopes = alibi_slopes(24)
    dil = [1, 4, 16]
    kj = np.arange(128)[:, None]
    qi = np.arange(128)[None, :]
    tab = np.zeros((128, 24, 256), np.float64)
    NEG = -240000.0
    for h in range(24):
        a = slopes[h] * dil[h // 8] * 8.0
        prev = np.where(qi <= kj, -a * (128 + qi - kj), NEG)
        cur = np.where(qi >= kj, -a * (qi - kj), NEG)
        tab[:, h, 0:128] = prev
        tab[:, h, 128:256] = cur
    hi = _bf16_round(tab.astype(np.float32))
    lo = _bf16_round((tab - hi.astype(np.float64)).astype(np.float32))
    return hi.reshape(128, 24 * 256), lo.reshape(128, 24 * 256)


def build_program(n_layers=NL, debug=False):
    nc = bass.Bass("TRN2", target_bir_lowering=False)
    S = Sched()

    def din(name, shape, dt=F32):
        return nc.dram_tensor(name, list(shape), dt, kind="ExternalInput").ap()

    def dscr(name, shape, dt):
        return nc.dram_tensor(name, list(shape), dt, kind=("ExternalOutput" if debug else "Internal")).ap()

    xT_in = din("xT", [D, L])
    cT_in = din("cT", [128, 16])
    w_mod = din("w_mod", [NL, D, 6 * D])
    w_in = din("w_in", [NL, D, N_IN])
    w_glu = din("w_glu", [NL, 512, 512])
    w_ssm_out = din("w_ssm_out", [NL, 512, D])
    w_attn_out = din("w_attn_out", [NL, 512, D])
    w_conv_out = din("w_conv_out", [NL, 512, D])
    w_o = din("w_o", [NL, D, D])
    w_up = din("w_up", [NL, D, 2 * DFF])
    w_down = din("w_down", [NL, DFF, D])
    bmodT_in = din("bmodT", [NL, 128, 96])
    gains_in = din("gains", [NL, 128, 64])
    bgateT_in = din("bgateT", [NL, 128, 48])
    ssm_sm_in = din("ssm_sm", [NL, 128, 48])
    ssm_flat_in = din("ssm_flat", [NL, 3, 2048])
    BTre_in = din("BTre", [NL, 128, 2048])
    BTim_in = din("BTim", [NL, 128, 2048])
    CTre_in = din("CTre", [NL, 128, 2048])
    CTim_in = din("CTim", [NL, 128, 2048])
    dT_in = din("dT", [NL, 128, 4])
    bgluT_in = din("bgluT", [NL, 128, 4])
    convwT_in = din("convwT", [NL, 128, 12])
    ffnwT_in = din("ffnwT", [NL, 128, 264])
    iota_in = din("iota", [128, 512])
    ones_in = din("ones", [128, 128])
    bias_hi_in = din("bias_hi", [128, 24 * 256])
    bias_lo_in = din("bias_lo", [128, 24 * 256])

    outT = nc.dram_tensor("outT", [D, L], F32, kind="ExternalOutput").ap()
    XT = dscr("XT", [D, L], F32)
    YT = dscr("YT", [D, L], F32)
    PROJ = dscr("PROJ", [N_IN, L], BF16)
    VTOK = dscr("VTOK", [L, 1536], BF16)
    GS = dscr("GS", [DFF, L], BF16)

    st = ExitStack()
    with st:
        def sbt(name, shape, dt):
            return st.enter_context(nc.sbuf_tensor("sb_" + name, list(shape), dt))

        R1 = sbt("R1", [128, 32768], BF16)
        WSall = sbt("WS", [128, 24576], BF16)
        WSa = [WSall[:, 8192 * i:8192 * (i + 1)] for i in range(3)]
        BRa = sbt("BR", [128, 24576], BF16)
        Ta = sbt("T", [128, 12288], BF16)
        ACC = sbt("ACC", [128, 2048], F32)
        RSTDY = sbt("RSTDY", [128, 2048], F32)
        PS = st.enter_context(nc.psum_tensor("PS", [128, 4096], F32))
        iota = sbt("iota", [128, 512], F32)
        ones32 = sbt("ones32", [128, 128], F32)
        onesbf = sbt("onesbf", [128, 128], BF16)
        epsT = sbt("epsT", [128, 1], F32)
        condT = sbt("condT", [128, 16], BF16)
        cTs = sbt("cTs", [128, 16], F32)
        modT = sbt("modT", [128, 96], F32)
        bmodT = sbt("bmodT", [128, 96], F32)
        gains = sbt("gains", [128, 64], F32)
        AG = sbt("AG", [128, 64], F32)
        bgateT = sbt("bgateT", [128, 48], F32)
        ssm_sm = sbt("ssm_sm", [128, 48], F32)
        sm_r = sbt("sm_r", [128, 16], F32)
        sm_th = sbt("sm_th", [128, 16], F32)
        sm_cT = sbt("sm_cT", [128, 16], F32)
        sm_sT = sbt("sm_sT", [128, 16], F32)
        sm_tmp = sbt("sm_tmp", [128, 96], F32)
        carry = sbt("carry", [128, 8], F32)
        cst = sbt("cst", [128, 4], F32)
        dTt = sbt("dTt", [128, 4], F32)
        bgluT = sbt("bgluT", [128, 4], F32)
        convwT = sbt("convwT", [128, 12], F32)
        ffnwT = sbt("ffnwT", [128, 264], F32)

        def psb(b, n=1):
            return PS[:, 512 * b:512 * (b + n)]

        def pk(b, n=1):
            return ["ps%d" % i for i in range(b, b + n)]

        class Arena:
            def __init__(self, t, size, name):
                self.t = t
                self.size = size
                self.off = 0
                self.name = name

            def reset(self):
                self.off = 0

            def bf(self, n):
                assert self.off + n <= self.size, (self.name, self.off, n, self.size)
                a = self.t[:, self.off:self.off + n]
                self.off += n
                return a

            def f32(self, n):
                return self.bf(2 * n).bitcast(F32)

        aR1 = Arena(R1, 32768, "R1")
        aBR = Arena(BRa, 24576, "BR")
        aT = Arena(Ta, 12288, "T")

        def reset_arenas():
            aR1.reset()
            aBR.reset()
            aT.reset()

        def A(eng, fn, r=(), w=(), dma=False):
            S.add(eng, fn, r, w, dma)

        def dma_sp(out, in_, r=(), w=()):
            A("sp", lambda e: e.dma_start(out=out, in_=in_), r, w, True)

        def dma_pool(out, in_, r=(), w=()):
            A("pool", lambda e: e.dma_start(out=out, in_=in_), r, w, True)

        dma_sp(iota[:], iota_in[:, :], w=["iota"])
        dma_sp(ones32[:], ones_in[:, :], w=["ones32"])
        dma_pool(onesbf[:], ones_in[:, :], w=["onesbf"])
        dma_sp(cTs[:], cT_in[:, :], w=["cTs"])
        A("dve", lambda e: e.memset(epsT[:], 1e-6), w=["epsT"])
        A("dve", lambda e: e.memset(cst[:, 0:1], 0.0), w=["cst"])
        A("dve", lambda e: e.memset(cst[:, 1:2], PI / 2), w=["cst"])
        A("dve", lambda e: e.memset(cst[:, 2:3], 12582912.0), w=["cst"])
        A("dve", lambda e: e.memset(cst[:, 3:4], -12582912.0), w=["cst"])
        A("act", lambda e: e.activation(out=condT[:], in_=cTs[:], func=AF.Silu), r=["cTs"], w=["condT"])
        S.barrier()

        def sincos(x_ap, n, out_sin, out_cos, tmpk, tmpp, keyp):
            M = 12582912.0
            C1 = 6.28125
            C2 = float(2 * np.pi - 6.28125)
            for which, outp, shift in (("s", out_sin, 0.0), ("c", out_cos, PI / 2)):
                kk = keyp + which
                A("dve", lambda e, shift=shift: e.tensor_scalar(out=tmpp, in0=x_ap, scalar1=shift, scalar2=None, op0=ALU.add), r=[keyp + "x"], w=[keyp + "p"])
                A("dve", lambda e: e.tensor_scalar(out=tmpk, in0=tmpp, scalar1=float(1 / (2 * np.pi)), scalar2=M, op0=ALU.mult, op1=ALU.add), r=[keyp + "p"], w=[keyp + "k"])
                A("dve", lambda e: e.tensor_scalar(out=tmpk, in0=tmpk, scalar1=-M, scalar2=None, op0=ALU.add), r=[keyp + "k"], w=[keyp + "k"])
                A("dve", lambda e: e.scalar_tensor_tensor(out=tmpp, in0=tmpk, scalar=-C1, in1=tmpp, op0=ALU.mult, op1=ALU.add), r=[keyp + "k", keyp + "p"], w=[keyp + "p"])
                A("dve", lambda e: e.scalar_tensor_tensor(out=tmpp, in0=tmpk, scalar=-C2, in1=tmpp, op0=ALU.mult, op1=ALU.add), r=[keyp + "k", keyp + "p"], w=[keyp + "p"])
                A("dve", lambda e: e.tensor_scalar(out=tmpp, in0=tmpp, scalar1=PI, scalar2=-PI, op0=ALU.min, op1=ALU.max), r=[keyp + "p"], w=[keyp + "p"])
                A("act", lambda e, outp=outp: e.activation(out=outp, in_=tmpp, func=AF.Sin), r=[keyp + "p"], w=[kk])

        def load_w(slot, dram_ap3, nkt, ncol, key):
            v = WSa[slot][:, 0:nkt * ncol].rearrange("p (k n) -> p k n", k=nkt)
            dma_pool(v, dram_ap3, w=[key])
            return v

        def phase_mod(l):
            S.mark("phase_mod")
            reset_arenas()
            dma_sp(bmodT[:], bmodT_in[l], w=["bmodT"])
            dma_sp(gains[:], gains_in[l], w=["gains"])
            dma_sp(bgateT[:], bgateT_in[l], w=["bgateT"])
            wv = w_mod[l].rearrange("(kt p) n -> p kt n", p=128)
            for cg in range(24):
                slot = cg % 3
                W = load_w(slot, wv[:, :, 512 * cg:512 * (cg + 1)], 16, 512, "WS%d" % slot)
                for j in range(4):
                    ct = 4 * cg + j
                    for kt in range(16):
                        A("pe", lambda e, W=W, j=j, kt=kt, ct=ct: e.matmul(PS[:, 1024 + ct:1025 + ct], W[:, kt, 128 * j:128 * (j + 1)], condT[:, kt:kt + 1], start=(kt == 0), stop=(kt == 15)),
                          r=["WS%d" % slot, "condT"], w=["ps2"])
            A("dve", lambda e: e.tensor_tensor(out=modT[:], in0=PS[:, 1024:1120], in1=bmodT[:], op=ALU.add), r=["ps2", "bmodT"], w=["modT"])
            A("dve", lambda e: e.scalar_tensor_tensor(out=AG[:, 0:16], in0=modT[:, 16:32], scalar=1.0, in1=gains[:, 0:16], op0=ALU.add, op1=ALU.mult), r=["modT", "gains"], w=["AG"])
            A("dve", lambda e: e.tensor_tensor(out=AG[:, 16:32], in0=modT[:, 32:48], in1=gains[:, 16:32], op=ALU.mult), r=["modT", "gains"], w=["AG"])
            A("dve", lambda e: e.scalar_tensor_tensor(out=AG[:, 32:48], in0=modT[:, 64:80], scalar=1.0, in1=gains[:, 32:48], op0=ALU.add, op1=ALU.mult), r=["modT", "gains"], w=["AG"])
            A("dve", lambda e: e.tensor_tensor(out=AG[:, 48:64], in0=modT[:, 80:96], in1=gains[:, 48:64], op=ALU.mult), r=["modT", "gains"], w=["AG"])
            S.barrier()

        def phase_norm(src_x, has_y, Gap, Aap, Bap, dst_x, make_h):
            S.mark("phase_norm")
            reset_arenas()
            hT = aR1.bf(32768).rearrange("p (k t) -> p k t", k=16)
            XN = aBR.f32(8192).rearrange("p (k t) -> p k t", k=16)
            NB = 3
            xt = [aT.f32(512) for _ in range(NB)]
            yt = [aT.f32(512) for _ in range(NB)]
            tmp = [aT.f32(512) for _ in range(2)]
            sq = [aT.bf(512) for _ in range(2)]
            rt = aT.f32(512)
            rstd = aT.f32(512)
            steps = [(c, ft) for c in range(NCH) for ft in range(16)]
            loaded = [0]

            def emit_loads(upto):
                while loaded[0] < min(upto, len(steps)):
                    i = loaded[0]
                    c, ft = steps[i]
                    bb = i % NB
                    rows = slice(128 * ft, 128 * (ft + 1))
                    cs = slice(TC * c, TC * (c + 1))
                    dma_sp(xt[bb], src_x[rows, cs], w=["xt%d" % bb])
                    dma_sp(yt[bb], YT[rows, cs], w=["yt%d" % bb])
                    loaded[0] += 1

            for i, (c, ft) in enumerate(steps):
                cs = slice(TC * c, TC * (c + 1))
                pb = c % 2
                b = ft % 2
                rows = slice(128 * ft, 128 * (ft + 1))
                kx = "XN%d" % ft
                if has_y:
                    emit_loads(i + NB)
                    bb = i % NB
                    A("dve", lambda e, b=b, bb=bb, cs=cs: e.tensor_tensor(out=tmp[b], in0=yt[bb], in1=RSTDY[:, cs], op=ALU.mult), r=["yt%d" % bb, "RSTDY"], w=["tmp%d" % b])
                    A("dve", lambda e, b=b, bb=bb, ft=ft: e.scalar_tensor_tensor(out=XN[:, ft, :], in0=tmp[b], scalar=Gap[:, ft:ft + 1], in1=xt[bb], op0=ALU.mult, op1=ALU.add),
                      r=["tmp%d" % b, "xt%d" % bb, "AG"], w=[kx])
                else:
                    dma_sp(XN[:, ft, :], src_x[rows, cs], w=[kx])
                if dst_x is not None:
                    dma_pool(dst_x[rows, cs], XN[:, ft, :], r=[kx])
                if make_h:
                    A("act", lambda e, b=b, ft=ft: e.activation(out=sq[b], in_=XN[:, ft, :], func=AF.Square), r=[kx], w=["sq%d" % b])
                    A("pe", lambda e, b=b, ft=ft, pb=pb: e.matmul(psb(pb), onesbf[:], sq[b], start=(ft == 0), stop=(ft == 15)), r=["sq%d" % b, "onesbf"], w=pk(pb))
                if make_h and ft == 15:
                    A("act", lambda e, pb=pb: e.activation(out=rt, in_=psb(pb), func=AF.Sqrt, bias=epsT[:, 0:1], scale=1.0 / D), r=pk(pb) + ["epsT"], w=["rt"])
                    A("dve", lambda e: e.reciprocal(out=rstd, in_=rt), r=["rt"], w=["rstd"])
                    for f2 in range(16):
                        b2 = f2 % 2
                        A("dve", lambda e, b2=b2, f2=f2: e.tensor_tensor(out=tmp[b2], in0=XN[:, f2, :], in1=rstd, op=ALU.mult), r=["XN%d" % f2, "rstd"], w=["tmp%d" % b2])
                        A("act", lambda e, b2=b2, f2=f2, cs=cs: e.activation(out=hT[:, f2, cs], in_=tmp[b2], func=AF.Identity, bias=Bap[:, f2:f2 + 1], scale=Aap[:, f2:f2 + 1]),
                          r=["tmp%d" % b2, "AG", "modT"], w=["hT%d" % f2])
            S.barrier()

        half_ctr = [0]

        def phase_inproj(l):
            S.mark("phase_inproj")
            reset_arenas()
            hT = aR1.bf(32768).rearrange("p (k t) -> p k t", k=16)
            stg = [aT.bf(2048) for _ in range(2)]
            vst = [aT.bf(512) for _ in range(2)]
            wv = w_in[l].rearrange("(kt p) n -> p kt n", p=128)
            hkeys = ["hT%d" % k for k in range(16)]
            vb = 0
            ev = 0
            for cg in range(25):
                slot = cg % 3
                wk = "WS%d" % slot
                W = load_w(slot, wv[:, :, 512 * cg:512 * (cg + 1)], 16, 512, wk)
                if 7 <= cg <= 9:
                    for tt in range(16):
                        bank = vb % 8
                        vb += 1
                        for kt in range(16):
                            A("pe", lambda e, W=W, kt=kt, tt=tt, bank=bank: e.matmul(psb(bank), hT[:, kt, 128 * tt:128 * (tt + 1)], W[:, kt, :], start=(kt == 0), stop=(kt == 15)),
                              r=[wk, "hT%d" % kt], w=pk(bank))
                        b = tt % 2
                        if tt % 2 == 0:
                            A("act", lambda e, b=b, bank=bank: e.activation(out=vst[b], in_=psb(bank), func=AF.Copy), r=pk(bank), w=["vst%d" % b])
                        else:
                            A("dve", lambda e, b=b, bank=bank: e.tensor_copy(out=vst[b], in_=psb(bank)), r=pk(bank), w=["vst%d" % b])
                        dma_sp(VTOK[128 * tt:128 * (tt + 1), 512 * (cg - 7):512 * (cg - 6)], vst[b], r=["vst%d" % b])
                    continue
                for j in range(4):
                    pt = 4 * cg + j
                    hb = 4 * (half_ctr[0] % 2)
                    half_ctr[0] += 1
                    for c in range(NCH):
                        for kt in range(16):
                            A("pe", lambda e, W=W, kt=kt, j=j, c=c, hb=hb: e.matmul(psb(hb + c), W[:, kt, 128 * j:128 * (j + 1)], hT[:, kt, TC * c:TC * (c + 1)], start=(kt == 0), stop=(kt == 15)),
                              r=[wk, "hT%d" % kt], w=pk(hb + c))
                    b = ev % 2
                    ev += 1
                    src = psb(hb, 4)
                    dst = stg[b]
                    sk = "stg%d" % b
                    if cg >= 13:
                        gi_ = pt - 52
                        A("act", lambda e, src=src, dst=dst, gi_=gi_: e.activation(out=dst, in_=src, func=AF.Sigmoid, bias=bgateT[:, gi_:gi_ + 1]), r=pk(hb, 4) + ["bgateT"], w=[sk])
                    elif cg in (2, 5, 3, 6):
                        dil = 4 if cg in (2, 5) else 16
                        A("dve", lambda e, src=src, dst=dst, dil=dil: e.tensor_copy(out=dst.rearrange("p (r m) -> p r m", r=dil), in_=src.rearrange("p (m r) -> p r m", r=dil)), r=pk(hb, 4), w=[sk])
                    elif ev % 2 == 0:
                        A("act", lambda e, src=src, dst=dst: e.activation(out=dst, in_=src, func=AF.Copy), r=pk(hb, 4), w=[sk])
                    else:
                        A("dve", lambda e, src=src, dst=dst: e.tensor_copy(out=dst, in_=src), r=pk(hb, 4), w=[sk])
                    dma_sp(PROJ[128 * pt:128 * (pt + 1), :], dst, r=[sk], w=["PROJ%d" % pt])
            S.barrier()

        def phase_ssm(l):
            S.mark("phase_ssm")
            reset_arenas()
            dma_sp(ssm_sm[:], ssm_sm_in[l], w=["ssm_sm"])
            dma_sp(dTt[:], dT_in[l], w=["dTt"])
            dma_sp(bgluT[:], bgluT_in[l], w=["bgluT"])
            dt_sm = sm_tmp[:, 0:16]
            A("act", lambda e: e.activation(out=dt_sm, in_=ssm_sm[:, 0:16], func=AF.Exp), r=["ssm_sm"], w=["dt_sm"])
            A("dve", lambda e: e.tensor_tensor(out=sm_tmp[:, 16:32], in0=ssm_sm[:, 16:32], in1=dt_sm, op=ALU.mult), r=["dt_sm", "ssm_sm"], w=["ardt"])
            A("act", lambda e: e.activation(out=sm_r[:], in_=sm_tmp[:, 16:32], func=AF.Exp), r=["ardt"], w=["sm_r"])
            A("dve", lambda e: e.tensor_tensor(out=sm_th[:], in0=ssm_sm[:, 32:48], in1=dt_sm, op=ALU.mult), r=["dt_sm", "ssm_sm"], w=["sm_th"])
            A("dve", lambda e: e.tensor_scalar(out=sm_tmp[:, 32:48], in0=sm_th[:], scalar1=float(TC), scalar2=None, op0=ALU.mult), r=["sm_th"], w=["smTx"])
            sincos(sm_tmp[:, 32:48], 16, sm_sT[:], sm_cT[:], sm_tmp[:, 48:64], sm_tmp[:, 64:80], "smT")
            BbrT = aR1.bf(2048).rearrange("p (s n) -> p s n", s=16)
            BbiT = aR1.bf(2048).rearrange("p (s n) -> p s n", s=16)
            CrT = aR1.bf(2048).rearrange("p (s n) -> p s n", s=16)
            CiT = aR1.bf(2048).rearrange("p (s n) -> p s n", s=16)
            Wglu = aR1.bf(2048).rearrange("p (k n) -> p k n", k=4)
            uT = aR1.bf(8192).rearrange("p (k t) -> p k t", k=4)
            GL = aR1.bf(8192).rearrange("p (k t) -> p k t", k=4)
            dma_pool(CrT.rearrange("p s n -> p (s n)"), CTre_in[l], w=["CrT"])
            dma_pool(CiT.rearrange("p s n -> p (s n)"), CTim_in[l], w=["CiT"])
            dma_pool(Wglu, w_glu[l].rearrange("(k p) n -> p k n", p=128), w=["Wglu"])
            for k in range(4):
                dma_sp(uT[:, k, :], PROJ[128 * k:128 * (k + 1), :], w=["uT%d" % k])
            fl = [aBR.f32(512) for _ in range(14)]
            (f_ld, f_ar, f_ai, f_dt, f_mag, f_th, f_sin, f_cos, f_k, f_p, f_a, f_b, f_fr, f_fi) = fl
            f_bre = aBR.f32(512)
            f_bim = aBR.f32(512)
            for q in range(4):
                qs = slice(512 * q, 512 * (q + 1))
                dma_sp(f_ld, ssm_flat_in[l, 0:1, qs].to_broadcast([128, 512]), w=["f_ld"])
                dma_sp(f_ar, ssm_flat_in[l, 1:2, qs].to_broadcast([128, 512]), w=["f_ar"])
                dma_sp(f_ai, ssm_flat_in[l, 2:3, qs].to_broadcast([128, 512]), w=["f_ai"])
                dma_sp(f_bre, BTre_in[l, :, qs], w=["f_bre"])
                dma_sp(f_bim, BTim_in[l, :, qs], w=["f_bim"])
                A("act", lambda e: e.activation(out=f_dt, in_=f_ld, func=AF.Exp), r=["f_ld"], w=["f_dt"])
                A("dve", lambda e: e.tensor_tensor(out=f_a, in0=f_ar, in1=f_dt, op=ALU.mult), r=["f_ar", "f_dt"], w=["f_a"])
                A("act", lambda e: e.activation(out=f_mag, in_=f_a, func=AF.Exp), r=["f_a"], w=["f_mag"])
                A("dve", lambda e: e.tensor_tensor(out=f_th, in0=f_ai, in1=f_dt, op=ALU.mult), r=["f_ai", "f_dt"], w=["flx"])
                sincos(f_th, 512, f_sin, f_cos, f_k, f_p, "fl")
                A("dve", lambda e: e.tensor_tensor(out=f_cos, in0=f_cos, in1=f_mag, op=ALU.mult), r=["flc", "f_mag"], w=["flc"])
                A("dve", lambda e: e.tensor_tensor(out=f_sin, in0=f_sin, in1=f_mag, op=ALU.mult), r=["fls", "f_mag"], w=["fls"])
                A("dve", lambda e: e.tensor_scalar(out=f_cos, in0=f_cos, scalar1=-1.0, scalar2=None, op0=ALU.add), r=["flc"], w=["flc"])
                A("dve", lambda e: e.tensor_tensor(out=f_a, in0=f_ar, in1=f_ar, op=ALU.mult), r=["f_ar"], w=["f_a"])
                A("dve", lambda e: e.tensor_tensor(out=f_b, in0=f_ai, in1=f_ai, op=ALU.mult), r=["f_ai"], w=["f_b"])
                A("dve", lambda e: e.tensor_tensor(out=f_a, in0=f_a, in1=f_b, op=ALU.add), r=["f_a", "f_b"], w=["f_a"])
                A("dve", lambda e: e.reciprocal(out=f_a, in_=f_a), r=["f_a"], w=["f_a"])
                A("dve", lambda e: e.tensor_tensor(out=f_fr, in0=f_cos, in1=f_ar, op=ALU.mult), r=["flc", "f_ar"], w=["f_fr"])
                A("dve", lambda e: e.tensor_tensor(out=f_b, in0=f_sin, in1=f_ai, op=ALU.mult), r=["fls", "f_ai"], w=["f_b"])
                A("dve", lambda e: e.tensor_tensor(out=f_fr, in0=f_fr, in1=f_b, op=ALU.add), r=["f_fr", "f_b"], w=["f_fr"])
                A("dve", lambda e: e.tensor_tensor(out=f_fr, in0=f_fr, in1=f_a, op=ALU.mult), r=["f_fr", "f_a"], w=["f_fr"])
                A("dve", lambda e: e.tensor_tensor(out=f_fi, in0=f_sin, in1=f_ar, op=ALU.mult), r=["fls", "f_ar"], w=["f_fi"])
                A("dve", lambda e: e.tensor_tensor(out=f_b, in0=f_cos, in1=f_ai, op=ALU.mult), r=["flc", "f_ai"], w=["f_b"])
                A("dve", lambda e: e.tensor_tensor(out=f_fi, in0=f_fi, in1=f_b, op=ALU.subtract), r=["f_fi", "f_b"], w=["f_fi"])
                A("dve", lambda e: e.tensor_tensor(out=f_fi, in0=f_fi, in1=f_a, op=ALU.mult), r=["f_fi", "f_a"], w=["f_fi"])
                brv = BbrT.rearrange("p s n -> p (s n)")[:, qs]
                biv = BbiT.rearrange("p s n -> p (s n)")[:, qs]
                A("dve", lambda e: e.tensor_tensor(out=f_k, in0=f_fr, in1=f_bre, op=ALU.mult), r=["f_fr", "f_bre"], w=["flk"])
                A("dve", lambda e: e.tensor_tensor(out=f_p, in0=f_fi, in1=f_bim, op=ALU.mult), r=["f_fi", "f_bim"], w=["flp"])
                A("dve", lambda e, brv=brv: e.tensor_tensor(out=brv, in0=f_k, in1=f_p, op=ALU.subtract), r=["flk", "flp"], w=["BbrT"])
                A("dve", lambda e: e.tensor_tensor(out=f_k, in0=f_fr, in1=f_bim, op=ALU.mult), r=["f_fr", "f_bim"], w=["flk"])
                A("dve", lambda e: e.tensor_tensor(out=f_p, in0=f_fi, in1=f_bre, op=ALU.mult), r=["f_fi", "f_bre"], w=["flp"])
                A("dve", lambda e, biv=biv: e.tensor_tensor(out=biv, in0=f_k, in1=f_p, op=ALU.add), r=["flk", "flp"], w=["BbiT"])
            aBR.reset()
            cosT = aBR.f32(512)
            sinT = aBR.f32(512)
            tk = aBR.f32(512)
            tp = aBR.f32(512)
            tk2 = aBR.f32(512)
            tp2 = aBR.f32(512)
            xr_s = [aBR.f32(512) for _ in range(2)]
            xi_s = [aBR.f32(512) for _ in range(2)]
            t1 = aBR.f32(512)
            t2 = aBR.f32(512)
            t3 = aBR.f32(512)
            t4 = aBR.f32(512)
            zr = aBR.f32(512)
            zi = aBR.f32(512)
            gr = [aBR.f32(512) for _ in range(2)]
            gi = [aBR.f32(512) for _ in range(2)]
            hr = [aBR.bf(512) for _ in range(2)]
            nhi = [aBR.bf(512) for _ in range(2)]
            yv = aBR.f32(512)
            y2 = aBR.f32(512)
            ctr = 0
            for stt in range(16):
                ut = stt // 4
                C1_ = 6.28125
                C2_ = float(2 * np.pi - 6.28125)
                if not USE_ACT_TABLES:
                    A("dve", lambda e, stt=stt: e.tensor_scalar(out=tk2, in0=iota[:], scalar1=sm_th[:, stt:stt + 1], scalar2=None, op0=ALU.mult), r=["iota", "sm_th"], w=["tbx"])
                    sincos(tk2, 512, sinT, cosT, tk, tp, "tb")
                for which, outp, sh_i, tkk, tpp in ((("s", sinT, 0, tk, tp), ("c", cosT, 1, tk2, tp2)) if USE_ACT_TABLES else ()):
                    A("act", lambda e, stt=stt, sh_i=sh_i, tpp=tpp: e.activation(out=tpp, in_=iota[:], func=AF.Identity, bias=cst[:, sh_i:sh_i + 1], scale=sm_th[:, stt:stt + 1]), r=["iota", "sm_th", "cst"], w=["tbp" + which])
                    A("act", lambda e, tkk=tkk, tpp=tpp: e.activation(out=tkk, in_=tpp, func=AF.Identity, bias=cst[:, 2:3], scale=float(1 / (2 * np.pi))), r=["tbp" + which, "cst"], w=["tbk" + which])
                    A("act", lambda e, tkk=tkk: e.activation(out=tkk, in_=tkk, func=AF.Identity, bias=cst[:, 3:4], scale=1.0), r=["tbk" + which, "cst"], w=["tbk" + which])
                    A("dve", lambda e, tkk=tkk, tpp=tpp: e.scalar_tensor_tensor(out=tpp, in0=tkk, scalar=-C1_, in1=tpp, op0=ALU.mult, op1=ALU.add), r=["tbk" + which, "tbp" + which], w=["tbp" + which])
                    A("dve", lambda e, tkk=tkk, tpp=tpp: e.scalar_tensor_tensor(out=tpp, in0=tkk, scalar=-C2_, in1=tpp, op0=ALU.mult, op1=ALU.add), r=["tbk" + which, "tbp" + which], w=["tbp" + which])
                    A("pool", lambda e, tpp=tpp: e.tensor_scalar(out=tpp, in0=tpp, scalar1=PI, scalar2=-PI, op0=ALU.min, op1=ALU.max), r=["tbp" + which], w=["tbp" + which])
                    A("act", lambda e, outp=outp, tpp=tpp: e.activation(out=outp, in_=tpp, func=AF.Sin), r=["tbp" + which], w=["tb" + which])
                for c in range(NCH):
                    cs = slice(TC * c, TC * (c + 1))
                    b = ctr % 2
                    ctr += 1
                    bx = 4 + 2 * b
                    A("pe", lambda e, stt=stt, ut=ut, cs=cs, bx=bx: e.matmul(psb(bx), BbrT[:, stt, :], uT[:, ut, cs], start=True, stop=True), r=["BbrT", "uT%d" % ut], w=pk(bx))
                    A("pe", lambda e, stt=stt, ut=ut, cs=cs, bx=bx: e.matmul(psb(bx + 1), BbiT[:, stt, :], uT[:, ut, cs], start=True, stop=True), r=["BbiT", "uT%d" % ut], w=pk(bx + 1))
                    A("act", lambda e, b=b, bx=bx: e.activation(out=xr_s[b], in_=psb(bx), func=AF.Copy), r=pk(bx), w=["xr%d" % b])
                    A("act", lambda e, b=b, bx=bx: e.activation(out=xi_s[b], in_=psb(bx + 1), func=AF.Copy), r=pk(bx + 1), w=["xi%d" % b])
                    A("dve", lambda e, b=b: e.tensor_tensor(out=t1, in0=xr_s[b], in1=cosT, op=ALU.mult), r=["xr%d" % b, "tbc"], w=["t1"])
                    A("pool", lambda e, b=b: e.tensor_tensor(out=t2, in0=xi_s[b], in1=sinT, op=ALU.mult), r=["xi%d" % b, "tbs"], w=["t2"])
                    A("dve", lambda e: e.tensor_tensor(out=zr, in0=t1, in1=t2, op=ALU.add), r=["t1", "t2"], w=["zr"])
                    A("pool", lambda e, b=b: e.tensor_tensor(out=t3, in0=xi_s[b], in1=cosT, op=ALU.mult), r=["xi%d" % b, "tbc"], w=["t3"])
                    A("dve", lambda e, b=b: e.tensor_tensor(out=t4, in0=xr_s[b], in1=sinT, op=ALU.mult), r=["xr%d" % b, "tbs"], w=["t4"])
                    A("pool", lambda e: e.tensor_tensor(out=zi, in0=t3, in1=t4, op=ALU.subtract), r=["t3", "t4"], w=["zi"])
                    rbc = sm_r[:, stt:stt + 1].to_broadcast([128, 512])
                    if c == 0:
                        A("dve", lambda e, b=b, rbc=rbc: e.tensor_tensor_scan(out=gr[b], data0=rbc, data1=zr, initial=0.0, op0=ALU.mult, op1=ALU.add), r=["zr", "sm_r"], w=["gr%d" % b])
                        A("dve", lambda e, b=b, rbc=rbc: e.tensor_tensor_scan(out=gi[b], data0=rbc, data1=zi, initial=0.0, op0=ALU.mult, op1=ALU.add), r=["zi", "sm_r"], w=["gi%d" % b])
                    else:
                        A("dve", lambda e, b=b, rbc=rbc: e.tensor_tensor_scan(out=gr[b], data0=rbc, data1=zr, initial=carry[:, 0:1], op0=ALU.mult, op1=ALU.add), r=["zr", "sm_r", "carry"], w=["gr%d" % b])
                        A("dve", lambda e, b=b, rbc=rbc: e.tensor_tensor_scan(out=gi[b], data0=rbc, data1=zi, initial=carry[:, 1:2], op0=ALU.mult, op1=ALU.add), r=["zi", "sm_r", "carry"], w=["gi%d" % b])
                    if c < NCH - 1:
                        cT_ = sm_cT[:, stt:stt + 1]
                        sT_ = sm_sT[:, stt:stt + 1]
                        A("dve", lambda e, b=b, cT_=cT_: e.tensor_tensor(out=carry[:, 2:3], in0=gr[b][:, 511:512], in1=cT_, op=ALU.mult), r=["gr%d" % b, "smTc"], w=["cy2"])
                        A("dve", lambda e, b=b, sT_=sT_: e.tensor_tensor(out=carry[:, 3:4], in0=gi[b][:, 511:512], in1=sT_, op=ALU.mult), r=["gi%d" % b, "smTs"], w=["cy3"])
                        A("dve", lambda e, b=b, sT_=sT_: e.tensor_tensor(out=carry[:, 4:5], in0=gr[b][:, 511:512], in1=sT_, op=ALU.mult), r=["gr%d" % b, "smTs"], w=["cy4"])
                        A("dve", lambda e, b=b, cT_=cT_: e.tensor_tensor(out=carry[:, 5:6], in0=gi[b][:, 511:512], in1=cT_, op=ALU.mult), r=["gi%d" % b, "smTc"], w=["cy5"])
                        A("dve", lambda e: e.tensor_tensor(out=carry[:, 0:1], in0=carry[:, 2:3], in1=carry[:, 3:4], op=ALU.subtract), r=["cy2", "cy3"], w=["carry"])
                        A("dve", lambda e: e.tensor_tensor(out=carry[:, 1:2], in0=carry[:, 4:5], in1=carry[:, 5:6], op=ALU.add), r=["cy4", "cy5", "carry"], w=["carry"])
                    A("dve", lambda e, b=b: e.tensor_tensor(out=t1, in0=gr[b], in1=cosT, op=ALU.mult), r=["gr%d" % b, "tbc"], w=["t1"])
                    A("pool", lambda e, b=b: e.tensor_tensor(out=t2, in0=gi[b], in1=sinT, op=ALU.mult), r=["gi%d" % b, "tbs"], w=["t2"])
                    A("dve", lambda e, b=b: e.tensor_tensor(out=hr[b], in0=t1, in1=t2, op=ALU.subtract), r=["t1", "t2"], w=["hr%d" % b])
                    A("pool", lambda e, b=b: e.tensor_tensor(out=t3, in0=gr[b], in1=sinT, op=ALU.mult), r=["gr%d" % b, "tbs"], w=["t3"])
                    A("dve", lambda e, b=b: e.tensor_tensor(out=t4, in0=gi[b], in1=cosT, op=ALU.mult), r=["gi%d" % b, "tbc"], w=["t4"])
                    A("dve", lambda e, b=b: e.scalar_tensor_tensor(out=nhi[b], in0=t3, scalar=-1.0, in1=t4, op0=ALU.mult, op1=ALU.subtract), r=["t3", "t4"], w=["nhi%d" % b])
                    first = (stt % 4 == 0)
                    last = (stt % 4 == 3)
                    A("pe", lambda e, stt=stt, b=b, c=c, first=first: e.matmul(psb(c), CrT[:, stt, :], hr[b], start=first, stop=False), r=["CrT", "hr%d" % b], w=pk(c))
                    A("pe", lambda e, stt=stt, b=b, c=c, last=last: e.matmul(psb(c), CiT[:, stt, :], nhi[b], start=False, stop=last), r=["CiT", "nhi%d" % b], w=pk(c))
                if stt % 4 == 3:
                    for c in range(NCH):
                        cs = slice(TC * c, TC * (c + 1))
                        A("dve", lambda e, ut=ut, cs=cs, c=c: e.scalar_tensor_tensor(out=yv, in0=uT[:, ut, cs], scalar=dTt[:, ut:ut + 1], in1=psb(c), op0=ALU.mult, op1=ALU.add), r=["uT%d" % ut, "dTt"] + pk(c), w=["yv"])
                        A("dve", lambda e: e.tensor_tensor(out=y2, in0=yv, in1=yv, op=ALU.mult), r=["yv"], w=["y2"])
                        A("dve", lambda e: e.tensor_scalar(out=y2, in0=y2, scalar1=0.044715, scalar2=1.0, op0=ALU.mult, op1=ALU.add), r=["y2"], w=["y2"])
                        A("dve", lambda e: e.tensor_tensor(out=y2, in0=y2, in1=yv, op=ALU.mult), r=["y2", "yv"], w=["y2"])
                        A("act", lambda e: e.activation(out=y2, in_=y2, func=AF.Sigmoid, scale=float(2.0 * np.sqrt(2.0 / np.pi))), r=["y2"], w=["y2"])
                        A("dve", lambda e, ut=ut, cs=cs: e.tensor_tensor(out=GL[:, ut, cs], in0=y2, in1=yv, op=ALU.mult), r=["y2", "yv"], w=["GL%d" % ut])
            S.barrier()
            BR = BRa[:, :].rearrange("p (k t) -> p k t", k=12)
            sg = aT.f32(512)
            n = 0
            for fo in range(4):
                for c in range(NCH):
                    cs = slice(TC * c, TC * (c + 1))
                    bank = 4 + (n % 4)
                    n += 1
                    for kt in range(4):
                        A("pe", lambda e, kt=kt, fo=fo, cs=cs, bank=bank: e.matmul(psb(bank), Wglu[:, kt, 128 * fo:128 * (fo + 1)], GL[:, kt, cs], start=(kt == 0), stop=(kt == 3)), r=["Wglu", "GL%d" % kt], w=pk(bank))
                    A("act", lambda e, fo=fo, bank=bank: e.activation(out=sg, in_=psb(bank), func=AF.Sigmoid, bias=bgluT[:, fo:fo + 1]), r=pk(bank) + ["bgluT"], w=["sg"])
                    A("dve", lambda e, fo=fo, cs=cs: e.tensor_tensor(out=BR[:, fo, cs], in0=GL[:, fo, cs], in1=sg, op=ALU.mult), r=["sg", "GL%d" % fo], w=["BR%d" % fo])
            S.barrier()

        def phase_conv(l):
            S.mark("phase_conv")
            reset_arenas()
            BR = BRa[:, :].rearrange("p (k t) -> p k t", k=12)
            dma_sp(convwT[:], convwT_in[l], w=["convwT"])
            cb = [aR1.bf(2048) for _ in range(2)]
            cc = [aR1.bf(2048) for _ in range(2)]
            ch = [aR1.bf(2048) for _ in range(2)]
            zb = aR1.f32(2064)
            acc = aR1.f32(2048)
            A("dve", lambda e: e.memset(zb[:, 0:16], 0.0), w=["zb"])
            for j in range(4):
                b = j % 2
                dma_sp(cb[b], PROJ[128 * (40 + j):128 * (41 + j), :], w=["cb%d" % b])
                dma_sp(cc[b], PROJ[128 * (44 + j):128 * (45 + j), :], w=["cc%d" % b])
                dma_sp(ch[b], PROJ[128 * (48 + j):128 * (49 + j), :], w=["ch%d" % b])
                A("dve", lambda e, b=b: e.tensor_tensor(out=zb[:, 16:2064], in0=cc[b], in1=ch[b], op=ALU.mult), r=["cc%d" % b, "ch%d" % b], w=["zb"])
                A("dve", lambda e, j=j: e.tensor_scalar(out=acc, in0=zb[:, 16:2064], scalar1=convwT[:, 3 * j:3 * j + 1], scalar2=None, op0=ALU.mult), r=["zb", "convwT"], w=["cacc"])
                A("dve", lambda e, j=j: e.scalar_tensor_tensor(out=acc, in0=zb[:, 15:2063], scalar=convwT[:, 3 * j + 1:3 * j + 2], in1=acc, op0=ALU.mult, op1=ALU.add), r=["zb", "convwT", "cacc"], w=["cacc"])
                A("dve", lambda e, j=j: e.scalar_tensor_tensor(out=acc, in0=zb[:, 14:2062], scalar=convwT[:, 3 * j + 2:3 * j + 3], in1=acc, op0=ALU.mult, op1=ALU.add), r=["zb", "convwT", "cacc"], w=["cacc"])
                A("dve", lambda e, j=j, b=b: e.tensor_tensor(out=BR[:, 8 + j, :], in0=acc, in1=cb[b], op=ALU.mult), r=["cacc", "cb%d" % b], w=["BR%d" % (8 + j)])
            S.barrier()

        def phase_attn(l):
            S.mark("phase_attn")
            reset_arenas()
            BR = BRa[:, :].rearrange("p (k t) -> p k t", k=12)
            bhi = aR1.bf(24 * 256).rearrange("p (h n) -> p h n", h=24)
            blo = aR1.bf(24 * 256).rearrange("p (h n) -> p h n", h=24)
            qT = [aR1.bf(2048) for _ in range(2)]
            kT = [aR1.bf(2048) for _ in range(2)]
            Vb = [aR1.bf(2048).rearrange("p (b f) -> p b f", b=16) for _ in range(2)]
            identb = aR1.bf(128)
            onesv = aR1.bf(64)
            Uacc = aT.f32(2048)
            Lacc = aT.f32(2048)
            pT = [aT.bf(256) for _ in range(2)]
            dma_pool(bhi.rearrange("p h n -> p (h n)"), bias_hi_in[:, :], w=["bhi"])
            dma_pool(blo.rearrange("p h n -> p (h n)"), bias_lo_in[:, :], w=["blo"])
            dma_pool(identb, ident_in[:, :], w=["identb"])
            A("dve", lambda e: e.memset(onesv, 1.0), w=["onesv"])
            dils = [1, 4, 16]
            n = 0
            sctr = 0
            uctr = 0
            for hp in range(4):
                for g in range(3):
                    dil = dils[g]
                    nb = 16 // dil
                    b = n % 2
                    n += 1
                    qt = 4 + 4 * g + hp
                    kt_ = 16 + 4 * g + hp
                    dma_sp(qT[b], PROJ[128 * qt:128 * (qt + 1), :], w=["qT%d" % b])
                    dma_sp(kT[b], PROJ[128 * kt_:128 * (kt_ + 1), :], w=["kT%d" % b])
                    vsrc = VTOK.rearrange("(bb i r) f -> i r bb f", i=128, r=dil)
                    for r in range(dil):
                        dma_sp(Vb[b][:, r * nb:(r + 1) * nb, :], vsrc[:, r, :, 512 * g + 128 * hp:512 * g + 128 * (hp + 1)], w=["Vb%d" % b])
                    for q4 in range(4):
                        ub = 2 + (uctr % 2)
                        lb = 4 + (uctr % 2)
                        uctr += 1
                        for bi4 in range(4):
                            bi = 4 * q4 + bi4
                            bblk = bi % nb
                            for hh in range(2):
                                head = 8 * g + 2 * hp + hh
                                prt = slice(64 * hh, 64 * (hh + 1))
                                sb_ = sctr % 2
                                sctr += 1
                                sps = PS[:, 256 * sb_:256 * (sb_ + 1)]
                                skey = ["pss%d" % sb_]
                                qv = qT[b][prt, 128 * bi:128 * (bi + 1)]
                                if bblk > 0:
                                    A("pe", lambda e, sps=sps, head=head: e.matmul(sps, identb, bhi[:, head, :], start=True, stop=False), r=["identb", "bhi"], w=skey)
                                    A("pe", lambda e, sps=sps, head=head: e.matmul(sps, identb, blo[:, head, :], start=False, stop=False), r=["identb", "blo"], w=skey)
                                    A("pe", lambda e, sps=sps, b=b, prt=prt, bi=bi, qv=qv: e.matmul(sps[:, 0:128], kT[b][prt, 128 * (bi - 1):128 * bi], qv, start=False, stop=False), r=["kT%d" % b, "qT%d" % b], w=skey)
                                    A("pe", lambda e, sps=sps, b=b, prt=prt, bi=bi, qv=qv: e.matmul(sps[:, 128:256], kT[b][prt, 128 * bi:128 * (bi + 1)], qv, start=False, stop=True), r=["kT%d" % b, "qT%d" % b], w=skey)
                                    A("act", lambda e, sps=sps, sb_=sb_: e.activation(out=pT[sb_], in_=sps, func=AF.Exp, scale=0.125), r=skey, w=["pT%d" % sb_])
                                    ucol = slice(128 * bi4, 128 * (bi4 + 1))
                                    A("pe", lambda e, ub=ub, prt=prt, ucol=ucol, b=b, bi=bi, hh=hh, sb_=sb_: e.matmul(psb(ub)[prt, ucol], Vb[b][:, bi - 1, 64 * hh:64 * (hh + 1)], pT[sb_][:, 0:128], start=True, stop=False), r=["Vb%d" % b, "pT%d" % sb_], w=pk(ub))
                                    A("pe", lambda e, ub=ub, prt=prt, ucol=ucol, b=b, bi=bi, hh=hh, sb_=sb_: e.matmul(psb(ub)[prt, ucol], Vb[b][:, bi, 64 * hh:64 * (hh + 1)], pT[sb_][:, 128:256], start=False, stop=True), r=["Vb%d" % b, "pT%d" % sb_], w=pk(ub))
                                    A("pe", lambda e, lb=lb, prt=prt, ucol=ucol, sb_=sb_: e.matmul(psb(lb)[prt, ucol], onesv, pT[sb_][:, 0:128], start=True, stop=False), r=["onesv", "pT%d" % sb_], w=pk(lb))
                                    A("pe", lambda e, lb=lb, prt=prt, ucol=ucol, sb_=sb_: e.matmul(psb(lb)[prt, ucol], onesv, pT[sb_][:, 128:256], start=False, stop=True), r=["onesv", "pT%d" % sb_], w=pk(lb))
                                else:
                                    sc_ = sps[:, 128:256]
                                    A("pe", lambda e, sc_=sc_, head=head: e.matmul(sc_, identb, bhi[:, head, 128:256], start=True, stop=False), r=["identb", "bhi"], w=skey)
                                    A("pe", lambda e, sc_=sc_, head=head: e.matmul(sc_, identb, blo[:, head, 128:256], start=False, stop=False), r=["identb", "blo"], w=skey)
                                    A("pe", lambda e, sc_=sc_, b=b, prt=prt, bi=bi, qv=qv: e.matmul(sc_, kT[b][prt, 128 * bi:128 * (bi + 1)], qv, start=False, stop=True), r=["kT%d" % b, "qT%d" % b], w=skey)
                                    A("act", lambda e, sc_=sc_, sb_=sb_: e.activation(out=pT[sb_][:, 128:256], in_=sc_, func=AF.Exp, scale=0.125), r=skey, w=["pT%d" % sb_])
                                    ucol = slice(128 * bi4, 128 * (bi4 + 1))
                                    A("pe", lambda e, ub=ub, prt=prt, ucol=ucol, b=b, bi=bi, hh=hh, sb_=sb_: e.matmul(psb(ub)[prt, ucol], Vb[b][:, bi, 64 * hh:64 * (hh + 1)], pT[sb_][:, 128:256], start=True, stop=True), r=["Vb%d" % b, "pT%d" % sb_], w=pk(ub))
                                    A("pe", lambda e, lb=lb, prt=prt, ucol=ucol, sb_=sb_: e.matmul(psb(lb)[prt, ucol], onesv, pT[sb_][:, 128:256], start=True, stop=True), r=["onesv", "pT%d" % sb_], w=pk(lb))
                        if dil == 1:
                            uo = Uacc[:, 512 * q4:512 * (q4 + 1)]
                            lo_ = Lacc[:, 512 * q4:512 * (q4 + 1)]
                            ui = psb(ub)
                            li = psb(lb)
                        elif dil == 4:
                            uo = Uacc.rearrange("p (m r) -> p r m", r=4)[:, q4, :]
                            lo_ = Lacc.rearrange("p (m r) -> p r m", r=4)[:, q4, :]
                            ui = psb(ub)
                            li = psb(lb)
                        else:
                            uo = Uacc.rearrange("p (i r) -> p i r", r=16)[:, :, 4 * q4:4 * (q4 + 1)]
                            lo_ = Lacc.rearrange("p (i r) -> p i r", r=16)[:, :, 4 * q4:4 * (q4 + 1)]
                            ui = psb(ub).rearrange("p (rr i) -> p i rr", rr=4)
                            li = psb(lb).rearrange("p (rr i) -> p i rr", rr=4)
                        if g == 0:
                            A("dve", lambda e, uo=uo, ui=ui: e.tensor_copy(out=uo, in_=ui), r=pk(ub), w=["Uacc"])
                            A("dve", lambda e, lo_=lo_, li=li: e.tensor_copy(out=lo_, in_=li), r=pk(lb), w=["Lacc"])
                        else:
                            A("dve", lambda e, uo=uo, ui=ui: e.tensor_tensor(out=uo, in0=uo, in1=ui, op=ALU.add), r=pk(ub) + ["Uacc"], w=["Uacc"])
                            A("dve", lambda e, lo_=lo_, li=li: e.tensor_tensor(out=lo_, in0=lo_, in1=li, op=ALU.add), r=pk(lb) + ["Lacc"], w=["Lacc"])
                A("dve", lambda e: e.reciprocal(out=Lacc, in_=Lacc), r=["Lacc"], w=["Lacc"])
                A("dve", lambda e, hp=hp: e.tensor_tensor(out=BR[:, 4 + hp, :], in0=Uacc, in1=Lacc, op=ALU.mult), r=["Uacc", "Lacc"], w=["BR%d" % (4 + hp)])
            S.barrier()

        def phase_merge(l):
            S.mark("phase_merge")
            reset_arenas()
            BR = BRa[:, :].rearrange("p (k t) -> p k t", k=12)
            mT = aR1.bf(32768).rearrange("p (k t) -> p k t", k=16)
            Wb = []
            for br, wd in enumerate((w_ssm_out, w_attn_out, w_conv_out)):
                Wb.append(load_w(br, wd[l].rearrange("(k p) n -> p k n", p=128), 4, 2048, "WS%d" % br))
            gt = [[aT.bf(512) for _ in range(3)] for _ in range(2)]
            m0 = aT.f32(512)
            m1 = aT.f32(512)
            m2 = aT.f32(512)
            n = 0
            for fo in range(16):
                for c in range(NCH):
                    cs = slice(TC * c, TC * (c + 1))
                    b = n % 2
                    n += 1
                    pb = 3 * b
                    for br in range(3):
                        gtile = 52 + 16 * br + fo
                        dma_sp(gt[b][br], PROJ[128 * gtile:128 * (gtile + 1), cs], w=["gt%d%d" % (b, br)])
                        for kt in range(4):
                            A("pe", lambda e, br=br, kt=kt, fo=fo, cs=cs, pb=pb: e.matmul(psb(pb + br), Wb[br][:, kt, 128 * fo:128 * (fo + 1)], BR[:, 4 * br + kt, cs], start=(kt == 0), stop=(kt == 3)),
                              r=["WS%d" % br, "BR%d" % (4 * br + kt)], w=pk(pb + br))
                    A("dve", lambda e, b=b, pb=pb: e.tensor_tensor(out=m0, in0=psb(pb), in1=gt[b][0], op=ALU.mult), r=pk(pb) + ["gt%d0" % b], w=["m0"])
                    A("dve", lambda e, b=b, pb=pb: e.tensor_tensor(out=m1, in0=psb(pb + 1), in1=gt[b][1], op=ALU.mult), r=pk(pb + 1) + ["gt%d1" % b], w=["m1"])
                    A("dve", lambda e, b=b, pb=pb: e.tensor_tensor(out=m2, in0=psb(pb + 2), in1=gt[b][2], op=ALU.mult), r=pk(pb + 2) + ["gt%d2" % b], w=["m2"])
                    A("pool", lambda e: e.tensor_tensor(out=m0, in0=m0, in1=m1, op=ALU.add), r=["m0", "m1"], w=["m0"])
                    A("pool", lambda e, fo=fo, cs=cs: e.tensor_tensor(out=mT[:, fo, cs], in0=m0, in1=m2, op=ALU.add), r=["m0", "m2"], w=["mT%d" % fo])
            S.barrier()
            aT.reset()
            aBR.reset()
            yst = [aBR.f32(2048) for _ in range(2)]
            sqf = aBR.f32(2048)
            wv = w_o[l].rearrange("(kt p) n -> p kt n", p=128)
            mkeys = ["mT%d" % k for k in range(16)]
            n = 0
            for cg in range(4):
                slot = cg % 3
                wk = "WS%d" % slot
                W = load_w(slot, wv[:, :, 512 * cg:512 * (cg + 1)], 16, 512, wk)
                for j in range(4):
                    fo = 4 * cg + j
                    hb = 4 * (half_ctr[0] % 2)
                    half_ctr[0] += 1
                    for c in range(NCH):
                        for kt in range(16):
                            A("pe", lambda e, W=W, kt=kt, j=j, c=c, hb=hb: e.matmul(psb(hb + c), W[:, kt, 128 * j:128 * (j + 1)], mT[:, kt, TC * c:TC * (c + 1)], start=(kt == 0), stop=(kt == 15)),
                              r=[wk, "mT%d" % kt], w=pk(hb + c))
                    b = n % 2
                    n += 1
                    A("act", lambda e, b=b, hb=hb: e.activation(out=yst[b], in_=psb(hb, 4), func=AF.Copy), r=pk(hb, 4), w=["yst%d" % b])
                    dma_sp(YT[128 * fo:128 * (fo + 1), :], yst[b], r=["yst%d" % b])
                    if fo == 0:
                        A("act", lambda e, hb=hb: e.activation(out=ACC[:], in_=psb(hb, 4), func=AF.Square), r=pk(hb, 4), w=["ACC"])
                    else:
                        A("act", lambda e, hb=hb: e.activation(out=sqf, in_=psb(hb, 4), func=AF.Square), r=pk(hb, 4), w=["sqf"])
                        A("pool", lambda e: e.tensor_tensor(out=ACC[:], in0=ACC[:], in1=sqf, op=ALU.add), r=["sqf", "ACC"], w=["ACC"])
            finish_rstd()
            S.barrier()

        def finish_rstd():
            for c in range(NCH):
                cs = slice(TC * c, TC * (c + 1))
                A("pe", lambda e, c=c, cs=cs: e.matmul(psb(c), ones32[:], ACC[:, cs], start=True, stop=True), r=["ones32", "ACC"], w=pk(c))
            A("act", lambda e: e.activation(out=RSTDY[:], in_=psb(0, 4), func=AF.Sqrt, bias=epsT[:, 0:1], scale=1.0 / D), r=pk(0, 4) + ["epsT"], w=["RSTDY"])
            A("dve", lambda e: e.reciprocal(out=RSTDY[:], in_=RSTDY[:]), r=["RSTDY"], w=["RSTDY"])

        def phase_ffn(l):
            S.mark("phase_ffn")
            reset_arenas()
            hT = aR1.bf(32768).rearrange("p (k t) -> p k t", k=16)
            dma_sp(ffnwT[:], ffnwT_in[l], w=["ffnwT"])
            abuf = [aBR.f32(2064) for _ in range(2)]
            bbuf = [aBR.f32(2064) for _ in range(2)]
            cb = aBR.f32(2048)
            ca = aT.f32(2048)
            sa = aT.f32(2048)
            gst = [aT.bf(2048) for _ in range(2)]
            for b in range(2):
                A("dve", lambda e, b=b: e.memset(abuf[b][:, 0:16], 0.0), w=["abuf%d" % b])
                A("dve", lambda e, b=b: e.memset(bbuf[b][:, 0:16], 0.0), w=["bbuf%d" % b])
            wv = w_up[l].rearrange("(kt p) n -> p kt n", p=128)
            n = 0
            hn = 0

            def conv3(dst, src, w0, dkey, skey):
                A("dve", lambda e: e.tensor_scalar(out=dst, in0=src[:, 16:2064], scalar1=ffnwT[:, w0:w0 + 1], scalar2=None, op0=ALU.mult), r=[skey, "ffnwT"], w=[dkey])
                A("dve", lambda e: e.scalar_tensor_tensor(out=dst, in0=src[:, 15:2063], scalar=ffnwT[:, w0 + 1:w0 + 2], in1=dst, op0=ALU.mult, op1=ALU.add), r=[skey, "ffnwT", dkey], w=[dkey])
                A("dve", lambda e: e.scalar_tensor_tensor(out=dst, in0=src[:, 14:2062], scalar=ffnwT[:, w0 + 2:w0 + 3], in1=dst, op0=ALU.mult, op1=ALU.add), r=[skey, "ffnwT", dkey], w=[dkey])

            for pg in range(11):
                sla = (2 * pg) % 3
                slb = (2 * pg + 1) % 3
                Wa = load_w(sla, wv[:, :, 512 * pg:512 * (pg + 1)], 16, 512, "WS%d" % sla)
                Wb_ = load_w(slb, wv[:, :, DFF + 512 * pg:DFF + 512 * (pg + 1)], 16, 512, "WS%d" % slb)
                for j in range(4):
                    fa = 4 * pg + j
                    b = n % 2
                    n += 1
                    for th in range(2):
                        hb = 4 * (hn % 2)
                        hn += 1
                        for (W, wk, boff) in ((Wa, "WS%d" % sla, 0), (Wb_, "WS%d" % slb, 2)):
                            for kt in range(16):
                                for c2 in range(2):
                                    tok = slice(1024 * th + 512 * c2, 1024 * th + 512 * (c2 + 1))
                                    A("pe", lambda e, W=W, kt=kt, j=j, tok=tok, bank=hb + boff + c2: e.matmul(psb(bank), W[:, kt, 128 * j:128 * (j + 1)], hT[:, kt, tok], start=(kt == 0), stop=(kt == 15)),
                                      r=[wk, "hT%d" % kt], w=pk(hb + boff + c2))
                        A("act", lambda e, b=b, th=th, hb=hb: e.activation(out=abuf[b][:, 16 + 1024 * th:16 + 1024 * (th + 1)], in_=psb(hb, 2), func=AF.Copy), r=pk(hb, 2), w=["abuf%d" % b])
                        A("act", lambda e, b=b, th=th, hb=hb: e.activation(out=bbuf[b][:, 16 + 1024 * th:16 + 1024 * (th + 1)], in_=psb(hb + 2, 2), func=AF.Copy), r=pk(hb + 2, 2), w=["bbuf%d" % b])
                    conv3(ca, abuf[b], 3 * fa, "ca", "abuf%d" % b)
                    A("act", lambda e: e.activation(out=sa, in_=ca, func=AF.Silu), r=["ca"], w=["sa"])
                    conv3(cb, bbuf[b], 3 * (44 + fa), "cb", "bbuf%d" % b)
                    A("dve", lambda e, b=b: e.tensor_tensor(out=gst[b], in0=sa, in1=cb, op=ALU.mult), r=["sa", "cb"], w=["gst%d" % b])
                    dma_sp(GS[128 * fa:128 * (fa + 1), :], gst[b], r=["gst%d" % b], w=["GS%d" % fa])
            S.barrier()
            reset_arenas()
            g1 = aR1.bf(32768).rearrange("p (k t) -> p k t", k=32)
            g2 = aBR.bf(12288).rearrange("p (k t) -> p k t", k=12)
            yst = [aBR.f32(1024) for _ in range(2)]
            sqf = [aT.f32(1024) for _ in range(2)]
            gv = GS.rearrange("(k p) t -> p k t", p=128)
            wdv = w_down[l].rearrange("(k p) n -> p k n", p=128)
            WD = [WSall[:, 11264 * i:11264 * (i + 1)].rearrange("p (k n) -> p k n", k=22) for i in range(2)]
            n = 0
            nl_ = 0
            for th in range(2):
                ts_ = slice(1024 * th, 1024 * (th + 1))
                for q in range(4):
                    dma_sp(g1[:, 8 * q:8 * (q + 1), :], gv[:, 8 * q:8 * (q + 1), ts_], r=["GS%d" % i for i in range(8 * q, 8 * q + 8)], w=["g1_%d" % q])
                dma_sp(g2[:, 0:6, :], gv[:, 32:38, ts_], r=["GS%d" % i for i in range(32, 38)], w=["g2_0"])
                dma_sp(g2[:, 6:12, :], gv[:, 38:44, ts_], r=["GS%d" % i for i in range(38, 44)], w=["g2_1"])

                def gsrc(kt, c2):
                    if kt < 32:
                        return g1[:, kt, 512 * c2:512 * (c2 + 1)], "g1_%d" % (kt // 8)
                    return g2[:, kt - 32, 512 * c2:512 * (c2 + 1)], "g2_%d" % ((kt - 32) // 6)

                for cg in range(4):
                    for kh in range(2):
                        slot = nl_ % 2
                        nl_ += 1
                        wk = "WD%d" % slot
                        dma_pool(WD[slot], wdv[:, 22 * kh:22 * (kh + 1), 512 * cg:512 * (cg + 1)], w=[wk])
                        for j in range(4):
                            fo = 4 * cg + j
                            bank = 2 * j
                            for k2 in range(22):
                                kt = 22 * kh + k2
                                for c2 in range(2):
                                    gs_, gk = gsrc(kt, c2)
                                    A("pe", lambda e, slot=slot, k2=k2, j=j, gs_=gs_, bank=bank + c2, st_=(kt == 0), sp_=(kt == 43): e.matmul(psb(bank), WD[slot][:, k2, 128 * j:128 * (j + 1)], gs_, start=st_, stop=sp_), r=[wk, gk], w=pk(bank + c2))
                            if kh == 1:
                                b = n % 2
                                n += 1
                                A("act", lambda e, b=b, bank=bank: e.activation(out=yst[b], in_=psb(bank, 2), func=AF.Copy), r=pk(bank, 2), w=["yst%d" % b])
                                dma_sp(YT[128 * fo:128 * (fo + 1), ts_], yst[b], r=["yst%d" % b])
                                if fo == 0:
                                    A("act", lambda e, bank=bank, ts_=ts_: e.activation(out=ACC[:, ts_], in_=psb(bank, 2), func=AF.Square), r=pk(bank, 2), w=["ACC"])
                                else:
                                    A("act", lambda e, b=b, bank=bank: e.activation(out=sqf[b], in_=psb(bank, 2), func=AF.Square), r=pk(bank, 2), w=["sqf%d" % b])
                                    A("pool", lambda e, b=b, ts_=ts_: e.tensor_tensor(out=ACC[:, ts_], in0=ACC[:, ts_], in1=sqf[b], op=ALU.add), r=["sqf%d" % b, "ACC"], w=["ACC"])
            S.barrier()
            finish_rstd()
            S.barrier()

        ident_in = din("ident", [128, 128])

        for l in range(n_layers):
            if l == 0:
                phase_mod(0)
                phase_norm(xT_in, False, None, AG[:, 0:16], modT[:, 0:16], XT, True)
            phase_inproj(l)
            phase_ssm(l)
            phase_conv(l)
            phase_attn(l)
            phase_merge(l)
            phase_norm(XT, True, AG[:, 16:32], AG[:, 32:48], modT[:, 48:64], XT, True)
            phase_ffn(l)
            if l + 1 < n_layers:
                G2 = sbt("G2_%d" % l, [128, 16], F32)
                A("dve", lambda e, G2=G2: e.tensor_copy(out=G2[:], in_=AG[:, 48:64]), w=["G2s"])
                S.barrier()
                phase_mod(l + 1)
                phase_norm(XT, True, G2[:], AG[:, 0:16], modT[:, 0:16], XT, True)
            else:
                phase_norm(XT, True, AG[:, 48:64], None, None, outT, False)

        sems = {e: st.enter_context(nc.semaphore("s_" + e)) for e in ENGS}
        dma_sems = {e: [st.enter_context(nc.semaphore("d_%s%d" % (e, i))) for i in range(DMA_K)] for e in ("sp", "pool")}
        S.mark("end")
        build_program.marks = S.marks
        S.prepare(sems, dma_sems)
        block = st.enter_context(nc.Block())

        @block.tensor
        def _(e):
            S.run("pe", e)

        @block.scalar
        def _(e):
            S.run("act", e)

        @block.vector
        def _(e):
            S.run("dve", e)

        @block.gpsimd
        def _(e):
            S.run("pool", e)

        @block.sync
        def _(e):
            S.run("sp", e)
    return nc


def prep_shared(inp):
    f = lambda a: np.ascontiguousarray(np.asarray(a, dtype=np.float32))
    sh = {}
    for k in ("w_mod", "w_in", "w_glu", "w_ssm_out", "w_attn_out", "w_conv_out", "w_o", "w_up", "w_down"):
        sh[k] = f(inp[k])
    sh["bmodT"] = f(np.asarray(inp["b_mod"]).reshape(NL, 96, 128).transpose(0, 2, 1))
    g = np.stack([np.asarray(inp[k]).reshape(NL, 16, 128).transpose(0, 2, 1) for k in ("g_pre_mix", "g_post_mix", "g_pre_ffn", "g_post_ffn")], axis=2)
    sh["gains"] = f(g.reshape(NL, 128, 64))
    sh["bgateT"] = f(np.asarray(inp["b_gate"]).reshape(NL, 48, 128).transpose(0, 2, 1))
    ld = np.repeat(np.asarray(inp["ssm_log_dt"]), 64, axis=1)
    ar = np.asarray(inp["ssm_a_re"]).reshape(NL, 2048)
    ai = np.asarray(inp["ssm_a_im"]).reshape(NL, 2048)
    sh["ssm_flat"] = f(np.stack([ld, ar, ai], axis=1))
    sm = np.stack([a.reshape(NL, 16, 128).transpose(0, 2, 1) for a in (ld, ar, ai)], axis=2)
    sh["ssm_sm"] = f(sm.reshape(NL, 128, 48))
    bre = np.asarray(inp["ssm_b_re"])
    bim = np.asarray(inp["ssm_b_im"])
    cre = np.asarray(inp["ssm_c_re"])
    cim = np.asarray(inp["ssm_c_im"])
    BTre = np.zeros((NL, 128, 16, 128), np.float32)
    BTim = np.zeros((NL, 128, 16, 128), np.float32)
    CTre = np.zeros((NL, 128, 16, 128), np.float32)
    CTim = np.zeros((NL, 128, 16, 128), np.float32)
    for g_ in range(32):
        cs = slice(16 * (g_ % 8), 16 * (g_ % 8) + 16)
        ss = slice(64 * (g_ % 2), 64 * (g_ % 2) + 64)
        BTre[:, cs, g_ // 2, ss] = bre[:, g_].transpose(0, 2, 1)
        BTim[:, cs, g_ // 2, ss] = bim[:, g_].transpose(0, 2, 1)
        CTre[:, ss, g_ // 2, cs] = cre[:, g_].transpose(0, 2, 1)
        CTim[:, ss, g_ // 2, cs] = cim[:, g_].transpose(0, 2, 1)
    sh["BTre"] = BTre.reshape(NL, 128, 2048)
    sh["BTim"] = BTim.reshape(NL, 128, 2048)
    sh["CTre"] = CTre.reshape(NL, 128, 2048)
    sh["CTim"] = CTim.reshape(NL, 128, 2048)
    sh["dT"] = f(np.asarray(inp["ssm_d"]).reshape(NL, 4, 128).transpose(0, 2, 1))
    sh["bgluT"] = f(np.asarray(inp["b_glu"]).reshape(NL, 4, 128).transpose(0, 2, 1))
    sh["convwT"] = f(np.asarray(inp["conv_mix_w"]).reshape(NL, 3, 4, 128).transpose(0, 3, 2, 1).reshape(NL, 128, 12))
    sh["ffnwT"] = f(np.asarray(inp["ffn_conv_w"]).reshape(NL, 3, 88, 128).transpose(0, 3, 2, 1).reshape(NL, 128, 264))
    sh["iota"] = f(np.tile(np.arange(512, dtype=np.float32)[None, :], (128, 1)))
    sh["ones"] = np.ones((128, 128), np.float32)
    sh["ident"] = np.eye(128, dtype=np.float32)
    hi, lo = make_bias_tables()
    sh["bias_hi"] = f(hi)
    sh["bias_lo"] = f(lo)
    return sh


def kernel(**inputs):
    n_layers = int(os.environ.get("K_NLAYERS", NL))
    debug = bool(int(os.environ.get("K_DEBUG", "0")))
    ncores = int(os.environ.get("K_NCORES", 8))
    sh = prep_shared(inputs)
    x = np.asarray(inputs["x"], dtype=np.float32)
    c = np.asarray(inputs["c"], dtype=np.float32)
    in_maps = []
    for b in range(ncores):
        m = dict(sh)
        m["xT"] = np.ascontiguousarray(x[b].T)
        m["cT"] = np.ascontiguousarray(c[b].reshape(16, 128).T)
        in_maps.append(m)
    nc = build_program(n_layers=n_layers, debug=debug)
    res = run_bass_kernel_spmd(nc, in_maps, core_ids=list(range(ncores)))
    if debug:
        kernel.last_results = res.results
    out = np.stack([np.ascontiguousarray(r["outT"].T) for r in res.results], axis=0)
    return out.astype(np.float32)
```

```python
import os
import numpy as np
import concourse.bass as bass
import concourse.mybir as mybir
from concourse.bass_utils import run_bass_kernel_spmd
from contextlib import ExitStack

F32 = mybir.dt.float32
BF16 = mybir.dt.bfloat16
AF = mybir.ActivationFunctionType
ALU = mybir.AluOpType

D = 2048
L = 2048
KT = 16
NL = 4
N_IN = 12800
DFF = 5632
NCH = 4
TC = 512
PI = float(np.pi)
USE_ACT_TABLES = 0

ENGS = ("pe", "act", "dve", "pool", "sp")
DMA_K = 8


class Op:
    __slots__ = ("eng", "fn", "deps", "is_dma", "signal", "sigval", "dsem", "dval", "dprev")

    def __init__(self, eng, fn, is_dma):
        self.eng = eng
        self.fn = fn
        self.is_dma = is_dma
        self.deps = ()
        self.signal = False
        self.sigval = 0
        self.dsem = None
        self.dval = 0
        self.dprev = None


class Sched:
    def __init__(self, same_engine_sync=True):
        self.ops = []
        self.lw = {}
        self.rd = {}
        self.same = same_engine_sync
        self.last_on = {e: None for e in ENGS}
        self.last_dmas = {e: [] for e in ENGS}
        self.barrier_deps = ()
        self._fresh = {e: False for e in ENGS}
        self.marks = []
        self.npe = 0

    def mark(self, name):
        self.marks.append((name, self.npe))

    def add(self, eng, fn, reads=(), writes=(), dma=False):
        op = Op(eng, fn, dma)
        if eng == "pe":
            self.npe += 1
        deps = set(self.barrier_deps) if self._fresh[eng] else set()
        self._fresh[eng] = False
        lw, rd = self.lw, self.rd
        for k in reads:
            w = lw.get(k)
            if w is not None:
                deps.add(w)
        for k in writes:
            w = lw.get(k)
            if w is not None:
                deps.add(w)
            r = rd.get(k)
            if r:
                deps.update(r)
        for k in reads:
            lst = rd.get(k)
            if lst is None:
                rd[k] = [op]
            elif (not dma) and lst and lst[-1].eng == eng and not lst[-1].is_dma:
                lst[-1] = op
            else:
                lst.append(op)
        for k in writes:
            lw[k] = op
            rd[k] = []
        deps.discard(op)
        op.deps = deps
        self.ops.append(op)
        if dma:
            ld = self.last_dmas[eng]
            ld.append(op)
            if len(ld) > DMA_K:
                ld.pop(0)
        else:
            self.last_on[eng] = op
        return op

    def barrier(self):
        deps = [o for o in self.last_on.values() if o is not None]
        for e in ENGS:
            deps.extend(self.last_dmas[e])
        self.barrier_deps = tuple(deps)
        self._fresh = {e: True for e in ENGS}
        self.lw = {}
        self.rd = {}

    def prepare(self, sems, dma_sems):
        ops = self.ops
        same = self.same
        for op in ops:
            for d in op.deps:
                if d.is_dma:
                    continue
                if d.eng == op.eng and not op.is_dma and (d.eng == "pe" or not same):
                    continue
                d.signal = True
        cnt = {e: 0 for e in ENGS}
        dcnt = {e: 0 for e in ENGS}
        hist = {e: [] for e in ENGS}
        for op in ops:
            if op.is_dma:
                n = dcnt[op.eng]
                dcnt[op.eng] = n + 1
                op.dsem = dma_sems[op.eng][n % DMA_K]
                op.dval = 16 * (n // DMA_K + 1)
                h = hist[op.eng]
                op.dprev = h[n - DMA_K] if n >= DMA_K else None
                h.append(op)
            elif op.signal:
                cnt[op.eng] += 1
                op.sigval = cnt[op.eng]
        self.by_eng = {e: [] for e in ENGS}
        for op in ops:
            self.by_eng[op.eng].append(op)
        self.sems = sems

    def run(self, eng_name, e):
        waited = {}
        sems = self.sems
        same = self.same
        for op in self.by_eng[eng_name]:
            need = {}
            deps = list(op.deps)
            if op.is_dma and op.dprev is not None:
                deps.append(op.dprev)
            for d in deps:
                if d.is_dma:
                    s, v = d.dsem, d.dval
                else:
                    if d.eng == eng_name and not op.is_dma and (eng_name == "pe" or not same):
                        continue
                    if not d.signal:
                        continue
                    s, v = sems[d.eng], d.sigval
                if waited.get(s, 0) >= v:
                    continue
                if need.get(s, 0) < v:
                    need[s] = v
            for s, v in need.items():
                e.wait_ge(s, v)
                waited[s] = v
            ins = op.fn(e)
            if op.is_dma:
                ins.then_inc(op.dsem, 16)
            elif op.signal:
                ins.then_inc(sems[eng_name], 1)
        last = {}
        for op in self.by_eng[eng_name]:
            if op.is_dma:
                last[op.dsem] = op.dval
        for s, v in last.items():
            e.wait_ge(s, v)


def alibi_slopes(n_heads):
    return np.array([2.0 ** (-8.0 * (h + 1) / n_heads) for h in range(n_heads)], dtype=np.float64)


def _bf16_round(x):
    u = np.asarray(x, np.float32).view(np.uint32).astype(np.uint64)
    r = ((u + 0x7FFF + ((u >> 16) & 1)) & 0xFFFF0000).astype(np.uint32)
    return r.view(np.float32)


def make_bias_tables():
    slopes = alibi_slopes(24)
    dil = [1, 4, 16]
    kj = np.arange(128)[:, None]
    qi = np.arange(128)[None, :]
    tab = np.zeros((128, 24, 256), np.float64)
    NEG = -240000.0
    for h in range(24):
        a = slopes[h] * dil[h // 8] * 8.0
        prev = np.where(qi <= kj, -a * (128 + qi - kj), NEG)
        cur = np.where(qi >= kj, -a * (qi - kj), NEG)
        tab[:, h, 0:128] = prev
        tab[:, h, 128:256] = cur
    hi = _bf16_round(tab.astype(np.float32))
    lo = _bf16_round((tab - hi.astype(np.float64)).astype(np.float32))
    return hi.reshape(128, 24 * 256), lo.reshape(128, 24 * 256)


def build_program(n_layers=NL, debug=False):
    nc = bass.Bass("TRN2", target_bir_lowering=False)
    S = Sched()

    def din(name, shape, dt=F32):
        return nc.dram_tensor(name, list(shape), dt, kind="ExternalInput").ap()

    def dscr(name, shape, dt):
        return nc.dram_tensor(name, list(shape), dt, kind=("ExternalOutput" if debug else "Internal")).ap()

    xT_in = din("xT", [D, L])
    cT_in = din("cT", [128, 16])
    w_mod = din("w_mod", [NL, D, 6 * D])
    w_in = din("w_in", [NL, D, N_IN])
    w_glu = din("w_glu", [NL, 512, 512])
    w_ssm_out = din("w_ssm_out", [NL, 512, D])
    w_attn_out = din("w_attn_out", [NL, 512, D])
    w_conv_out = din("w_conv_out", [NL, 512, D])
    w_o = din("w_o", [NL, D, D])
    w_up = din("w_up", [NL, D, 2 * DFF])
    w_down = din("w_down", [NL, DFF, D])
    bmodT_in = din("bmodT", [NL, 128, 96])
    gains_in = din("gains", [NL, 128, 64])
    bgateT_in = din("bgateT", [NL, 128, 48])
    ssm_sm_in = din("ssm_sm", [NL, 128, 48])
    ssm_flat_in = din("ssm_flat", [NL, 3, 2048])
    BTre_in = din("BTre", [NL, 128, 2048])
    BTim_in = din("BTim", [NL, 128, 2048])
    CTre_in = din("CTre", [NL, 128, 2048])
    CTim_in = din("CTim", [NL, 128, 2048])
    dT_in = din("dT", [NL, 128, 4])
    bgluT_in = din("bgluT", [NL, 128, 4])
    convwT_in = din("convwT", [NL, 128, 12])
    ffnwT_in = din("ffnwT", [NL, 128, 264])
    iota_in = din("iota", [128, 512])
    ones_in = din("ones", [128, 128])
    bias_hi_in = din("bias_hi", [128, 24 * 256])
    bias_lo_in = din("bias_lo", [128, 24 * 256])

    outT = nc.dram_tensor("outT", [D, L], F32, kind="ExternalOutput").ap()
    XT = dscr("XT", [D, L], F32)
    YT = dscr("YT", [D, L], F32)
    PROJ = dscr("PROJ", [N_IN, L], BF16)
    VTOK = dscr("VTOK", [L, 1536], BF16)
    GS = dscr("GS", [DFF, L], BF16)

    st = ExitStack()
    with st:
        def sbt(name, shape, dt):
            return st.enter_context(nc.sbuf_tensor("sb_" + name, list(shape), dt))

        R1 = sbt("R1", [128, 32768], BF16)
        WSall = sbt("WS", [128, 24576], BF16)
        WSa = [WSall[:, 8192 * i:8192 * (i + 1)] for i in range(3)]
        BRa = sbt("BR", [128, 24576], BF16)
        Ta = sbt("T", [128, 12288], BF16)
        ACC = sbt("ACC", [128, 2048], F32)
        RSTDY = sbt("RSTDY", [128, 2048], F32)
        PS = st.enter_context(nc.psum_tensor("PS", [128, 4096], F32))
        iota = sbt("iota", [128, 512], F32)
        ones32 = sbt("ones32", [128, 128], F32)
        onesbf = sbt("onesbf", [128, 128], BF16)
        epsT = sbt("epsT", [128, 1], F32)
        condT = sbt("condT", [128, 16], BF16)
        cTs = sbt("cTs", [128, 16], F32)
        modT = sbt("modT", [128, 96], F32)
        bmodT = sbt("bmodT", [128, 96], F32)
        gains = sbt("gains", [128, 64], F32)
        AG = sbt("AG", [128, 64], F32)
        bgateT = sbt("bgateT", [128, 48], F32)
        ssm_sm = sbt("ssm_sm", [128, 48], F32)
        sm_r = sbt("sm_r", [128, 16], F32)
        sm_th = sbt("sm_th", [128, 16], F32)
        sm_cT = sbt("sm_cT", [128, 16], F32)
        sm_sT = sbt("sm_sT", [128, 16], F32)
        sm_tmp = sbt("sm_tmp", [128, 96], F32)
        carry = sbt("carry", [128, 8], F32)
        cst = sbt("cst", [128, 4], F32)
        dTt = sbt("dTt", [128, 4], F32)
        bgluT = sbt("bgluT", [128, 4], F32)
        convwT = sbt("convwT", [128, 12], F32)
        ffnwT = sbt("ffnwT", [128, 264], F32)

        def psb(b, n=1):
            return PS[:, 512 * b:512 * (b + n)]

        def pk(b, n=1):
            return ["ps%d" % i for i in range(b, b + n)]

        class Arena:
            def __init__(self, t, size, name):
                self.t = t
                self.size = size
                self.off = 0
                self.name = name

            def reset(self):
                self.off = 0

            def bf(self, n):
                assert self.off + n <= self.size, (self.name, self.off, n, self.size)
                a = self.t[:, self.off:self.off + n]
                self.off += n
                return a

            def f32(self, n):
                return self.bf(2 * n).bitcast(F32)

        aR1 = Arena(R1, 32768, "R1")
        aBR = Arena(BRa, 24576, "BR")
        aT = Arena(Ta, 12288, "T")

        def reset_arenas():
            aR1.reset()
            aBR.reset()
            aT.reset()

        def A(eng, fn, r=(), w=(), dma=False):
            S.add(eng, fn, r, w, dma)

        def dma_sp(out, in_, r=(), w=()):
            A("sp", lambda e: e.dma_start(out=out, in_=in_), r, w, True)

        def dma_pool(out, in_, r=(), w=()):
            A("pool", lambda e: e.dma_start(out=out, in_=in_), r, w, True)

        dma_sp(iota[:], iota_in[:, :], w=["iota"])
        dma_sp(ones32[:], ones_in[:, :], w=["ones32"])
        dma_pool(onesbf[:], ones_in[:, :], w=["onesbf"])
        dma_sp(cTs[:], cT_in[:, :], w=["cTs"])
        A("dve", lambda e: e.memset(epsT[:], 1e-6), w=["epsT"])
        A("dve", lambda e: e.memset(cst[:, 0:1], 0.0), w=["cst"])
        A("dve", lambda e: e.memset(cst[:, 1:2], PI / 2), w=["cst"])
        A("dve", lambda e: e.memset(cst[:, 2:3], 12582912.0), w=["cst"])
        A("dve", lambda e: e.memset(cst[:, 3:4], -12582912.0), w=["cst"])
        A("act", lambda e: e.activation(out=condT[:], in_=cTs[:], func=AF.Silu), r=["cTs"], w=["condT"])
        S.barrier()

        def sincos(x_ap, n, out_sin, out_cos, tmpk, tmpp, keyp):
            M = 12582912.0
            C1 = 6.28125
            C2 = float(2 * np.pi - 6.28125)
            for which, outp, shift in (("s", out_sin, 0.0), ("c", out_cos, PI / 2)):
                kk = keyp + which
                A("dve", lambda e, shift=shift: e.tensor_scalar(out=tmpp, in0=x_ap, scalar1=shift, scalar2=None, op0=ALU.add), r=[keyp + "x"], w=[keyp + "p"])
                A("dve", lambda e: e.tensor_scalar(out=tmpk, in0=tmpp, scalar1=float(1 / (2 * np.pi)), scalar2=M, op0=ALU.mult, op1=ALU.add), r=[keyp + "p"], w=[keyp + "k"])
                A("dve", lambda e: e.tensor_scalar(out=tmpk, in0=tmpk, scalar1=-M, scalar2=None, op0=ALU.add), r=[keyp + "k"], w=[keyp + "k"])
                A("dve", lambda e: e.scalar_tensor_tensor(out=tmpp, in0=tmpk, scalar=-C1, in1=tmpp, op0=ALU.mult, op1=ALU.add), r=[keyp + "k", keyp + "p"], w=[keyp + "p"])
                A("dve", lambda e: e.scalar_tensor_tensor(out=tmpp, in0=tmpk, scalar=-C2, in1=tmpp, op0=ALU.mult, op1=ALU.add), r=[keyp + "k", keyp + "p"], w=[keyp + "p"])
                A("dve", lambda e: e.tensor_scalar(out=tmpp, in0=tmpp, scalar1=PI, scalar2=-PI, op0=ALU.min, op1=ALU.max), r=[keyp + "p"], w=[keyp + "p"])
                A("act", lambda e, outp=outp: e.activation(out=outp, in_=tmpp, func=AF.Sin), r=[keyp + "p"], w=[kk])

        def load_w(slot, dram_ap3, nkt, ncol, key):
            v = WSa[slot][:, 0:nkt * ncol].rearrange("p (k n) -> p k n", k=nkt)
            dma_pool(v, dram_ap3, w=[key])
            return v

        def phase_mod(l):
            S.mark("phase_mod")
            reset_arenas()
            dma_sp(bmodT[:], bmodT_in[l], w=["bmodT"])
            dma_sp(gains[:], gains_in[l], w=["gains"])
            dma_sp(bgateT[:], bgateT_in[l], w=["bgateT"])
            wv = w_mod[l].rearrange("(kt p) n -> p kt n", p=128)
            for cg in range(24):
                slot = cg % 3
                W = load_w(slot, wv[:, :, 512 * cg:512 * (cg + 1)], 16, 512, "WS%d" % slot)
                for j in range(4):
                    ct = 4 * cg + j
                    for kt in range(16):
                        A("pe", lambda e, W=W, j=j, kt=kt, ct=ct: e.matmul(PS[:, 1024 + ct:1025 + ct], W[:, kt, 128 * j:128 * (j + 1)], condT[:, kt:kt + 1], start=(kt == 0), stop=(kt == 15)),
                          r=["WS%d" % slot, "condT"], w=["ps2"])
            A("dve", lambda e: e.tensor_tensor(out=modT[:], in0=PS[:, 1024:1120], in1=bmodT[:], op=ALU.add), r=["ps2", "bmodT"], w=["modT"])
            A("dve", lambda e: e.scalar_tensor_tensor(out=AG[:, 0:16], in0=modT[:, 16:32], scalar=1.0, in1=gains[:, 0:16], op0=ALU.add, op1=ALU.mult), r=["modT", "gains"], w=["AG"])
            A("dve", lambda e: e.tensor_tensor(out=AG[:, 16:32], in0=modT[:, 32:48], in1=gains[:, 16:32], op=ALU.mult), r=["modT", "gains"], w=["AG"])
            A("dve", lambda e: e.scalar_tensor_tensor(out=AG[:, 32:48], in0=modT[:, 64:80], scalar=1.0, in1=gains[:, 32:48], op0=ALU.add, op1=ALU.mult), r=["modT", "gains"], w=["AG"])
            A("dve", lambda e: e.tensor_tensor(out=AG[:, 48:64], in0=modT[:, 80:96], in1=gains[:, 48:64], op=ALU.mult), r=["modT", "gains"], w=["AG"])
            S.barrier()

        def phase_norm(src_x, has_y, Gap, Aap, Bap, dst_x, make_h):
            S.mark("phase_norm")
            reset_arenas()
            hT = aR1.bf(32768).rearrange("p (k t) -> p k t", k=16)
            XN = aBR.f32(8192).rearrange("p (k t) -> p k t", k=16)
            NB = 3
            xt = [aT.f32(512) for _ in range(NB)]
            yt = [aT.f32(512) for _ in range(NB)]
            tmp = [aT.f32(512) for _ in range(2)]
            sq = [aT.bf(512) for _ in range(2)]
            rt = aT.f32(512)
            rstd = aT.f32(512)
            steps = [(c, ft) for c in range(NCH) for ft in range(16)]
            loaded = [0]

            def emit_loads(upto):
                while loaded[0] < min(upto, len(steps)):
                    i = loaded[0]
                    c, ft = steps[i]
                    bb = i % NB
                    rows = slice(128 * ft, 128 * (ft + 1))
                    cs = slice(TC * c, TC * (c + 1))
                    dma_sp(xt[bb], src_x[rows, cs], w=["xt%d" % bb])
                    dma_sp(yt[bb], YT[rows, cs], w=["yt%d" % bb])
                    loaded[0] += 1

            for i, (c, ft) in enumerate(steps):
                cs = slice(TC * c, TC * (c + 1))
                pb = c % 2
                b = ft % 2
                rows = slice(128 * ft, 128 * (ft + 1))
                kx = "XN%d" % ft
                if has_y:
                    emit_loads(i + NB)
                    bb = i % NB
                    A("dve", lambda e, b=b, bb=bb, cs=cs: e.tensor_tensor(out=tmp[b], in0=yt[bb], in1=RSTDY[:, cs], op=ALU.mult), r=["yt%d" % bb, "RSTDY"], w=["tmp%d" % b])
                    A("dve", lambda e, b=b, bb=bb, ft=ft: e.scalar_tensor_tensor(out=XN[:, ft, :], in0=tmp[b], scalar=Gap[:, ft:ft + 1], in1=xt[bb], op0=ALU.mult, op1=ALU.add),
                      r=["tmp%d" % b, "xt%d" % bb, "AG"], w=[kx])
                else:
                    dma_sp(XN[:, ft, :], src_x[rows, cs], w=[kx])
                if dst_x is not None:
                    dma_pool(dst_x[rows, cs], XN[:, ft, :], r=[kx])
                if make_h:
                    A("act", lambda e, b=b, ft=ft: e.activation(out=sq[b], in_=XN[:, ft, :], func=AF.Square), r=[kx], w=["sq%d" % b])
                    A("pe", lambda e, b=b, ft=ft, pb=pb: e.matmul(psb(pb), onesbf[:], sq[b], start=(ft == 0), stop=(ft == 15)), r=["sq%d" % b, "onesbf"], w=pk(pb))
                if make_h and ft == 15:
                    A("act", lambda e, pb=pb: e.activation(out=rt, in_=psb(pb), func=AF.Sqrt, bias=epsT[:, 0:1], scale=1.0 / D), r=pk(pb) + ["epsT"], w=["rt"])
                    A("dve", lambda e: e.reciprocal(out=rstd, in_=rt), r=["rt"], w=["rstd"])
                    for f2 in range(16):
                        b2 = f2 % 2
                        A("dve", lambda e, b2=b2, f2=f2: e.tensor_tensor(out=tmp[b2], in0=XN[:, f2, :], in1=rstd, op=ALU.mult), r=["XN%d" % f2, "rstd"], w=["tmp%d" % b2])
                        A("act", lambda e, b2=b2, f2=f2, cs=cs: e.activation(out=hT[:, f2, cs], in_=tmp[b2], func=AF.Identity, bias=Bap[:, f2:f2 + 1], scale=Aap[:, f2:f2 + 1]),
                          r=["tmp%d" % b2, "AG", "modT"], w=["hT%d" % f2])
            S.barrier()

        half_ctr = [0]

        def gen_inproj(l, uT):
            hT = R1[:, :].rearrange("p (k t) -> p k t", k=16)
            stg = [aT.bf(2048) for _ in range(2)]
            vst = [aT.bf(512) for _ in range(2)]
            wv = w_in[l].rearrange("(kt p) n -> p kt n", p=128)
            nb_ = 0
            ev = 0
            for cg in range(25):
                slot = cg % 3
                wk = "WS%d" % slot
                W = load_w(slot, wv[:, :, 512 * cg:512 * (cg + 1)], 16, 512, wk)
                if 7 <= cg <= 9:
                    for tt in range(16):
                        bank = nb_ % 2
                        nb_ += 1
                        for kt in range(16):
                            A("pe", lambda e, W=W, kt=kt, tt=tt, bank=bank: e.matmul(psb(bank), hT[:, kt, 128 * tt:128 * (tt + 1)], W[:, kt, :], start=(kt == 0), stop=(kt == 15)),
                              r=[wk, "hT%d" % kt], w=pk(bank))
                        b = tt % 2
                        A("act", lambda e, b=b, bank=bank: e.activation(out=vst[b], in_=psb(bank), func=AF.Copy), r=pk(bank), w=["vst%d" % b])
                        dma_sp(VTOK[128 * tt:128 * (tt + 1), 512 * (cg - 7):512 * (cg - 6)], vst[b], r=["vst%d" % b])
                        if tt % 4 == 3:
                            yield
                    continue
                for j in range(4):
                    pt = 4 * cg + j
                    b = ev % 2
                    ev += 1
                    if cg == 0:
                        dst = uT[:, j, :]
                        sk = "uT%d" % j
                    else:
                        dst = stg[b]
                        sk = "stg%d" % b
                    for c_ in range(NCH):
                        bank = nb_ % 2
                        nb_ += 1
                        for kt in range(16):
                            A("pe", lambda e, W=W, kt=kt, j=j, c_=c_, bank=bank: e.matmul(psb(bank), W[:, kt, 128 * j:128 * (j + 1)], hT[:, kt, TC * c_:TC * (c_ + 1)], start=(kt == 0), stop=(kt == 15)),
                              r=[wk, "hT%d" % kt], w=pk(bank))
                        src = psb(bank)
                        if cg >= 13:
                            gi_ = pt - 52
                            A("act", lambda e, src=src, dst=dst, gi_=gi_, c_=c_: e.activation(out=dst[:, TC * c_:TC * (c_ + 1)], in_=src, func=AF.Sigmoid, bias=bgateT[:, gi_:gi_ + 1]), r=pk(bank) + ["bgateT"], w=[sk])
                        elif cg in (2, 5, 3, 6):
                            dil = 4 if cg in (2, 5) else 16
                            w_ = 512 // dil
                            A("act", lambda e, src=src, dst=dst, dil=dil, w_=w_, c_=c_: e.activation(out=dst.rearrange("p (r m) -> p r m", r=dil)[:, :, w_ * c_:w_ * (c_ + 1)], in_=src.rearrange("p (m r) -> p r m", r=dil), func=AF.Copy), r=pk(bank), w=[sk])
                        else:
                            A("act", lambda e, src=src, dst=dst, c_=c_: e.activation(out=dst[:, TC * c_:TC * (c_ + 1)], in_=src, func=AF.Copy), r=pk(bank), w=[sk])
                    if cg != 0:
                        dma_sp(PROJ[128 * pt:128 * (pt + 1), :], dst, r=[sk], w=["PROJ%d" % pt])
                    yield

        def ssm_prep(l, ctx):
            dma_sp(ssm_sm[:], ssm_sm_in[l], w=["ssm_sm"])
            dma_sp(dTt[:], dT_in[l], w=["dTt"])
            dma_sp(bgluT[:], bgluT_in[l], w=["bgluT"])
            dt_sm = sm_tmp[:, 0:16]
            A("act", lambda e: e.activation(out=dt_sm, in_=ssm_sm[:, 0:16], func=AF.Exp), r=["ssm_sm"], w=["dt_sm"])
            A("dve", lambda e: e.tensor_tensor(out=sm_tmp[:, 16:32], in0=ssm_sm[:, 16:32], in1=dt_sm, op=ALU.mult), r=["dt_sm", "ssm_sm"], w=["ardt"])
            A("act", lambda e: e.activation(out=sm_r[:], in_=sm_tmp[:, 16:32], func=AF.Exp), r=["ardt"], w=["sm_r"])
            A("dve", lambda e: e.tensor_tensor(out=sm_th[:], in0=ssm_sm[:, 32:48], in1=dt_sm, op=ALU.mult), r=["dt_sm", "ssm_sm"], w=["sm_th"])
            A("dve", lambda e: e.tensor_scalar(out=sm_tmp[:, 32:48], in0=sm_th[:], scalar1=float(TC), scalar2=None, op0=ALU.mult), r=["sm_th"], w=["smTx"])
            sincos(sm_tmp[:, 32:48], 16, sm_sT[:], sm_cT[:], sm_tmp[:, 48:64], sm_tmp[:, 64:80], "smT")
            BbrT = aBR.bf(2048).rearrange("p (s n) -> p s n", s=16)
            BbiT = aBR.bf(2048).rearrange("p (s n) -> p s n", s=16)
            CrT = aBR.bf(2048).rearrange("p (s n) -> p s n", s=16)
            CiT = aT.bf(2048).rearrange("p (s n) -> p s n", s=16)
            Wglu = aT.bf(2048).rearrange("p (k n) -> p k n", k=4)
            ctx.update(BbrT=BbrT, BbiT=BbiT, CrT=CrT, CiT=CiT, Wglu=Wglu)
            dma_pool(CrT.rearrange("p s n -> p (s n)"), CTre_in[l], w=["CrT"])
            dma_pool(CiT.rearrange("p s n -> p (s n)"), CTim_in[l], w=["CiT"])
            dma_pool(Wglu, w_glu[l].rearrange("(k p) n -> p k n", p=128), w=["Wglu"])
            mark_br = aBR.off
            fl = [aBR.f32(512) for _ in range(14)]
            (f_ld, f_ar, f_ai, f_dt, f_mag, f_th, f_sin, f_cos, f_k, f_p, f_a, f_b, f_fr, f_fi) = fl
            f_bre = aBR.f32(512)
            f_bim = aBR.f32(512)
            for q in range(4):
                qs = slice(512 * q, 512 * (q + 1))
                dma_sp(f_ld, ssm_flat_in[l, 0:1, qs].to_broadcast([128, 512]), w=["f_ld"])
                dma_sp(f_ar, ssm_flat_in[l, 1:2, qs].to_broadcast([128, 512]), w=["f_ar"])
                dma_sp(f_ai, ssm_flat_in[l, 2:3, qs].to_broadcast([128, 512]), w=["f_ai"])
                dma_sp(f_bre, BTre_in[l, :, qs], w=["f_bre"])
                dma_sp(f_bim, BTim_in[l, :, qs], w=["f_bim"])
                A("act", lambda e: e.activation(out=f_dt, in_=f_ld, func=AF.Exp), r=["f_ld"], w=["f_dt"])
                A("dve", lambda e: e.tensor_tensor(out=f_a, in0=f_ar, in1=f_dt, op=ALU.mult), r=["f_ar", "f_dt"], w=["f_a"])
                A("act", lambda e: e.activation(out=f_mag, in_=f_a, func=AF.Exp), r=["f_a"], w=["f_mag"])
                A("dve", lambda e: e.tensor_tensor(out=f_th, in0=f_ai, in1=f_dt, op=ALU.mult), r=["f_ai", "f_dt"], w=["flx"])
                sincos(f_th, 512, f_sin, f_cos, f_k, f_p, "fl")
                A("dve", lambda e: e.tensor_tensor(out=f_cos, in0=f_cos, in1=f_mag, op=ALU.mult), r=["flc", "f_mag"], w=["flc"])
                A("dve", lambda e: e.tensor_tensor(out=f_sin, in0=f_sin, in1=f_mag, op=ALU.mult), r=["fls", "f_mag"], w=["fls"])
                A("dve", lambda e: e.tensor_scalar(out=f_cos, in0=f_cos, scalar1=-1.0, scalar2=None, op0=ALU.add), r=["flc"], w=["flc"])
                A("dve", lambda e: e.tensor_tensor(out=f_a, in0=f_ar, in1=f_ar, op=ALU.mult), r=["f_ar"], w=["f_a"])
                A("dve", lambda e: e.tensor_tensor(out=f_b, in0=f_ai, in1=f_ai, op=ALU.mult), r=["f_ai"], w=["f_b"])
                A("dve", lambda e: e.tensor_tensor(out=f_a, in0=f_a, in1=f_b, op=ALU.add), r=["f_a", "f_b"], w=["f_a"])
                A("dve", lambda e: e.reciprocal(out=f_a, in_=f_a), r=["f_a"], w=["f_a"])
                A("dve", lambda e: e.tensor_tensor(out=f_fr, in0=f_cos, in1=f_ar, op=ALU.mult), r=["flc", "f_ar"], w=["f_fr"])
                A("dve", lambda e: e.tensor_tensor(out=f_b, in0=f_sin, in1=f_ai, op=ALU.mult), r=["fls", "f_ai"], w=["f_b"])
                A("dve", lambda e: e.tensor_tensor(out=f_fr, in0=f_fr, in1=f_b, op=ALU.add), r=["f_fr", "f_b"], w=["f_fr"])
                A("dve", lambda e: e.tensor_tensor(out=f_fr, in0=f_fr, in1=f_a, op=ALU.mult), r=["f_fr", "f_a"], w=["f_fr"])
                A("dve", lambda e: e.tensor_tensor(out=f_fi, in0=f_sin, in1=f_ar, op=ALU.mult), r=["fls", "f_ar"], w=["f_fi"])
                A("dve", lambda e: e.tensor_tensor(out=f_b, in0=f_cos, in1=f_ai, op=ALU.mult), r=["flc", "f_ai"], w=["f_b"])
                A("dve", lambda e: e.tensor_tensor(out=f_fi, in0=f_fi, in1=f_b, op=ALU.subtract), r=["f_fi", "f_b"], w=["f_fi"])
                A("dve", lambda e: e.tensor_tensor(out=f_fi, in0=f_fi, in1=f_a, op=ALU.mult), r=["f_fi", "f_a"], w=["f_fi"])
                brv = BbrT.rearrange("p s n -> p (s n)")[:, qs]
                biv = BbiT.rearrange("p s n -> p (s n)")[:, qs]
                A("dve", lambda e: e.tensor_tensor(out=f_k, in0=f_fr, in1=f_bre, op=ALU.mult), r=["f_fr", "f_bre"], w=["flk"])
                A("dve", lambda e: e.tensor_tensor(out=f_p, in0=f_fi, in1=f_bim, op=ALU.mult), r=["f_fi", "f_bim"], w=["flp"])
                A("dve", lambda e, brv=brv: e.tensor_tensor(out=brv, in0=f_k, in1=f_p, op=ALU.subtract), r=["flk", "flp"], w=["BbrT"])
                A("dve", lambda e: e.tensor_tensor(out=f_k, in0=f_fr, in1=f_bim, op=ALU.mult), r=["f_fr", "f_bim"], w=["flk"])
                A("dve", lambda e: e.tensor_tensor(out=f_p, in0=f_fi, in1=f_bre, op=ALU.mult), r=["f_fi", "f_bre"], w=["flp"])
                A("dve", lambda e, biv=biv: e.tensor_tensor(out=biv, in0=f_k, in1=f_p, op=ALU.add), r=["flk", "flp"], w=["BbiT"])
            aBR.off = mark_br
            S.barrier()

        def gen_ssm_main(l, ctx):
            BbrT, BbiT, CrT, CiT = ctx["BbrT"], ctx["BbiT"], ctx["CrT"], ctx["CiT"]
            uT = ctx["uT"]
            GL = uT
            cosT = aBR.f32(512)
            sinT = aBR.f32(512)
            tk = aBR.f32(512)
            tp = aBR.f32(512)
            tk2 = aBR.f32(512)
            xr_s = aBR.f32(512)
            xi_s = aBR.f32(512)
            t1 = aBR.f32(512)
            t2 = aBR.f32(512)
            t3 = aBR.f32(512)
            t4 = aBR.f32(512)
            zr = aBR.f32(512)
            zi = aBR.f32(512)
            gr = aBR.f32(512)
            gi = aBR.f32(512)
            hr = [aBR.bf(512) for _ in range(2)]
            nhi = [aBR.bf(512) for _ in range(2)]
            yv = t1
            y2 = t2
            ctr = 0
            pending = []

            def flush():
                for f in pending:
                    f()
                del pending[:]

            for stt in range(16):
                ut = stt // 4
                A("dve", lambda e, stt=stt: e.tensor_scalar(out=tk2, in0=iota[:], scalar1=sm_th[:, stt:stt + 1], scalar2=None, op0=ALU.mult), r=["iota", "sm_th"], w=["tbx"])
                sincos(tk2, 512, sinT, cosT, tk, tp, "tb")
                for c in range(NCH):
                    cs = slice(TC * c, TC * (c + 1))
                    b = ctr % 2
                    ctr += 1
                    flush()
                    A("pe", lambda e, stt=stt, ut=ut, cs=cs: e.matmul(psb(6), BbrT[:, stt, :], uT[:, ut, cs], start=True, stop=True), r=["BbrT", "uT%d" % ut], w=pk(6))
                    A("pe", lambda e, stt=stt, ut=ut, cs=cs: e.matmul(psb(7), BbiT[:, stt, :], uT[:, ut, cs], start=True, stop=True), r=["BbiT", "uT%d" % ut], w=pk(7))
                    A("act", lambda e: e.activation(out=xr_s, in_=psb(6), func=AF.Copy), r=pk(6), w=["xr"])
                    A("act", lambda e: e.activation(out=xi_s, in_=psb(7), func=AF.Copy), r=pk(7), w=["xi"])
                    A("dve", lambda e: e.tensor_tensor(out=t1, in0=xr_s, in1=cosT, op=ALU.mult), r=["xr", "tbc"], w=["t1"])
                    A("pool", lambda e: e.tensor_tensor(out=t2, in0=xi_s, in1=sinT, op=ALU.mult), r=["xi", "tbs"], w=["t2"])
                    A("dve", lambda e: e.tensor_tensor(out=zr, in0=t1, in1=t2, op=ALU.add), r=["t1", "t2"], w=["zr"])
                    A("pool", lambda e: e.tensor_tensor(out=t3, in0=xi_s, in1=cosT, op=ALU.mult), r=["xi", "tbc"], w=["t3"])
                    A("dve", lambda e: e.tensor_tensor(out=t4, in0=xr_s, in1=sinT, op=ALU.mult), r=["xr", "tbs"], w=["t4"])
                    A("pool", lambda e: e.tensor_tensor(out=zi, in0=t3, in1=t4, op=ALU.subtract), r=["t3", "t4"], w=["zi"])
                    rbc = sm_r[:, stt:stt + 1].to_broadcast([128, 512])
                    if c == 0:
                        A("dve", lambda e, rbc=rbc: e.tensor_tensor_scan(out=gr, data0=rbc, data1=zr, initial=0.0, op0=ALU.mult, op1=ALU.add), r=["zr", "sm_r"], w=["gr"])
                        A("dve", lambda e, rbc=rbc: e.tensor_tensor_scan(out=gi, data0=rbc, data1=zi, initial=0.0, op0=ALU.mult, op1=ALU.add), r=["zi", "sm_r"], w=["gi"])
                    else:
                        A("dve", lambda e, rbc=rbc: e.tensor_tensor_scan(out=gr, data0=rbc, data1=zr, initial=carry[:, 0:1], op0=ALU.mult, op1=ALU.add), r=["zr", "sm_r", "carry"], w=["gr"])
                        A("dve", lambda e, rbc=rbc: e.tensor_tensor_scan(out=gi, data0=rbc, data1=zi, initial=carry[:, 1:2], op0=ALU.mult, op1=ALU.add), r=["zi", "sm_r", "carry"], w=["gi"])
                    if c < NCH - 1:
                        cT_ = sm_cT[:, stt:stt + 1]
                        sT_ = sm_sT[:, stt:stt + 1]
                        A("dve", lambda e, cT_=cT_: e.tensor_tensor(out=carry[:, 2:3], in0=gr[:, 511:512], in1=cT_, op=ALU.mult), r=["gr", "smTc"], w=["cy2"])
                        A("dve", lambda e, sT_=sT_: e.tensor_tensor(out=carry[:, 3:4], in0=gi[:, 511:512], in1=sT_, op=ALU.mult), r=["gi", "smTs"], w=["cy3"])
                        A("dve", lambda e, sT_=sT_: e.tensor_tensor(out=carry[:, 4:5], in0=gr[:, 511:512], in1=sT_, op=ALU.mult), r=["gr", "smTs"], w=["cy4"])
                        A("dve", lambda e, cT_=cT_: e.tensor_tensor(out=carry[:, 5:6], in0=gi[:, 511:512], in1=cT_, op=ALU.mult), r=["gi", "smTc"], w=["cy5"])
                        A("dve", lambda e: e.tensor_tensor(out=carry[:, 0:1], in0=carry[:, 2:3], in1=carry[:, 3:4], op=ALU.subtract), r=["cy2", "cy3"], w=["carry"])
                        A("dve", lambda e: e.tensor_tensor(out=carry[:, 1:2], in0=carry[:, 4:5], in1=carry[:, 5:6], op=ALU.add), r=["cy4", "cy5", "carry"], w=["carry"])
                    A("dve", lambda e: e.tensor_tensor(out=t1, in0=gr, in1=cosT, op=ALU.mult), r=["gr", "tbc"], w=["t1"])
                    A("pool", lambda e: e.tensor_tensor(out=t2, in0=gi, in1=sinT, op=ALU.mult), r=["gi", "tbs"], w=["t2"])
                    A("dve", lambda e, b=b: e.tensor_tensor(out=hr[b], in0=t1, in1=t2, op=ALU.subtract), r=["t1", "t2"], w=["hr%d" % b])
                    A("pool", lambda e: e.tensor_tensor(out=t3, in0=gr, in1=sinT, op=ALU.mult), r=["gr", "tbs"], w=["t3"])
                    A("dve", lambda e: e.tensor_tensor(out=t4, in0=gi, in1=cosT, op=ALU.mult), r=["gi", "tbc"], w=["t4"])
                    A("dve", lambda e, b=b: e.scalar_tensor_tensor(out=nhi[b], in0=t3, scalar=-1.0, in1=t4, op0=ALU.mult, op1=ALU.subtract), r=["t3", "t4"], w=["nhi%d" % b])
                    first = (stt % 4 == 0)
                    last = (stt % 4 == 3)

                    def cmm(stt=stt, b=b, c=c, first=first, last=last):
                        A("pe", lambda e: e.matmul(psb(2 + c), CrT[:, stt, :], hr[b], start=first, stop=False), r=["CrT", "hr%d" % b], w=pk(2 + c))
                        A("pe", lambda e: e.matmul(psb(2 + c), CiT[:, stt, :], nhi[b], start=False, stop=last), r=["CiT", "nhi%d" % b], w=pk(2 + c))
                    pending.append(cmm)
                    yield
                if stt % 4 == 3:
                    flush()
                    for c in range(NCH):
                        cs = slice(TC * c, TC * (c + 1))
                        A("dve", lambda e, ut=ut, cs=cs, c=c: e.scalar_tensor_tensor(out=yv, in0=uT[:, ut, cs], scalar=dTt[:, ut:ut + 1], in1=psb(2 + c), op0=ALU.mult, op1=ALU.add), r=["uT%d" % ut, "dTt"] + pk(2 + c), w=["t1"])
                        A("dve", lambda e: e.tensor_tensor(out=y2, in0=yv, in1=yv, op=ALU.mult), r=["t1"], w=["t2"])
                        A("dve", lambda e: e.tensor_scalar(out=y2, in0=y2, scalar1=0.044715, scalar2=1.0, op0=ALU.mult, op1=ALU.add), r=["t2"], w=["t2"])
                        A("dve", lambda e: e.tensor_tensor(out=y2, in0=y2, in1=yv, op=ALU.mult), r=["t2", "t1"], w=["t2"])
                        A("act", lambda e: e.activation(out=y2, in_=y2, func=AF.Sigmoid, scale=float(2.0 * np.sqrt(2.0 / np.pi))), r=["t2"], w=["t2"])
                        A("dve", lambda e, ut=ut, cs=cs: e.tensor_tensor(out=GL[:, ut, cs], in0=y2, in1=yv, op=ALU.mult), r=["t2", "t1"], w=["uT%d" % ut])
                    yield

        def ssm_glu(l, ctx):
            Wglu = ctx["Wglu"]
            GL = ctx["uT"]
            BR = BRa[:, :].rearrange("p (k t) -> p k t", k=12)
            sg = aT.f32(512)
            n = 0
            for fo in range(4):
                for c in range(NCH):
                    cs = slice(TC * c, TC * (c + 1))
                    bank = 4 + (n % 4)
                    n += 1
                    for kt in range(4):
                        A("pe", lambda e, kt=kt, fo=fo, cs=cs, bank=bank: e.matmul(psb(bank), Wglu[:, kt, 128 * fo:128 * (fo + 1)], GL[:, kt, cs], start=(kt == 0), stop=(kt == 3)), r=["Wglu", "uT%d" % kt], w=pk(bank))
                    A("act", lambda e, fo=fo, bank=bank: e.activation(out=sg, in_=psb(bank), func=AF.Sigmoid, bias=bgluT[:, fo:fo + 1]), r=pk(bank) + ["bgluT"], w=["sg"])
                    A("dve", lambda e, fo=fo, cs=cs: e.tensor_tensor(out=BR[:, fo, cs], in0=GL[:, fo, cs], in1=sg, op=ALU.mult), r=["sg", "uT%d" % fo], w=["BR%d" % fo])
            S.barrier()

        def phase_inproj_ssm(l):
            S.mark("phase_inproj")
            reset_arenas()
            ctx = {}
            class _U:
                def __getitem__(self, key):
                    p, k, t = key
                    base = ACC if k < 2 else RSTDY
                    v = base[:, :].bitcast(BF16).rearrange("p (k t) -> p k t", k=2)
                    return v[p, k % 2, t]
            ctx["uT"] = _U()
            gi_ = gen_inproj(l, ctx["uT"])
            for _ in range(4):
                next(gi_)
            ssm_prep(l, ctx)
            S.mark("phase_ssm")
            gs_ = gen_ssm_main(l, ctx)
            done_i = done_s = False
            while not (done_i and done_s):
                if not done_s:
                    try:
                        next(gs_)
                    except StopIteration:
                        done_s = True
                if not done_i:
                    try:
                        next(gi_)
                    except StopIteration:
                        done_i = True
            S.barrier()
            ssm_glu(l, ctx)

        def phase_conv(l):
            S.mark("phase_conv")
            reset_arenas()
            BR = BRa[:, :].rearrange("p (k t) -> p k t", k=12)
            dma_sp(convwT[:], convwT_in[l], w=["convwT"])
            cb = [aR1.bf(2048) for _ in range(2)]
            cc = [aR1.bf(2048) for _ in range(2)]
            ch = [aR1.bf(2048) for _ in range(2)]
            zb = aR1.f32(2064)
            acc = aR1.f32(2048)
            A("dve", lambda e: e.memset(zb[:, 0:16], 0.0), w=["zb"])
            for j in range(4):
                b = j % 2
                dma_sp(cb[b], PROJ[128 * (40 + j):128 * (41 + j), :], w=["cb%d" % b])
                dma_sp(cc[b], PROJ[128 * (44 + j):128 * (45 + j), :], w=["cc%d" % b])
                dma_sp(ch[b], PROJ[128 * (48 + j):128 * (49 + j), :], w=["ch%d" % b])
                A("dve", lambda e, b=b: e.tensor_tensor(out=zb[:, 16:2064], in0=cc[b], in1=ch[b], op=ALU.mult), r=["cc%d" % b, "ch%d" % b], w=["zb"])
                A("dve", lambda e, j=j: e.tensor_scalar(out=acc, in0=zb[:, 16:2064], scalar1=convwT[:, 3 * j:3 * j + 1], scalar2=None, op0=ALU.mult), r=["zb", "convwT"], w=["cacc"])
                A("dve", lambda e, j=j: e.scalar_tensor_tensor(out=acc, in0=zb[:, 15:2063], scalar=convwT[:, 3 * j + 1:3 * j + 2], in1=acc, op0=ALU.mult, op1=ALU.add), r=["zb", "convwT", "cacc"], w=["cacc"])
                A("dve", lambda e, j=j: e.scalar_tensor_tensor(out=acc, in0=zb[:, 14:2062], scalar=convwT[:, 3 * j + 2:3 * j + 3], in1=acc, op0=ALU.mult, op1=ALU.add), r=["zb", "convwT", "cacc"], w=["cacc"])
                A("dve", lambda e, j=j, b=b: e.tensor_tensor(out=BR[:, 8 + j, :], in0=acc, in1=cb[b], op=ALU.mult), r=["cacc", "cb%d" % b], w=["BR%d" % (8 + j)])
            S.barrier()

        def phase_attn(l):
            S.mark("phase_attn")
            reset_arenas()
            BR = BRa[:, :].rearrange("p (k t) -> p k t", k=12)
            bhi = aR1.bf(24 * 256).rearrange("p (h n) -> p h n", h=24)
            blo = aR1.bf(24 * 256).rearrange("p (h n) -> p h n", h=24)
            qT = [aR1.bf(2048) for _ in range(2)]
            kT = [aR1.bf(2048) for _ in range(2)]
            Vb = [aR1.bf(2048).rearrange("p (b f) -> p b f", b=16) for _ in range(2)]
            identb = aR1.bf(128)
            onesv = aR1.bf(64)
            Uacc = aT.f32(2048)
            Lacc = aT.f32(2048)
            pT = [aT.bf(256) for _ in range(2)]
            dma_pool(bhi.rearrange("p h n -> p (h n)"), bias_hi_in[:, :], w=["bhi"])
            dma_pool(blo.rearrange("p h n -> p (h n)"), bias_lo_in[:, :], w=["blo"])
            dma_pool(identb, ident_in[:, :], w=["identb"])
            A("dve", lambda e: e.memset(onesv, 1.0), w=["onesv"])
            dils = [1, 4, 16]
            n = 0
            sctr = 0
            uctr = 0
            for hp in range(4):
                for g in range(3):
                    dil = dils[g]
                    nb = 16 // dil
                    b = n % 2
                    n += 1
                    qt = 4 + 4 * g + hp
                    kt_ = 16 + 4 * g + hp
                    dma_sp(qT[b], PROJ[128 * qt:128 * (qt + 1), :], w=["qT%d" % b])
                    dma_sp(kT[b], PROJ[128 * kt_:128 * (kt_ + 1), :], w=["kT%d" % b])
                    vsrc = VTOK.rearrange("(bb i r) f -> i r bb f", i=128, r=dil)
                    for r in range(dil):
                        dma_sp(Vb[b][:, r * nb:(r + 1) * nb, :], vsrc[:, r, :, 512 * g + 128 * hp:512 * g + 128 * (hp + 1)], w=["Vb%d" % b])
                    for q4 in range(4):
                        ub = 2 + (uctr % 2)
                        lb = 4 + (uctr % 2)
                        uctr += 1
                        for bi4 in range(4):
                            bi = 4 * q4 + bi4
                            bblk = bi % nb
                            for hh in range(2):
                                head = 8 * g + 2 * hp + hh
                                prt = slice(64 * hh, 64 * (hh + 1))
                                sb_ = sctr % 2
                                sctr += 1
                                sps = PS[:, 256 * sb_:256 * (sb_ + 1)]
                                skey = ["pss%d" % sb_]
                                qv = qT[b][prt, 128 * bi:128 * (bi + 1)]
                                if bblk > 0:
                                    A("pe", lambda e, sps=sps, head=head: e.matmul(sps, identb, bhi[:, head, :], start=True, stop=False), r=["identb", "bhi"], w=skey)
                                    A("pe", lambda e, sps=sps, head=head: e.matmul(sps, identb, blo[:, head, :], start=False, stop=False), r=["identb", "blo"], w=skey)
                                    A("pe", lambda e, sps=sps, b=b, prt=prt, bi=bi, qv=qv: e.matmul(sps[:, 0:128], kT[b][prt, 128 * (bi - 1):128 * bi], qv, start=False, stop=False), r=["kT%d" % b, "qT%d" % b], w=skey)
                                    A("pe", lambda e, sps=sps, b=b, prt=prt, bi=bi, qv=qv: e.matmul(sps[:, 128:256], kT[b][prt, 128 * bi:128 * (bi + 1)], qv, start=False, stop=True), r=["kT%d" % b, "qT%d" % b], w=skey)
                                    A("act", lambda e, sps=sps, sb_=sb_: e.activation(out=pT[sb_], in_=sps, func=AF.Exp, scale=0.125), r=skey, w=["pT%d" % sb_])
                                    ucol = slice(128 * bi4, 128 * (bi4 + 1))
                                    A("pe", lambda e, ub=ub, prt=prt, ucol=ucol, b=b, bi=bi, hh=hh, sb_=sb_: e.matmul(psb(ub)[prt, ucol], Vb[b][:, bi - 1, 64 * hh:64 * (hh + 1)], pT[sb_][:, 0:128], start=True, stop=False), r=["Vb%d" % b, "pT%d" % sb_], w=pk(ub))
                                    A("pe", lambda e, ub=ub, prt=prt, ucol=ucol, b=b, bi=bi, hh=hh, sb_=sb_: e.matmul(psb(ub)[prt, ucol], Vb[b][:, bi, 64 * hh:64 * (hh + 1)], pT[sb_][:, 128:256], start=False, stop=True), r=["Vb%d" % b, "pT%d" % sb_], w=pk(ub))
                                    A("pe", lambda e, lb=lb, prt=prt, ucol=ucol, sb_=sb_: e.matmul(psb(lb)[prt, ucol], onesv, pT[sb_][:, 0:128], start=True, stop=False), r=["onesv", "pT%d" % sb_], w=pk(lb))
                                    A("pe", lambda e, lb=lb, prt=prt, ucol=ucol, sb_=sb_: e.matmul(psb(lb)[prt, ucol], onesv, pT[sb_][:, 128:256], start=False, stop=True), r=["onesv", "pT%d" % sb_], w=pk(lb))
                                else:
                                    sc_ = sps[:, 128:256]
                                    A("pe", lambda e, sc_=sc_, head=head: e.matmul(sc_, identb, bhi[:, head, 128:256], start=True, stop=False), r=["identb", "bhi"], w=skey)
                                    A("pe", lambda e, sc_=sc_, head=head: e.matmul(sc_, identb, blo[:, head, 128:256], start=False, stop=False), r=["identb", "blo"], w=skey)
                                    A("pe", lambda e, sc_=sc_, b=b, prt=prt, bi=bi, qv=qv: e.matmul(sc_, kT[b][prt, 128 * bi:128 * (bi + 1)], qv, start=False, stop=True), r=["kT%d" % b, "qT%d" % b], w=skey)
                                    A("act", lambda e, sc_=sc_, sb_=sb_: e.activation(out=pT[sb_][:, 128:256], in_=sc_, func=AF.Exp, scale=0.125), r=skey, w=["pT%d" % sb_])
                                    ucol = slice(128 * bi4, 128 * (bi4 + 1))
                                    A("pe", lambda e, ub=ub, prt=prt, ucol=ucol, b=b, bi=bi, hh=hh, sb_=sb_: e.matmul(psb(ub)[prt, ucol], Vb[b][:, bi, 64 * hh:64 * (hh + 1)], pT[sb_][:, 128:256], start=True, stop=True), r=["Vb%d" % b, "pT%d" % sb_], w=pk(ub))
                                    A("pe", lambda e, lb=lb, prt=prt, ucol=ucol, sb_=sb_: e.matmul(psb(lb)[prt, ucol], onesv, pT[sb_][:, 128:256], start=True, stop=True), r=["onesv", "pT%d" % sb_], w=pk(lb))
                        if dil == 1:
                            uo = Uacc[:, 512 * q4:512 * (q4 + 1)]
                            lo_ = Lacc[:, 512 * q4:512 * (q4 + 1)]
                            ui = psb(ub)
                            li = psb(lb)
                        elif dil == 4:
                            uo = Uacc.rearrange("p (m r) -> p r m", r=4)[:, q4, :]
                            lo_ = Lacc.rearrange("p (m r) -> p r m", r=4)[:, q4, :]
                            ui = psb(ub)
                            li = psb(lb)
                        else:
                            uo = Uacc.rearrange("p (i r) -> p i r", r=16)[:, :, 4 * q4:4 * (q4 + 1)]
                            lo_ = Lacc.rearrange("p (i r) -> p i r", r=16)[:, :, 4 * q4:4 * (q4 + 1)]
                            ui = psb(ub).rearrange("p (rr i) -> p i rr", rr=4)
                            li = psb(lb).rearrange("p (rr i) -> p i rr", rr=4)
                        if g == 0:
                            A("dve", lambda e, uo=uo, ui=ui: e.tensor_copy(out=uo, in_=ui), r=pk(ub), w=["Uacc"])
                            A("dve", lambda e, lo_=lo_, li=li: e.tensor_copy(out=lo_, in_=li), r=pk(lb), w=["Lacc"])
                        else:
                            A("dve", lambda e, uo=uo, ui=ui: e.tensor_tensor(out=uo, in0=uo, in1=ui, op=ALU.add), r=pk(ub) + ["Uacc"], w=["Uacc"])
                            A("dve", lambda e, lo_=lo_, li=li: e.tensor_tensor(out=lo_, in0=lo_, in1=li, op=ALU.add), r=pk(lb) + ["Lacc"], w=["Lacc"])
                A("dve", lambda e: e.reciprocal(out=Lacc, in_=Lacc), r=["Lacc"], w=["Lacc"])
                A("dve", lambda e, hp=hp: e.tensor_tensor(out=BR[:, 4 + hp, :], in0=Uacc, in1=Lacc, op=ALU.mult), r=["Uacc", "Lacc"], w=["BR%d" % (4 + hp)])
            S.barrier()

        def phase_merge(l):
            S.mark("phase_merge")
            reset_arenas()
            BR = BRa[:, :].rearrange("p (k t) -> p k t", k=12)
            mT = aR1.bf(32768).rearrange("p (k t) -> p k t", k=16)
            Wb = []
            for br, wd in enumerate((w_ssm_out, w_attn_out, w_conv_out)):
                Wb.append(load_w(br, wd[l].rearrange("(k p) n -> p k n", p=128), 4, 2048, "WS%d" % br))
            gt = [[aT.bf(512) for _ in range(3)] for _ in range(2)]
            m0 = aT.f32(512)
            m1 = aT.f32(512)
            m2 = aT.f32(512)
            n = 0
            for fo in range(16):
                for c in range(NCH):
                    cs = slice(TC * c, TC * (c + 1))
                    b = n % 2
                    n += 1
                    pb = 3 * b
                    for br in range(3):
                        gtile = 52 + 16 * br + fo
                        dma_sp(gt[b][br], PROJ[128 * gtile:128 * (gtile + 1), cs], w=["gt%d%d" % (b, br)])
                        for kt in range(4):
                            A("pe", lambda e, br=br, kt=kt, fo=fo, cs=cs, pb=pb: e.matmul(psb(pb + br), Wb[br][:, kt, 128 * fo:128 * (fo + 1)], BR[:, 4 * br + kt, cs], start=(kt == 0), stop=(kt == 3)),
                              r=["WS%d" % br, "BR%d" % (4 * br + kt)], w=pk(pb + br))
                    A("dve", lambda e, b=b, pb=pb: e.tensor_tensor(out=m0, in0=psb(pb), in1=gt[b][0], op=ALU.mult), r=pk(pb) + ["gt%d0" % b], w=["m0"])
                    A("dve", lambda e, b=b, pb=pb: e.tensor_tensor(out=m1, in0=psb(pb + 1), in1=gt[b][1], op=ALU.mult), r=pk(pb + 1) + ["gt%d1" % b], w=["m1"])
                    A("dve", lambda e, b=b, pb=pb: e.tensor_tensor(out=m2, in0=psb(pb + 2), in1=gt[b][2], op=ALU.mult), r=pk(pb + 2) + ["gt%d2" % b], w=["m2"])
                    A("pool", lambda e: e.tensor_tensor(out=m0, in0=m0, in1=m1, op=ALU.add), r=["m0", "m1"], w=["m0"])
                    A("pool", lambda e, fo=fo, cs=cs: e.tensor_tensor(out=mT[:, fo, cs], in0=m0, in1=m2, op=ALU.add), r=["m0", "m2"], w=["mT%d" % fo])
            S.barrier()
            aT.reset()
            aBR.reset()
            yst = [aBR.f32(2048) for _ in range(2)]
            sqf = aBR.f32(2048)
            wv = w_o[l].rearrange("(kt p) n -> p kt n", p=128)
            mkeys = ["mT%d" % k for k in range(16)]
            n = 0
            for cg in range(4):
                slot = cg % 3
                wk = "WS%d" % slot
                W = load_w(slot, wv[:, :, 512 * cg:512 * (cg + 1)], 16, 512, wk)
                for j in range(4):
                    fo = 4 * cg + j
                    hb = 4 * (half_ctr[0] % 2)
                    half_ctr[0] += 1
                    for c in range(NCH):
                        for kt in range(16):
                            A("pe", lambda e, W=W, kt=kt, j=j, c=c, hb=hb: e.matmul(psb(hb + c), W[:, kt, 128 * j:128 * (j + 1)], mT[:, kt, TC * c:TC * (c + 1)], start=(kt == 0), stop=(kt == 15)),
                              r=[wk, "mT%d" % kt], w=pk(hb + c))
                    b = n % 2
                    n += 1
                    A("act", lambda e, b=b, hb=hb: e.activation(out=yst[b], in_=psb(hb, 4), func=AF.Copy), r=pk(hb, 4), w=["yst%d" % b])
                    dma_sp(YT[128 * fo:128 * (fo + 1), :], yst[b], r=["yst%d" % b])
                    if fo == 0:
                        A("act", lambda e, hb=hb: e.activation(out=ACC[:], in_=psb(hb, 4), func=AF.Square), r=pk(hb, 4), w=["ACC"])
                    else:
                        A("act", lambda e, hb=hb: e.activation(out=sqf, in_=psb(hb, 4), func=AF.Square), r=pk(hb, 4), w=["sqf"])
                        A("pool", lambda e: e.tensor_tensor(out=ACC[:], in0=ACC[:], in1=sqf, op=ALU.add), r=["sqf", "ACC"], w=["ACC"])
            finish_rstd()
            S.barrier()

        def finish_rstd():
            for c in range(NCH):
                cs = slice(TC * c, TC * (c + 1))
                A("pe", lambda e, c=c, cs=cs: e.matmul(psb(c), ones32[:], ACC[:, cs], start=True, stop=True), r=["ones32", "ACC"], w=pk(c))
            A("act", lambda e: e.activation(out=RSTDY[:], in_=psb(0, 4), func=AF.Sqrt, bias=epsT[:, 0:1], scale=1.0 / D), r=pk(0, 4) + ["epsT"], w=["RSTDY"])
            A("dve", lambda e: e.reciprocal(out=RSTDY[:], in_=RSTDY[:]), r=["RSTDY"], w=["RSTDY"])

        def phase_ffn(l):
            S.mark("phase_ffn")
            reset_arenas()
            hT = aR1.bf(32768).rearrange("p (k t) -> p k t", k=16)
            dma_sp(ffnwT[:], ffnwT_in[l], w=["ffnwT"])
            abuf = [aBR.f32(2064) for _ in range(2)]
            bbuf = [aBR.f32(2064) for _ in range(2)]
            cb = aBR.f32(2048)
            ca = aT.f32(2048)
            sa = aT.f32(2048)
            gst = [aT.bf(2048) for _ in range(2)]
            for b in range(2):
                A("dve", lambda e, b=b: e.memset(abuf[b][:, 0:16], 0.0), w=["abuf%d" % b])
                A("dve", lambda e, b=b: e.memset(bbuf[b][:, 0:16], 0.0), w=["bbuf%d" % b])
            wv = w_up[l].rearrange("(kt p) n -> p kt n", p=128)
            n = 0
            hn = 0

            def conv3(dst, src, w0, dkey, skey):
                A("dve", lambda e: e.tensor_scalar(out=dst, in0=src[:, 16:2064], scalar1=ffnwT[:, w0:w0 + 1], scalar2=None, op0=ALU.mult), r=[skey, "ffnwT"], w=[dkey])
                A("dve", lambda e: e.scalar_tensor_tensor(out=dst, in0=src[:, 15:2063], scalar=ffnwT[:, w0 + 1:w0 + 2], in1=dst, op0=ALU.mult, op1=ALU.add), r=[skey, "ffnwT", dkey], w=[dkey])
                A("dve", lambda e: e.scalar_tensor_tensor(out=dst, in0=src[:, 14:2062], scalar=ffnwT[:, w0 + 2:w0 + 3], in1=dst, op0=ALU.mult, op1=ALU.add), r=[skey, "ffnwT", dkey], w=[dkey])

            for pg in range(11):
                sla = (2 * pg) % 3
                slb = (2 * pg + 1) % 3
                Wa = load_w(sla, wv[:, :, 512 * pg:512 * (pg + 1)], 16, 512, "WS%d" % sla)
                Wb_ = load_w(slb, wv[:, :, DFF + 512 * pg:DFF + 512 * (pg + 1)], 16, 512, "WS%d" % slb)
                for j in range(4):
                    fa = 4 * pg + j
                    b = n % 2
                    n += 1
                    for th in range(2):
                        hb = 4 * (hn % 2)
                        hn += 1
                        for (W, wk, boff) in ((Wa, "WS%d" % sla, 0), (Wb_, "WS%d" % slb, 2)):
                            for kt in range(16):
                                for c2 in range(2):
                                    tok = slice(1024 * th + 512 * c2, 1024 * th + 512 * (c2 + 1))
                                    A("pe", lambda e, W=W, kt=kt, j=j, tok=tok, bank=hb + boff + c2: e.matmul(psb(bank), W[:, kt, 128 * j:128 * (j + 1)], hT[:, kt, tok], start=(kt == 0), stop=(kt == 15)),
                                      r=[wk, "hT%d" % kt], w=pk(hb + boff + c2))
                        A("act", lambda e, b=b, th=th, hb=hb: e.activation(out=abuf[b][:, 16 + 1024 * th:16 + 1024 * (th + 1)], in_=psb(hb, 2), func=AF.Copy), r=pk(hb, 2), w=["abuf%d" % b])
                        A("act", lambda e, b=b, th=th, hb=hb: e.activation(out=bbuf[b][:, 16 + 1024 * th:16 + 1024 * (th + 1)], in_=psb(hb + 2, 2), func=AF.Copy), r=pk(hb + 2, 2), w=["bbuf%d" % b])
                    conv3(ca, abuf[b], 3 * fa, "ca", "abuf%d" % b)
                    A("act", lambda e: e.activation(out=sa, in_=ca, func=AF.Silu), r=["ca"], w=["sa"])
                    conv3(cb, bbuf[b], 3 * (44 + fa), "cb", "bbuf%d" % b)
                    A("dve", lambda e, b=b: e.tensor_tensor(out=gst[b], in0=sa, in1=cb, op=ALU.mult), r=["sa", "cb"], w=["gst%d" % b])
                    dma_sp(GS[128 * fa:128 * (fa + 1), :], gst[b], r=["gst%d" % b], w=["GS%d" % fa])
            S.barrier()
            reset_arenas()
            g1 = aR1.bf(32768).rearrange("p (k t) -> p k t", k=32)
            g2 = aBR.bf(12288).rearrange("p (k t) -> p k t", k=12)
            yst = [aBR.f32(1024) for _ in range(2)]
            sqf = [aT.f32(1024) for _ in range(2)]
            gv = GS.rearrange("(k p) t -> p k t", p=128)
            wdv = w_down[l].rearrange("(k p) n -> p k n", p=128)
            WD = [WSall[:, 11264 * i:11264 * (i + 1)].rearrange("p (k n) -> p k n", k=22) for i in range(2)]
            n = 0
            nl_ = 0
            for th in range(2):
                ts_ = slice(1024 * th, 1024 * (th + 1))
                for q in range(4):
                    dma_sp(g1[:, 8 * q:8 * (q + 1), :], gv[:, 8 * q:8 * (q + 1), ts_], r=["GS%d" % i for i in range(8 * q, 8 * q + 8)], w=["g1_%d" % q])
                dma_sp(g2[:, 0:6, :], gv[:, 32:38, ts_], r=["GS%d" % i for i in range(32, 38)], w=["g2_0"])
                dma_sp(g2[:, 6:12, :], gv[:, 38:44, ts_], r=["GS%d" % i for i in range(38, 44)], w=["g2_1"])

                def gsrc(kt, c2):
                    if kt < 32:
                        return g1[:, kt, 512 * c2:512 * (c2 + 1)], "g1_%d" % (kt // 8)
                    return g2[:, kt - 32, 512 * c2:512 * (c2 + 1)], "g2_%d" % ((kt - 32) // 6)

                for cg in range(4):
                    for kh in range(2):
                        slot = nl_ % 2
                        nl_ += 1
                        wk = "WD%d" % slot
                        dma_pool(WD[slot], wdv[:, 22 * kh:22 * (kh + 1), 512 * cg:512 * (cg + 1)], w=[wk])
                        for j in range(4):
                            fo = 4 * cg + j
                            bank = 2 * j
                            for k2 in range(22):
                                kt = 22 * kh + k2
                                for c2 in range(2):
                                    gs_, gk = gsrc(kt, c2)
                                    A("pe", lambda e, slot=slot, k2=k2, j=j, gs_=gs_, bank=bank + c2, st_=(kt == 0), sp_=(kt == 43): e.matmul(psb(bank), WD[slot][:, k2, 128 * j:128 * (j + 1)], gs_, start=st_, stop=sp_), r=[wk, gk], w=pk(bank + c2))
                            if kh == 1:
                                b = n % 2
                                n += 1
                                A("act", lambda e, b=b, bank=bank: e.activation(out=yst[b], in_=psb(bank, 2), func=AF.Copy), r=pk(bank, 2), w=["yst%d" % b])
                                dma_sp(YT[128 * fo:128 * (fo + 1), ts_], yst[b], r=["yst%d" % b])
                                if fo == 0:
                                    A("act", lambda e, bank=bank, ts_=ts_: e.activation(out=ACC[:, ts_], in_=psb(bank, 2), func=AF.Square), r=pk(bank, 2), w=["ACC"])
                                else:
                                    A("act", lambda e, b=b, bank=bank: e.activation(out=sqf[b], in_=psb(bank, 2), func=AF.Square), r=pk(bank, 2), w=["sqf%d" % b])
                                    A("pool", lambda e, b=b, ts_=ts_: e.tensor_tensor(out=ACC[:, ts_], in0=ACC[:, ts_], in1=sqf[b], op=ALU.add), r=["sqf%d" % b, "ACC"], w=["ACC"])
            S.barrier()
            finish_rstd()
            S.barrier()

        ident_in = din("ident", [128, 128])

        for l in range(n_layers):
            if l == 0:
                phase_mod(0)
                phase_norm(xT_in, False, None, AG[:, 0:16], modT[:, 0:16], XT, True)
            phase_inproj_ssm(l)
            phase_conv(l)
            phase_attn(l)
            phase_merge(l)
            phase_norm(XT, True, AG[:, 16:32], AG[:, 32:48], modT[:, 48:64], XT, True)
            phase_ffn(l)
            if l + 1 < n_layers:
                G2 = sbt("G2_%d" % l, [128, 16], F32)
                A("dve", lambda e, G2=G2: e.tensor_copy(out=G2[:], in_=AG[:, 48:64]), w=["G2s"])
                S.barrier()
                phase_mod(l + 1)
                phase_norm(XT, True, G2[:], AG[:, 0:16], modT[:, 0:16], XT, True)
            else:
                phase_norm(XT, True, AG[:, 48:64], None, None, outT, False)

        sems = {e: st.enter_context(nc.semaphore("s_" + e)) for e in ENGS}
        dma_sems = {e: [st.enter_context(nc.semaphore("d_%s%d" % (e, i))) for i in range(DMA_K)] for e in ("sp", "pool")}
        S.mark("end")
        build_program.marks = S.marks
        S.prepare(sems, dma_sems)
        block = st.enter_context(nc.Block())

        @block.tensor
        def _(e):
            S.run("pe", e)

        @block.scalar
        def _(e):
            S.run("act", e)

        @block.vector
        def _(e):
            S.run("dve", e)

        @block.gpsimd
        def _(e):
            S.run("pool", e)

        @block.sync
        def _(e):
            S.run("sp", e)
    return nc


def prep_shared(inp):
    f = lambda a: np.ascontiguousarray(np.asarray(a, dtype=np.float32))
    sh = {}
    for k in ("w_mod", "w_in", "w_glu", "w_ssm_out", "w_attn_out", "w_conv_out", "w_o", "w_up", "w_down"):
        sh[k] = f(inp[k])
    sh["bmodT"] = f(np.asarray(inp["b_mod"]).reshape(NL, 96, 128).transpose(0, 2, 1))
    g = np.stack([np.asarray(inp[k]).reshape(NL, 16, 128).transpose(0, 2, 1) for k in ("g_pre_mix", "g_post_mix", "g_pre_ffn", "g_post_ffn")], axis=2)
    sh["gains"] = f(g.reshape(NL, 128, 64))
    sh["bgateT"] = f(np.asarray(inp["b_gate"]).reshape(NL, 48, 128).transpose(0, 2, 1))
    ld = np.repeat(np.asarray(inp["ssm_log_dt"]), 64, axis=1)
    ar = np.asarray(inp["ssm_a_re"]).reshape(NL, 2048)
    ai = np.asarray(inp["ssm_a_im"]).reshape(NL, 2048)
    sh["ssm_flat"] = f(np.stack([ld, ar, ai], axis=1))
    sm = np.stack([a.reshape(NL, 16, 128).transpose(0, 2, 1) for a in (ld, ar, ai)], axis=2)
    sh["ssm_sm"] = f(sm.reshape(NL, 128, 48))
    bre = np.asarray(inp["ssm_b_re"])
    bim = np.asarray(inp["ssm_b_im"])
    cre = np.asarray(inp["ssm_c_re"])
    cim = np.asarray(inp["ssm_c_im"])
    BTre = np.zeros((NL, 128, 16, 128), np.float32)
    BTim = np.zeros((NL, 128, 16, 128), np.float32)
    CTre = np.zeros((NL, 128, 16, 128), np.float32)
    CTim = np.zeros((NL, 128, 16, 128), np.float32)
    for g_ in range(32):
        cs = slice(16 * (g_ % 8), 16 * (g_ % 8) + 16)
        ss = slice(64 * (g_ % 2), 64 * (g_ % 2) + 64)
        BTre[:, cs, g_ // 2, ss] = bre[:, g_].transpose(0, 2, 1)
        BTim[:, cs, g_ // 2, ss] = bim[:, g_].transpose(0, 2, 1)
        CTre[:, ss, g_ // 2, cs] = cre[:, g_].transpose(0, 2, 1)
        CTim[:, ss, g_ // 2, cs] = cim[:, g_].transpose(0, 2, 1)
    sh["BTre"] = BTre.reshape(NL, 128, 2048)
    sh["BTim"] = BTim.reshape(NL, 128, 2048)
    sh["CTre"] = CTre.reshape(NL, 128, 2048)
    sh["CTim"] = CTim.reshape(NL, 128, 2048)
    sh["dT"] = f(np.asarray(inp["ssm_d"]).reshape(NL, 4, 128).transpose(0, 2, 1))
    sh["bgluT"] = f(np.asarray(inp["b_glu"]).reshape(NL, 4, 128).transpose(0, 2, 1))
    sh["convwT"] = f(np.asarray(inp["conv_mix_w"]).reshape(NL, 3, 4, 128).transpose(0, 3, 2, 1).reshape(NL, 128, 12))
    sh["ffnwT"] = f(np.asarray(inp["ffn_conv_w"]).reshape(NL, 3, 88, 128).transpose(0, 3, 2, 1).reshape(NL, 128, 264))
    sh["iota"] = f(np.tile(np.arange(512, dtype=np.float32)[None, :], (128, 1)))
    sh["ones"] = np.ones((128, 128), np.float32)
    sh["ident"] = np.eye(128, dtype=np.float32)
    hi, lo = make_bias_tables()
    sh["bias_hi"] = f(hi)
    sh["bias_lo"] = f(lo)
    return sh


def kernel(**inputs):
    n_layers = int(os.environ.get("K_NLAYERS", NL))
    debug = bool(int(os.environ.get("K_DEBUG", "0")))
    ncores = int(os.environ.get("K_NCORES", 8))
    sh = prep_shared(inputs)
    x = np.asarray(inputs["x"], dtype=np.float32)
    c = np.asarray(inputs["c"], dtype=np.float32)
    in_maps = []
    for b in range(ncores):
        m = dict(sh)
        m["xT"] = np.ascontiguousarray(x[b].T)
        m["cT"] = np.ascontiguousarray(c[b].reshape(16, 128).T)
        in_maps.append(m)
    nc = build_program(n_layers=n_layers, debug=debug)
    res = run_bass_kernel_spmd(nc, in_maps, core_ids=list(range(ncores)))
    if debug:
        kernel.last_results = res.results
    out = np.stack([np.ascontiguousarray(r["outT"].T) for r in res.results], axis=0)
    return out.astype(np.float32)
```

```python
import os
import numpy as np
import concourse.bass as bass
import concourse.mybir as mybir
from concourse.bass_utils import run_bass_kernel_spmd
from contextlib import ExitStack

F32 = mybir.dt.float32
BF16 = mybir.dt.bfloat16
AF = mybir.ActivationFunctionType
ALU = mybir.AluOpType

D = 2048
L = 2048
KT = 16
NL = 4
N_IN = 12800
DFF = 5632
NCH = 4
TC = 512
PI = float(np.pi)
USE_ACT_TABLES = 0

ENGS = ("pe", "act", "dve", "pool", "sp")
DMA_K = 8


class Op:
    __slots__ = ("eng", "fn", "deps", "is_dma", "signal", "sigval", "dsem", "dval", "dprev")

    def __init__(self, eng, fn, is_dma):
        self.eng = eng
        self.fn = fn
        self.is_dma = is_dma
        self.deps = ()
        self.signal = False
        self.sigval = 0
        self.dsem = None
        self.dval = 0
        self.dprev = None


class Sched:
    def __init__(self, same_engine_sync=True):
        self.ops = []
        self.lw = {}
        self.rd = {}
        self.same = same_engine_sync
        self.last_on = {e: None for e in ENGS}
        self.last_dmas = {e: [] for e in ENGS}
        self.barrier_deps = ()
        self._fresh = {e: False for e in ENGS}
        self.marks = []
        self.npe = 0

    def mark(self, name):
        self.marks.append((name, self.npe))

    def add(self, eng, fn, reads=(), writes=(), dma=False):
        op = Op(eng, fn, dma)
        if eng == "pe":
            self.npe += 1
        deps = set(self.barrier_deps) if self._fresh[eng] else set()
        self._fresh[eng] = False
        lw, rd = self.lw, self.rd
        for k in reads:
            w = lw.get(k)
            if w is not None:
                deps.add(w)
        for k in writes:
            w = lw.get(k)
            if w is not None:
                deps.add(w)
            r = rd.get(k)
            if r:
                deps.update(r)
        for k in reads:
            lst = rd.get(k)
            if lst is None:
                rd[k] = [op]
            elif (not dma) and lst and lst[-1].eng == eng and not lst[-1].is_dma:
                lst[-1] = op
            else:
                lst.append(op)
        for k in writes:
            lw[k] = op
            rd[k] = []
        deps.discard(op)
        op.deps = deps
        self.ops.append(op)
        if dma:
            ld = self.last_dmas[eng]
            ld.append(op)
            if len(ld) > DMA_K:
                ld.pop(0)
        else:
            self.last_on[eng] = op
        return op

    def barrier(self):
        deps = [o for o in self.last_on.values() if o is not None]
        for e in ENGS:
            deps.extend(self.last_dmas[e])
        self.barrier_deps = tuple(deps)
        self._fresh = {e: True for e in ENGS}
        self.lw = {}
        self.rd = {}

    def prepare(self, sems, dma_sems):
        ops = self.ops
        same = self.same
        for op in ops:
            for d in op.deps:
                if d.is_dma:
                    continue
                if d.eng == op.eng and not op.is_dma and (d.eng == "pe" or not same):
                    continue
                d.signal = True
        cnt = {e: 0 for e in ENGS}
        dcnt = {e: 0 for e in ENGS}
        hist = {e: [] for e in ENGS}
        for op in ops:
            if op.is_dma:
                n = dcnt[op.eng]
                dcnt[op.eng] = n + 1
                op.dsem = dma_sems[op.eng][n % DMA_K]
                op.dval = 16 * (n // DMA_K + 1)
                h = hist[op.eng]
                op.dprev = h[n - DMA_K] if n >= DMA_K else None
                h.append(op)
            elif op.signal:
                cnt[op.eng] += 1
                op.sigval = cnt[op.eng]
        self.by_eng = {e: [] for e in ENGS}
        for op in ops:
            self.by_eng[op.eng].append(op)
        self.sems = sems

    def run(self, eng_name, e):
        waited = {}
        sems = self.sems
        same = self.same
        for op in self.by_eng[eng_name]:
            need = {}
            deps = list(op.deps)
            if op.is_dma and op.dprev is not None:
                deps.append(op.dprev)
            for d in deps:
                if d.is_dma:
                    s, v = d.dsem, d.dval
                else:
                    if d.eng == eng_name and not op.is_dma and (eng_name == "pe" or not same):
                        continue
                    if not d.signal:
                        continue
                    s, v = sems[d.eng], d.sigval
                if waited.get(s, 0) >= v:
                    continue
                if need.get(s, 0) < v:
                    need[s] = v
            for s, v in need.items():
                e.wait_ge(s, v)
                waited[s] = v
            ins = op.fn(e)
            if op.is_dma:
                ins.then_inc(op.dsem, 16)
            elif op.signal:
                ins.then_inc(sems[eng_name], 1)
        last = {}
        for op in self.by_eng[eng_name]:
            if op.is_dma:
                last[op.dsem] = op.dval
        for s, v in last.items():
            e.wait_ge(s, v)


def alibi_slopes(n_heads):
    return np.array([2.0 ** (-8.0 * (h + 1) / n_heads) for h in range(n_heads)], dtype=np.float64)


def _bf16_round(x):
    u = np.asarray(x, np.float32).view(np.uint32).astype(np.uint64)
    r = ((u + 0x7FFF + ((u >> 16) & 1)) & 0xFFFF0000).astype(np.uint32)
    return r.view(np.float32)


def make_bias_tables():
    slopes = alibi_slopes(24)
    dil = [1, 4, 16]
    kj = np.arange(128)[:, None]
    qi = np.arange(128)[None, :]
    tab = np.zeros((128, 24, 256), np.float64)
    NEG = -240000.0
    for h in range(24):
        a = slopes[h] * dil[h // 8] * 8.0
        prev = np.where(qi <= kj, -a * (128 + qi - kj), NEG)
        cur = np.where(qi >= kj, -a * (qi - kj), NEG)
        tab[:, h, 0:128] = prev
        tab[:, h, 128:256] = cur
    hi = _bf16_round(tab.astype(np.float32))
    lo = _bf16_round((tab - hi.astype(np.float64)).astype(np.float32))
    return hi.reshape(128, 24 * 256), lo.reshape(128, 24 * 256)


def build_program(n_layers=NL, debug=False):
    nc = bass.Bass("TRN2", target_bir_lowering=False)
    S = Sched()

    def din(name, shape, dt=F32):
        return nc.dram_tensor(name, list(shape), dt, kind="ExternalInput").ap()

    def dscr(name, shape, dt):
        return nc.dram_tensor(name, list(shape), dt, kind=("ExternalOutput" if debug else "Internal")).ap()

    xT_in = din("xT", [D, L])
    cT_in = din("cT", [128, 16])
    w_mod = din("w_mod", [NL, D, 6 * D])
    w_in = din("w_in", [NL, D, N_IN])
    w_glu = din("w_glu", [NL, 512, 512])
    w_ssm_out = din("w_ssm_out", [NL, 512, D])
    w_attn_out = din("w_attn_out", [NL, 512, D])
    w_conv_out = din("w_conv_out", [NL, 512, D])
    w_o = din("w_o", [NL, D, D])
    w_up = din("w_up", [NL, D, 2 * DFF])
    w_down = din("w_down", [NL, DFF, D])
    bmodT_in = din("bmodT", [NL, 128, 96])
    gains_in = din("gains", [NL, 128, 64])
    bgateT_in = din("bgateT", [NL, 128, 48])
    ssm_sm_in = din("ssm_sm", [NL, 128, 48])
    ssm_flat_in = din("ssm_flat", [NL, 3, 2048])
    BTre_in = din("BTre", [NL, 128, 2048])
    BTim_in = din("BTim", [NL, 128, 2048])
    CTre_in = din("CTre", [NL, 128, 2048])
    CTim_in = din("CTim", [NL, 128, 2048])
    dT_in = din("dT", [NL, 128, 4])
    bgluT_in = din("bgluT", [NL, 128, 4])
    convwT_in = din("convwT", [NL, 128, 12])
    ffnwT_in = din("ffnwT", [NL, 128, 264])
    iota_in = din("iota", [128, 512])
    ones_in = din("ones", [128, 128])
    bias_hi_in = din("bias_hi", [128, 24 * 256])
    bias_lo_in = din("bias_lo", [128, 24 * 256])

    outT = nc.dram_tensor("outT", [D, L], F32, kind="ExternalOutput").ap()
    XT = dscr("XT", [D, L], F32)
    YT = dscr("YT", [D, L], F32)
    PROJ = dscr("PROJ", [N_IN, L], BF16)
    VTOK = dscr("VTOK", [L, 1536], BF16)
    GS = dscr("GS", [DFF, L], BF16)

    st = ExitStack()
    with st:
        def sbt(name, shape, dt):
            return st.enter_context(nc.sbuf_tensor("sb_" + name, list(shape), dt))

        R1 = sbt("R1", [128, 32768], BF16)
        WSall = sbt("WS", [128, 24576], BF16)
        WSa = [WSall[:, 8192 * i:8192 * (i + 1)] for i in range(3)]
        BRa = sbt("BR", [128, 24576], BF16)
        Ta = sbt("T", [128, 12288], BF16)
        ACC = sbt("ACC", [128, 2048], F32)
        RSTDY = sbt("RSTDY", [128, 2048], F32)
        PS = st.enter_context(nc.psum_tensor("PS", [128, 4096], F32))
        iota = sbt("iota", [128, 512], F32)
        ones32 = sbt("ones32", [128, 128], F32)
        onesbf = sbt("onesbf", [128, 128], BF16)
        epsT = sbt("epsT", [128, 1], F32)
        condT = sbt("condT", [128, 16], BF16)
        cTs = sbt("cTs", [128, 16], F32)
        modT2 = [sbt("modT%d" % i, [128, 96], F32) for i in range(2)]
        bmodT = sbt("bmodT", [128, 96], F32)
        gains = sbt("gains", [128, 64], F32)
        AG2 = [sbt("AG%d" % i, [128, 64], F32) for i in range(2)]
        bgateT = sbt("bgateT", [128, 48], F32)
        ssm_sm = sbt("ssm_sm", [128, 48], F32)
        sm_r = sbt("sm_r", [128, 16], F32)
        sm_th = sbt("sm_th", [128, 16], F32)
        sm_cT = sbt("sm_cT", [128, 16], F32)
        sm_sT = sbt("sm_sT", [128, 16], F32)
        sm_tmp = sbt("sm_tmp", [128, 96], F32)
        carry = sbt("carry", [128, 8], F32)
        cst = sbt("cst", [128, 4], F32)
        dTt = sbt("dTt", [128, 4], F32)
        bgluT = sbt("bgluT", [128, 4], F32)
        convwT = sbt("convwT", [128, 12], F32)
        ffnwT = sbt("ffnwT", [128, 264], F32)

        def psb(b, n=1):
            return PS[:, 512 * b:512 * (b + n)]

        def pk(b, n=1):
            return ["ps%d" % i for i in range(b, b + n)]

        class Arena:
            def __init__(self, t, size, name):
                self.t = t
                self.size = size
                self.off = 0
                self.name = name

            def reset(self):
                self.off = 0

            def bf(self, n):
                assert self.off + n <= self.size, (self.name, self.off, n, self.size)
                a = self.t[:, self.off:self.off + n]
                self.off += n
                return a

            def f32(self, n):
                return self.bf(2 * n).bitcast(F32)

        aR1 = Arena(R1, 32768, "R1")
        aBR = Arena(BRa, 24576, "BR")
        aT = Arena(Ta, 12288, "T")

        def reset_arenas():
            aR1.reset()
            aBR.reset()
            aT.reset()

        def A(eng, fn, r=(), w=(), dma=False):
            S.add(eng, fn, r, w, dma)

        def dma_sp(out, in_, r=(), w=()):
            A("sp", lambda e: e.dma_start(out=out, in_=in_), r, w, True)

        def dma_pool(out, in_, r=(), w=()):
            A("pool", lambda e: e.dma_start(out=out, in_=in_), r, w, True)

        dma_sp(iota[:], iota_in[:, :], w=["iota"])
        dma_sp(ones32[:], ones_in[:, :], w=["ones32"])
        dma_pool(onesbf[:], ones_in[:, :], w=["onesbf"])
        dma_sp(cTs[:], cT_in[:, :], w=["cTs"])
        A("dve", lambda e: e.memset(epsT[:], 1e-6), w=["epsT"])
        A("dve", lambda e: e.memset(cst[:, 0:1], 0.0), w=["cst"])
        A("dve", lambda e: e.memset(cst[:, 1:2], PI / 2), w=["cst"])
        A("dve", lambda e: e.memset(cst[:, 2:3], 12582912.0), w=["cst"])
        A("dve", lambda e: e.memset(cst[:, 3:4], -12582912.0), w=["cst"])
        A("act", lambda e: e.activation(out=condT[:], in_=cTs[:], func=AF.Silu), r=["cTs"], w=["condT"])
        S.barrier()

        def sincos(x_ap, n, out_sin, out_cos, tmpk, tmpp, keyp):
            M = 12582912.0
            C1 = 6.28125
            C2 = float(2 * np.pi - 6.28125)
            for which, outp, shift in (("s", out_sin, 0.0), ("c", out_cos, PI / 2)):
                kk = keyp + which
                A("dve", lambda e, shift=shift: e.tensor_scalar(out=tmpp, in0=x_ap, scalar1=shift, scalar2=None, op0=ALU.add), r=[keyp + "x"], w=[keyp + "p"])
                A("dve", lambda e: e.tensor_scalar(out=tmpk, in0=tmpp, scalar1=float(1 / (2 * np.pi)), scalar2=M, op0=ALU.mult, op1=ALU.add), r=[keyp + "p"], w=[keyp + "k"])
                A("dve", lambda e: e.tensor_scalar(out=tmpk, in0=tmpk, scalar1=-M, scalar2=None, op0=ALU.add), r=[keyp + "k"], w=[keyp + "k"])
                A("dve", lambda e: e.scalar_tensor_tensor(out=tmpp, in0=tmpk, scalar=-C1, in1=tmpp, op0=ALU.mult, op1=ALU.add), r=[keyp + "k", keyp + "p"], w=[keyp + "p"])
                A("dve", lambda e: e.scalar_tensor_tensor(out=tmpp, in0=tmpk, scalar=-C2, in1=tmpp, op0=ALU.mult, op1=ALU.add), r=[keyp + "k", keyp + "p"], w=[keyp + "p"])
                A("dve", lambda e: e.tensor_scalar(out=tmpp, in0=tmpp, scalar1=PI, scalar2=-PI, op0=ALU.min, op1=ALU.max), r=[keyp + "p"], w=[keyp + "p"])
                A("act", lambda e, outp=outp: e.activation(out=outp, in_=tmpp, func=AF.Sin), r=[keyp + "p"], w=[kk])

        def load_w(slot, dram_ap3, nkt, ncol, key):
            v = WSa[slot][:, 0:nkt * ncol].rearrange("p (k n) -> p k n", k=nkt)
            dma_pool(v, dram_ap3, w=[key])
            return v

        def gen_mod(l):
            par = l % 2
            modT = modT2[par]
            AG = AG2[par]
            mk, ak = "modT%d" % par, "AG%d" % par
            dma_sp(bmodT[:], bmodT_in[l], w=["bmodT"])
            dma_sp(gains[:], gains_in[l], w=["gains"])
            dma_sp(bgateT[:], bgateT_in[l], w=["bgateT"])
            wv = w_mod[l].rearrange("(kt p) n -> p kt n", p=128)
            for cg in range(24):
                slot = cg % 3
                W = load_w(slot, wv[:, :, 512 * cg:512 * (cg + 1)], 16, 512, "WS%d" % slot)
                for j in range(4):
                    ct = 4 * cg + j
                    for kt in range(16):
                        A("pe", lambda e, W=W, j=j, kt=kt, ct=ct: e.matmul(PS[:, 1024 + ct:1025 + ct], W[:, kt, 128 * j:128 * (j + 1)], condT[:, kt:kt + 1], start=(kt == 0), stop=(kt == 15)),
                          r=["WS%d" % slot, "condT"], w=["ps2"])
                yield
            A("dve", lambda e: e.tensor_tensor(out=modT[:], in0=PS[:, 1024:1120], in1=bmodT[:], op=ALU.add), r=["ps2", "bmodT"], w=[mk])
            A("dve", lambda e: e.scalar_tensor_tensor(out=AG[:, 0:16], in0=modT[:, 16:32], scalar=1.0, in1=gains[:, 0:16], op0=ALU.add, op1=ALU.mult), r=[mk, "gains"], w=[ak])
            A("dve", lambda e: e.tensor_tensor(out=AG[:, 16:32], in0=modT[:, 32:48], in1=gains[:, 16:32], op=ALU.mult), r=[mk, "gains"], w=[ak])
            A("dve", lambda e: e.scalar_tensor_tensor(out=AG[:, 32:48], in0=modT[:, 64:80], scalar=1.0, in1=gains[:, 32:48], op0=ALU.add, op1=ALU.mult), r=[mk, "gains"], w=[ak])
            A("dve", lambda e: e.tensor_tensor(out=AG[:, 48:64], in0=modT[:, 80:96], in1=gains[:, 48:64], op=ALU.mult), r=[mk, "gains"], w=[ak])
            yield

        def phase_mod(l):
            S.mark("phase_mod")
            for _ in gen_mod(l):
                pass
            S.barrier()

        def phase_norm(src_x, has_y, Gap, Aap, Bap, dst_x, make_h, gpar=0, apar=0, other=None):
            S.mark("phase_norm")
            g = gen_norm(src_x, has_y, Gap, Aap, Bap, dst_x, make_h, gpar, apar)
            done_o = other is None
            k = 0
            for _ in g:
                k += 1
                if not done_o and k % 2 == 0:
                    try:
                        next(other)
                    except StopIteration:
                        done_o = True
            if not done_o:
                for _ in other:
                    pass
            S.barrier()

        def gen_norm(src_x, has_y, Gap, Aap, Bap, dst_x, make_h, gpar, apar):
            reset_arenas()
            gkey = "AG%d" % gpar
            akey = "AG%d" % apar
            mkey = "modT%d" % apar
            hT = aR1.bf(32768).rearrange("p (k t) -> p k t", k=16)
            XN = aBR.f32(8192).rearrange("p (k t) -> p k t", k=16)
            NB = 3
            xt = [aT.f32(512) for _ in range(NB)]
            yt = [aT.f32(512) for _ in range(NB)]
            tmp = [aT.f32(512) for _ in range(2)]
            sq = [aT.bf(512) for _ in range(2)]
            rt = aT.f32(512)
            rstd = aT.f32(512)
            steps = [(c, ft) for c in range(NCH) for ft in range(16)]
            loaded = [0]

            def emit_loads(upto):
                while loaded[0] < min(upto, len(steps)):
                    i = loaded[0]
                    c, ft = steps[i]
                    bb = i % NB
                    rows = slice(128 * ft, 128 * (ft + 1))
                    cs = slice(TC * c, TC * (c + 1))
                    dma_sp(xt[bb], src_x[rows, cs], w=["xt%d" % bb])
                    dma_sp(yt[bb], YT[rows, cs], w=["yt%d" % bb])
                    loaded[0] += 1

            for i, (c, ft) in enumerate(steps):
                cs = slice(TC * c, TC * (c + 1))
                pb = c % 2
                b = ft % 2
                rows = slice(128 * ft, 128 * (ft + 1))
                kx = "XN%d" % ft
                if has_y:
                    emit_loads(i + NB)
                    bb = i % NB
                    A("dve", lambda e, b=b, bb=bb, cs=cs: e.tensor_tensor(out=tmp[b], in0=yt[bb], in1=RSTDY[:, cs], op=ALU.mult), r=["yt%d" % bb, "RSTDY"], w=["tmp%d" % b])
                    A("dve", lambda e, b=b, bb=bb, ft=ft: e.scalar_tensor_tensor(out=XN[:, ft, :], in0=tmp[b], scalar=Gap[:, ft:ft + 1], in1=xt[bb], op0=ALU.mult, op1=ALU.add),
                      r=["tmp%d" % b, "xt%d" % bb, gkey], w=[kx])
                else:
                    dma_sp(XN[:, ft, :], src_x[rows, cs], w=[kx])
                if dst_x is not None:
                    dma_pool(dst_x[rows, cs], XN[:, ft, :], r=[kx])
                if make_h:
                    A("act", lambda e, b=b, ft=ft: e.activation(out=sq[b], in_=XN[:, ft, :], func=AF.Square), r=[kx], w=["sq%d" % b])
                    A("pe", lambda e, b=b, ft=ft, pb=pb: e.matmul(psb(pb), onesbf[:], sq[b], start=(ft == 0), stop=(ft == 15)), r=["sq%d" % b, "onesbf"], w=pk(pb))
                if make_h and ft == 15:
                    A("act", lambda e, pb=pb: e.activation(out=rt, in_=psb(pb), func=AF.Sqrt, bias=epsT[:, 0:1], scale=1.0 / D), r=pk(pb) + ["epsT"], w=["rt"])
                    A("dve", lambda e: e.reciprocal(out=rstd, in_=rt), r=["rt"], w=["rstd"])
                    for f2 in range(16):
                        b2 = f2 % 2
                        A("dve", lambda e, b2=b2, f2=f2: e.tensor_tensor(out=tmp[b2], in0=XN[:, f2, :], in1=rstd, op=ALU.mult), r=["XN%d" % f2, "rstd"], w=["tmp%d" % b2])
                        A("act", lambda e, b2=b2, f2=f2, cs=cs: e.activation(out=hT[:, f2, cs], in_=tmp[b2], func=AF.Identity, bias=Bap[:, f2:f2 + 1], scale=Aap[:, f2:f2 + 1]),
                          r=["tmp%d" % b2, akey, mkey], w=["hT%d" % f2])
                yield

        half_ctr = [0]

        def gen_inproj(l, uT):
            hT = R1[:, :].rearrange("p (k t) -> p k t", k=16)
            stg = [aT.bf(2048) for _ in range(2)]
            vst = [aT.bf(512) for _ in range(2)]
            wv = w_in[l].rearrange("(kt p) n -> p kt n", p=128)
            nb_ = 0
            ev = 0
            for cg in range(25):
                slot = cg % 3
                wk = "WS%d" % slot
                W = load_w(slot, wv[:, :, 512 * cg:512 * (cg + 1)], 16, 512, wk)
                if 7 <= cg <= 9:
                    for tt in range(16):
                        bank = nb_ % 2
                        nb_ += 1
                        for kt in range(16):
                            A("pe", lambda e, W=W, kt=kt, tt=tt, bank=bank: e.matmul(psb(bank), hT[:, kt, 128 * tt:128 * (tt + 1)], W[:, kt, :], start=(kt == 0), stop=(kt == 15)),
                              r=[wk, "hT%d" % kt], w=pk(bank))
                        b = tt % 2
                        A("act", lambda e, b=b, bank=bank: e.activation(out=vst[b], in_=psb(bank), func=AF.Copy), r=pk(bank), w=["vst%d" % b])
                        dma_sp(VTOK[128 * tt:128 * (tt + 1), 512 * (cg - 7):512 * (cg - 6)], vst[b], r=["vst%d" % b])
                        if tt % 4 == 3:
                            yield
                    continue
                for j in range(4):
                    pt = 4 * cg + j
                    b = ev % 2
                    ev += 1
                    if cg == 0:
                        dst = uT[:, j, :]
                        sk = "uT%d" % j
                    else:
                        dst = stg[b]
                        sk = "stg%d" % b
                    for c_ in range(NCH):
                        bank = nb_ % 2
                        nb_ += 1
                        for kt in range(16):
                            A("pe", lambda e, W=W, kt=kt, j=j, c_=c_, bank=bank: e.matmul(psb(bank), W[:, kt, 128 * j:128 * (j + 1)], hT[:, kt, TC * c_:TC * (c_ + 1)], start=(kt == 0), stop=(kt == 15)),
                              r=[wk, "hT%d" % kt], w=pk(bank))
                        src = psb(bank)
                        if cg >= 13:
                            gi_ = pt - 52
                            A("act", lambda e, src=src, dst=dst, gi_=gi_, c_=c_: e.activation(out=dst[:, TC * c_:TC * (c_ + 1)], in_=src, func=AF.Sigmoid, bias=bgateT[:, gi_:gi_ + 1]), r=pk(bank) + ["bgateT"], w=[sk])
                        elif cg in (2, 5, 3, 6):
                            dil = 4 if cg in (2, 5) else 16
                            w_ = 512 // dil
                            A("act", lambda e, src=src, dst=dst, dil=dil, w_=w_, c_=c_: e.activation(out=dst.rearrange("p (r m) -> p r m", r=dil)[:, :, w_ * c_:w_ * (c_ + 1)], in_=src.rearrange("p (m r) -> p r m", r=dil), func=AF.Copy), r=pk(bank), w=[sk])
                        else:
                            A("act", lambda e, src=src, dst=dst, c_=c_: e.activation(out=dst[:, TC * c_:TC * (c_ + 1)], in_=src, func=AF.Copy), r=pk(bank), w=[sk])
                    if cg != 0:
                        dma_sp(PROJ[128 * pt:128 * (pt + 1), :], dst, r=[sk], w=["PROJ%d" % pt])
                    yield

        def ssm_prep(l, ctx):
            dma_sp(ssm_sm[:], ssm_sm_in[l], w=["ssm_sm"])
            dma_sp(dTt[:], dT_in[l], w=["dTt"])
            dma_sp(bgluT[:], bgluT_in[l], w=["bgluT"])
            dt_sm = sm_tmp[:, 0:16]
            A("act", lambda e: e.activation(out=dt_sm, in_=ssm_sm[:, 0:16], func=AF.Exp), r=["ssm_sm"], w=["dt_sm"])
            A("dve", lambda e: e.tensor_tensor(out=sm_tmp[:, 16:32], in0=ssm_sm[:, 16:32], in1=dt_sm, op=ALU.mult), r=["dt_sm", "ssm_sm"], w=["ardt"])
            A("act", lambda e: e.activation(out=sm_r[:], in_=sm_tmp[:, 16:32], func=AF.Exp), r=["ardt"], w=["sm_r"])
            A("dve", lambda e: e.tensor_tensor(out=sm_th[:], in0=ssm_sm[:, 32:48], in1=dt_sm, op=ALU.mult), r=["dt_sm", "ssm_sm"], w=["sm_th"])
            A("dve", lambda e: e.tensor_scalar(out=sm_tmp[:, 32:48], in0=sm_th[:], scalar1=float(TC), scalar2=None, op0=ALU.mult), r=["sm_th"], w=["smTx"])
            sincos(sm_tmp[:, 32:48], 16, sm_sT[:], sm_cT[:], sm_tmp[:, 48:64], sm_tmp[:, 64:80], "smT")
            BbrT = aBR.bf(2048).rearrange("p (s n) -> p s n", s=16)
            BbiT = aBR.bf(2048).rearrange("p (s n) -> p s n", s=16)
            CrT = aBR.bf(2048).rearrange("p (s n) -> p s n", s=16)
            CiT = aT.bf(2048).rearrange("p (s n) -> p s n", s=16)
            Wglu = aT.bf(2048).rearrange("p (k n) -> p k n", k=4)
            ctx.update(BbrT=BbrT, BbiT=BbiT, CrT=CrT, CiT=CiT, Wglu=Wglu)
            dma_pool(CrT.rearrange("p s n -> p (s n)"), CTre_in[l], w=["CrT"])
            dma_pool(CiT.rearrange("p s n -> p (s n)"), CTim_in[l], w=["CiT"])
            dma_pool(Wglu, w_glu[l].rearrange("(k p) n -> p k n", p=128), w=["Wglu"])
            mark_br = aBR.off
            fl = [aBR.f32(512) for _ in range(14)]
            (f_ld, f_ar, f_ai, f_dt, f_mag, f_th, f_sin, f_cos, f_k, f_p, f_a, f_b, f_fr, f_fi) = fl
            f_bre = aBR.f32(512)
            f_bim = aBR.f32(512)
            for q in range(4):
                qs = slice(512 * q, 512 * (q + 1))
                dma_sp(f_ld, ssm_flat_in[l, 0:1, qs].to_broadcast([128, 512]), w=["f_ld"])
                dma_sp(f_ar, ssm_flat_in[l, 1:2, qs].to_broadcast([128, 512]), w=["f_ar"])
                dma_sp(f_ai, ssm_flat_in[l, 2:3, qs].to_broadcast([128, 512]), w=["f_ai"])
                dma_sp(f_bre, BTre_in[l, :, qs], w=["f_bre"])
                dma_sp(f_bim, BTim_in[l, :, qs], w=["f_bim"])
                A("act", lambda e: e.activation(out=f_dt, in_=f_ld, func=AF.Exp), r=["f_ld"], w=["f_dt"])
                A("dve", lambda e: e.tensor_tensor(out=f_a, in0=f_ar, in1=f_dt, op=ALU.mult), r=["f_ar", "f_dt"], w=["f_a"])
                A("act", lambda e: e.activation(out=f_mag, in_=f_a, func=AF.Exp), r=["f_a"], w=["f_mag"])
                A("dve", lambda e: e.tensor_tensor(out=f_th, in0=f_ai, in1=f_dt, op=ALU.mult), r=["f_ai", "f_dt"], w=["flx"])
                sincos(f_th, 512, f_sin, f_cos, f_k, f_p, "fl")
                A("dve", lambda e: e.tensor_tensor(out=f_cos, in0=f_cos, in1=f_mag, op=ALU.mult), r=["flc", "f_mag"], w=["flc"])
                A("dve", lambda e: e.tensor_tensor(out=f_sin, in0=f_sin, in1=f_mag, op=ALU.mult), r=["fls", "f_mag"], w=["fls"])
                A("dve", lambda e: e.tensor_scalar(out=f_cos, in0=f_cos, scalar1=-1.0, scalar2=None, op0=ALU.add), r=["flc"], w=["flc"])
                A("dve", lambda e: e.tensor_tensor(out=f_a, in0=f_ar, in1=f_ar, op=ALU.mult), r=["f_ar"], w=["f_a"])
                A("dve", lambda e: e.tensor_tensor(out=f_b, in0=f_ai, in1=f_ai, op=ALU.mult), r=["f_ai"], w=["f_b"])
                A("dve", lambda e: e.tensor_tensor(out=f_a, in0=f_a, in1=f_b, op=ALU.add), r=["f_a", "f_b"], w=["f_a"])
                A("dve", lambda e: e.reciprocal(out=f_a, in_=f_a), r=["f_a"], w=["f_a"])
                A("dve", lambda e: e.tensor_tensor(out=f_fr, in0=f_cos, in1=f_ar, op=ALU.mult), r=["flc", "f_ar"], w=["f_fr"])
                A("dve", lambda e: e.tensor_tensor(out=f_b, in0=f_sin, in1=f_ai, op=ALU.mult), r=["fls", "f_ai"], w=["f_b"])
                A("dve", lambda e: e.tensor_tensor(out=f_fr, in0=f_fr, in1=f_b, op=ALU.add), r=["f_fr", "f_b"], w=["f_fr"])
                A("dve", lambda e: e.tensor_tensor(out=f_fr, in0=f_fr, in1=f_a, op=ALU.mult), r=["f_fr", "f_a"], w=["f_fr"])
                A("dve", lambda e: e.tensor_tensor(out=f_fi, in0=f_sin, in1=f_ar, op=ALU.mult), r=["fls", "f_ar"], w=["f_fi"])
                A("dve", lambda e: e.tensor_tensor(out=f_b, in0=f_cos, in1=f_ai, op=ALU.mult), r=["flc", "f_ai"], w=["f_b"])
                A("dve", lambda e: e.tensor_tensor(out=f_fi, in0=f_fi, in1=f_b, op=ALU.subtract), r=["f_fi", "f_b"], w=["f_fi"])
                A("dve", lambda e: e.tensor_tensor(out=f_fi, in0=f_fi, in1=f_a, op=ALU.mult), r=["f_fi", "f_a"], w=["f_fi"])
                brv = BbrT.rearrange("p s n -> p (s n)")[:, qs]
                biv = BbiT.rearrange("p s n -> p (s n)")[:, qs]
                A("dve", lambda e: e.tensor_tensor(out=f_k, in0=f_fr, in1=f_bre, op=ALU.mult), r=["f_fr", "f_bre"], w=["flk"])
                A("dve", lambda e: e.tensor_tensor(out=f_p, in0=f_fi, in1=f_bim, op=ALU.mult), r=["f_fi", "f_bim"], w=["flp"])
                A("dve", lambda e, brv=brv: e.tensor_tensor(out=brv, in0=f_k, in1=f_p, op=ALU.subtract), r=["flk", "flp"], w=["BbrT"])
                A("dve", lambda e: e.tensor_tensor(out=f_k, in0=f_fr, in1=f_bim, op=ALU.mult), r=["f_fr", "f_bim"], w=["flk"])
                A("dve", lambda e: e.tensor_tensor(out=f_p, in0=f_fi, in1=f_bre, op=ALU.mult), r=["f_fi", "f_bre"], w=["flp"])
                A("dve", lambda e, biv=biv: e.tensor_tensor(out=biv, in0=f_k, in1=f_p, op=ALU.add), r=["flk", "flp"], w=["BbiT"])
            aBR.off = mark_br
            S.barrier()

        def gen_ssm_main(l, ctx):
            BbrT, BbiT, CrT, CiT = ctx["BbrT"], ctx["BbiT"], ctx["CrT"], ctx["CiT"]
            uT = ctx["uT"]
            GL = uT
            cosT2 = [aBR.f32(512), aBR.f32(512)]
            sinT2 = [aBR.f32(512), aT.f32(512)]
            tk = aBR.f32(512)
            tp = aBR.f32(512)
            xr_s = aBR.f32(512)
            xi_s = aBR.f32(512)
            t1 = aBR.f32(512)
            t2 = aBR.f32(512)
            t3 = aBR.f32(512)
            t4 = aBR.f32(512)
            zr = aBR.f32(512)
            zi = aBR.f32(512)
            gr = aBR.f32(512)
            gi = aBR.f32(512)
            hr = [aBR.bf(512) for _ in range(2)]
            nhi = [aBR.bf(512) for _ in range(2)]
            yv = t1
            y2 = t2
            ctr = 0
            pending = []

            def flush():
                for f in pending:
                    f()
                del pending[:]

            M_ = 12582912.0
            C1_ = 6.28125
            C2_ = float(2 * np.pi - 6.28125)

            def table_dve(st_, shift):
                A("dve", lambda e: e.tensor_scalar(out=tp, in0=iota[:], scalar1=sm_th[:, st_:st_ + 1], scalar2=None, op0=ALU.mult), r=["iota", "sm_th"], w=["tbp"])
                if shift != 0.0:
                    A("dve", lambda e: e.tensor_scalar(out=tp, in0=tp, scalar1=shift, scalar2=None, op0=ALU.add), r=["tbp"], w=["tbp"])
                A("dve", lambda e: e.tensor_scalar(out=tk, in0=tp, scalar1=float(1 / (2 * np.pi)), scalar2=M_, op0=ALU.mult, op1=ALU.add), r=["tbp"], w=["tbk"])
                A("dve", lambda e: e.tensor_scalar(out=tk, in0=tk, scalar1=-M_, scalar2=None, op0=ALU.add), r=["tbk"], w=["tbk"])
                A("dve", lambda e: e.scalar_tensor_tensor(out=tp, in0=tk, scalar=-C1_, in1=tp, op0=ALU.mult, op1=ALU.add), r=["tbk", "tbp"], w=["tbp"])
                A("dve", lambda e: e.scalar_tensor_tensor(out=tp, in0=tk, scalar=-C2_, in1=tp, op0=ALU.mult, op1=ALU.add), r=["tbk", "tbp"], w=["tbp"])
                A("dve", lambda e: e.tensor_scalar(out=tp, in0=tp, scalar1=PI, scalar2=-PI, op0=ALU.min, op1=ALU.max), r=["tbp"], w=["tbp"])

            def table_act(outp, okey):
                A("act", lambda e: e.activation(out=outp, in_=tp, func=AF.Sin), r=["tbp"], w=[okey])

            for stt in range(16):
                ut = stt // 4
                tpar = stt % 2
                cosT = cosT2[tpar]
                sinT = sinT2[tpar]
                kc = "tbc%d" % tpar
                ks = "tbs%d" % tpar
                if stt == 0:
                    table_dve(0, 0.0)
                    table_act(sinT2[0], "tbs0")
                    table_dve(0, PI / 2)
                    table_act(cosT2[0], "tbc0")
                for c in range(NCH):
                    cs = slice(TC * c, TC * (c + 1))
                    b = ctr % 2
                    ctr += 1
                    flush()
                    A("pe", lambda e, stt=stt, ut=ut, cs=cs: e.matmul(psb(6), BbrT[:, stt, :], uT[:, ut, cs], start=True, stop=True), r=["BbrT", "uT%d" % ut], w=pk(6))
                    A("pe", lambda e, stt=stt, ut=ut, cs=cs: e.matmul(psb(7), BbiT[:, stt, :], uT[:, ut, cs], start=True, stop=True), r=["BbiT", "uT%d" % ut], w=pk(7))
                    A("act", lambda e: e.activation(out=xr_s, in_=psb(6), func=AF.Copy), r=pk(6), w=["xr"])
                    A("act", lambda e: e.activation(out=xi_s, in_=psb(7), func=AF.Copy), r=pk(7), w=["xi"])
                    A("dve", lambda e, cosT=cosT, sinT=sinT: e.tensor_tensor(out=t1, in0=xr_s, in1=cosT, op=ALU.mult), r=["xr", kc], w=["t1"])
                    A("pool", lambda e, cosT=cosT, sinT=sinT: e.tensor_tensor(out=t2, in0=xi_s, in1=sinT, op=ALU.mult), r=["xi", ks], w=["t2"])
                    A("dve", lambda e: e.tensor_tensor(out=zr, in0=t1, in1=t2, op=ALU.add), r=["t1", "t2"], w=["zr"])
                    A("pool", lambda e, cosT=cosT, sinT=sinT: e.tensor_tensor(out=t3, in0=xi_s, in1=cosT, op=ALU.mult), r=["xi", kc], w=["t3"])
                    A("dve", lambda e, cosT=cosT, sinT=sinT: e.tensor_tensor(out=t4, in0=xr_s, in1=sinT, op=ALU.mult), r=["xr", ks], w=["t4"])
                    A("pool", lambda e: e.tensor_tensor(out=zi, in0=t3, in1=t4, op=ALU.subtract), r=["t3", "t4"], w=["zi"])
                    rbc = sm_r[:, stt:stt + 1].to_broadcast([128, 512])
                    if c == 0:
                        A("dve", lambda e, rbc=rbc: e.tensor_tensor_scan(out=gr, data0=rbc, data1=zr, initial=0.0, op0=ALU.mult, op1=ALU.add), r=["zr", "sm_r"], w=["gr"])
                        A("dve", lambda e, rbc=rbc: e.tensor_tensor_scan(out=gi, data0=rbc, data1=zi, initial=0.0, op0=ALU.mult, op1=ALU.add), r=["zi", "sm_r"], w=["gi"])
                    else:
                        A("dve", lambda e, rbc=rbc: e.tensor_tensor_scan(out=gr, data0=rbc, data1=zr, initial=carry[:, 0:1], op0=ALU.mult, op1=ALU.add), r=["zr", "sm_r", "carry"], w=["gr"])
                        A("dve", lambda e, rbc=rbc: e.tensor_tensor_scan(out=gi, data0=rbc, data1=zi, initial=carry[:, 1:2], op0=ALU.mult, op1=ALU.add), r=["zi", "sm_r", "carry"], w=["gi"])
                    if c < NCH - 1:
                        cT_ = sm_cT[:, stt:stt + 1]
                        sT_ = sm_sT[:, stt:stt + 1]
                        A("dve", lambda e, cT_=cT_: e.tensor_tensor(out=carry[:, 2:3], in0=gr[:, 511:512], in1=cT_, op=ALU.mult), r=["gr", "smTc"], w=["cy2"])
                        A("dve", lambda e, sT_=sT_: e.tensor_tensor(out=carry[:, 3:4], in0=gi[:, 511:512], in1=sT_, op=ALU.mult), r=["gi", "smTs"], w=["cy3"])
                        A("dve", lambda e, sT_=sT_: e.tensor_tensor(out=carry[:, 4:5], in0=gr[:, 511:512], in1=sT_, op=ALU.mult), r=["gr", "smTs"], w=["cy4"])
                        A("dve", lambda e, cT_=cT_: e.tensor_tensor(out=carry[:, 5:6], in0=gi[:, 511:512], in1=cT_, op=ALU.mult), r=["gi", "smTc"], w=["cy5"])
                        A("dve", lambda e: e.tensor_tensor(out=carry[:, 0:1], in0=carry[:, 2:3], in1=carry[:, 3:4], op=ALU.subtract), r=["cy2", "cy3"], w=["carry"])
                        A("dve", lambda e: e.tensor_tensor(out=carry[:, 1:2], in0=carry[:, 4:5], in1=carry[:, 5:6], op=ALU.add), r=["cy4", "cy5", "carry"], w=["carry"])
                    A("dve", lambda e, cosT=cosT, sinT=sinT: e.tensor_tensor(out=t1, in0=gr, in1=cosT, op=ALU.mult), r=["gr", kc], w=["t1"])
                    A("pool", lambda e, cosT=cosT, sinT=sinT: e.tensor_tensor(out=t2, in0=gi, in1=sinT, op=ALU.mult), r=["gi", ks], w=["t2"])
                    A("dve", lambda e, b=b: e.tensor_tensor(out=hr[b], in0=t1, in1=t2, op=ALU.subtract), r=["t1", "t2"], w=["hr%d" % b])
                    A("pool", lambda e, cosT=cosT, sinT=sinT: e.tensor_tensor(out=t3, in0=gr, in1=sinT, op=ALU.mult), r=["gr", ks], w=["t3"])
                    A("dve", lambda e, cosT=cosT, sinT=sinT: e.tensor_tensor(out=t4, in0=gi, in1=cosT, op=ALU.mult), r=["gi", kc], w=["t4"])
                    A("dve", lambda e, b=b: e.scalar_tensor_tensor(out=nhi[b], in0=t3, scalar=-1.0, in1=t4, op0=ALU.mult, op1=ALU.subtract), r=["t3", "t4"], w=["nhi%d" % b])
                    first = (stt % 4 == 0)
                    last = (stt % 4 == 3)

                    def cmm(stt=stt, b=b, c=c, first=first, last=last):
                        A("pe", lambda e: e.matmul(psb(2 + c), CrT[:, stt, :], hr[b], start=first, stop=False), r=["CrT", "hr%d" % b], w=pk(2 + c))
                        A("pe", lambda e: e.matmul(psb(2 + c), CiT[:, stt, :], nhi[b], start=False, stop=last), r=["CiT", "nhi%d" % b], w=pk(2 + c))
                    pending.append(cmm)
                    if stt < 15:
                        npar = (stt + 1) % 2
                        if c == 0:
                            table_dve(stt + 1, 0.0)
                        elif c == 1:
                            table_act(sinT2[npar], "tbs%d" % npar)
                            table_dve(stt + 1, PI / 2)
                        elif c == 2:
                            table_act(cosT2[npar], "tbc%d" % npar)
                    yield
                if stt % 4 == 3:
                    flush()
                    for c in range(NCH):
                        cs = slice(TC * c, TC * (c + 1))
                        A("dve", lambda e, ut=ut, cs=cs, c=c: e.scalar_tensor_tensor(out=yv, in0=uT[:, ut, cs], scalar=dTt[:, ut:ut + 1], in1=psb(2 + c), op0=ALU.mult, op1=ALU.add), r=["uT%d" % ut, "dTt"] + pk(2 + c), w=["t1"])
                        A("dve", lambda e: e.tensor_tensor(out=y2, in0=yv, in1=yv, op=ALU.mult), r=["t1"], w=["t2"])
                        A("dve", lambda e: e.tensor_scalar(out=y2, in0=y2, scalar1=0.044715, scalar2=1.0, op0=ALU.mult, op1=ALU.add), r=["t2"], w=["t2"])
                        A("dve", lambda e: e.tensor_tensor(out=y2, in0=y2, in1=yv, op=ALU.mult), r=["t2", "t1"], w=["t2"])
                        A("act", lambda e: e.activation(out=y2, in_=y2, func=AF.Sigmoid, scale=float(2.0 * np.sqrt(2.0 / np.pi))), r=["t2"], w=["t2"])
                        A("dve", lambda e, ut=ut, cs=cs: e.tensor_tensor(out=GL[:, ut, cs], in0=y2, in1=yv, op=ALU.mult), r=["t2", "t1"], w=["uT%d" % ut])
                    yield

        def ssm_glu(l, ctx):
            Wglu = ctx["Wglu"]
            GL = ctx["uT"]
            BR = BRa[:, :].rearrange("p (k t) -> p k t", k=12)
            sg = aT.f32(512)
            n = 0
            for fo in range(4):
                for c in range(NCH):
                    cs = slice(TC * c, TC * (c + 1))
                    bank = 4 + (n % 4)
                    n += 1
                    for kt in range(4):
                        A("pe", lambda e, kt=kt, fo=fo, cs=cs, bank=bank: e.matmul(psb(bank), Wglu[:, kt, 128 * fo:128 * (fo + 1)], GL[:, kt, cs], start=(kt == 0), stop=(kt == 3)), r=["Wglu", "uT%d" % kt], w=pk(bank))
                    A("act", lambda e, fo=fo, bank=bank: e.activation(out=sg, in_=psb(bank), func=AF.Sigmoid, bias=bgluT[:, fo:fo + 1]), r=pk(bank) + ["bgluT"], w=["sg"])
                    A("dve", lambda e, fo=fo, cs=cs: e.tensor_tensor(out=BR[:, fo, cs], in0=GL[:, fo, cs], in1=sg, op=ALU.mult), r=["sg", "uT%d" % fo], w=["BR%d" % fo])
            S.barrier()

        def phase_inproj_ssm(l):
            S.mark("phase_inproj")
            reset_arenas()
            ctx = {}
            class _U:
                def __getitem__(self, key):
                    p, k, t = key
                    base = ACC if k < 2 else RSTDY
                    v = base[:, :].bitcast(BF16).rearrange("p (k t) -> p k t", k=2)
                    return v[p, k % 2, t]
            ctx["uT"] = _U()
            gi_ = gen_inproj(l, ctx["uT"])
            for _ in range(4):
                next(gi_)
            ssm_prep(l, ctx)
            S.mark("phase_ssm")
            gs_ = gen_ssm_main(l, ctx)
            done_i = done_s = False
            while not (done_i and done_s):
                if not done_s:
                    try:
                        next(gs_)
                    except StopIteration:
                        done_s = True
                if not done_i:
                    try:
                        next(gi_)
                    except StopIteration:
                        done_i = True
            S.barrier()
            ssm_glu(l, ctx)

        def phase_conv(l):
            S.mark("phase_conv")
            reset_arenas()
            BR = BRa[:, :].rearrange("p (k t) -> p k t", k=12)
            dma_sp(convwT[:], convwT_in[l], w=["convwT"])
            cb = [aR1.bf(2048) for _ in range(2)]
            cc = [aR1.bf(2048) for _ in range(2)]
            ch = [aR1.bf(2048) for _ in range(2)]
            zb = aR1.f32(2064)
            acc = aR1.f32(2048)
            A("dve", lambda e: e.memset(zb[:, 0:16], 0.0), w=["zb"])
            for j in range(4):
                b = j % 2
                dma_sp(cb[b], PROJ[128 * (40 + j):128 * (41 + j), :], w=["cb%d" % b])
                dma_sp(cc[b], PROJ[128 * (44 + j):128 * (45 + j), :], w=["cc%d" % b])
                dma_sp(ch[b], PROJ[128 * (48 + j):128 * (49 + j), :], w=["ch%d" % b])
                A("dve", lambda e, b=b: e.tensor_tensor(out=zb[:, 16:2064], in0=cc[b], in1=ch[b], op=ALU.mult), r=["cc%d" % b, "ch%d" % b], w=["zb"])
                A("dve", lambda e, j=j: e.tensor_scalar(out=acc, in0=zb[:, 16:2064], scalar1=convwT[:, 3 * j:3 * j + 1], scalar2=None, op0=ALU.mult), r=["zb", "convwT"], w=["cacc"])
                A("dve", lambda e, j=j: e.scalar_tensor_tensor(out=acc, in0=zb[:, 15:2063], scalar=convwT[:, 3 * j + 1:3 * j + 2], in1=acc, op0=ALU.mult, op1=ALU.add), r=["zb", "convwT", "cacc"], w=["cacc"])
                A("dve", lambda e, j=j: e.scalar_tensor_tensor(out=acc, in0=zb[:, 14:2062], scalar=convwT[:, 3 * j + 2:3 * j + 3], in1=acc, op0=ALU.mult, op1=ALU.add), r=["zb", "convwT", "cacc"], w=["cacc"])
                A("dve", lambda e, j=j, b=b: e.tensor_tensor(out=BR[:, 8 + j, :], in0=acc, in1=cb[b], op=ALU.mult), r=["cacc", "cb%d" % b], w=["BR%d" % (8 + j)])
            S.barrier()

        def phase_attn(l):
            S.mark("phase_attn")
            reset_arenas()
            BR = BRa[:, :].rearrange("p (k t) -> p k t", k=12)
            bhi = aR1.bf(24 * 256).rearrange("p (h n) -> p h n", h=24)
            blo = aR1.bf(24 * 256).rearrange("p (h n) -> p h n", h=24)
            qT = [aR1.bf(2048) for _ in range(2)]
            kT = [aR1.bf(2048) for _ in range(2)]
            Vb = [aR1.bf(2048).rearrange("p (b f) -> p b f", b=16) for _ in range(2)]
            identb = aR1.bf(128)
            onesv = aR1.bf(64)
            Uacc = aT.f32(2048)
            Lacc = aT.f32(2048)
            pT = [aT.bf(256) for _ in range(2)]
            dma_pool(bhi.rearrange("p h n -> p (h n)"), bias_hi_in[:, :], w=["bhi"])
            dma_pool(blo.rearrange("p h n -> p (h n)"), bias_lo_in[:, :], w=["blo"])
            dma_pool(identb, ident_in[:, :], w=["identb"])
            A("dve", lambda e: e.memset(onesv, 1.0), w=["onesv"])
            dils = [1, 4, 16]
            n = 0
            sctr = 0
            uctr = 0
            for hp in range(4):
                for g in range(3):
                    dil = dils[g]
                    nb = 16 // dil
                    b = n % 2
                    n += 1
                    qt = 4 + 4 * g + hp
                    kt_ = 16 + 4 * g + hp
                    dma_sp(qT[b], PROJ[128 * qt:128 * (qt + 1), :], w=["qT%d" % b])
                    dma_sp(kT[b], PROJ[128 * kt_:128 * (kt_ + 1), :], w=["kT%d" % b])
                    vsrc = VTOK.rearrange("(bb i r) f -> i r bb f", i=128, r=dil)
                    for r in range(dil):
                        dma_sp(Vb[b][:, r * nb:(r + 1) * nb, :], vsrc[:, r, :, 512 * g + 128 * hp:512 * g + 128 * (hp + 1)], w=["Vb%d" % b])
                    for q4 in range(4):
                        ub = 2 + (uctr % 2)
                        lb = 4 + (uctr % 2)
                        uctr += 1
                        for bi4 in range(4):
                            bi = 4 * q4 + bi4
                            bblk = bi % nb
                            for hh in range(2):
                                head = 8 * g + 2 * hp + hh
                                prt = slice(64 * hh, 64 * (hh + 1))
                                sb_ = sctr % 2
                                sctr += 1
                                sps = PS[:, 256 * sb_:256 * (sb_ + 1)]
                                skey = ["pss%d" % sb_]
                                qv = qT[b][prt, 128 * bi:128 * (bi + 1)]
                                if bblk > 0:
                                    A("pe", lambda e, sps=sps, head=head: e.matmul(sps, identb, bhi[:, head, :], start=True, stop=False), r=["identb", "bhi"], w=skey)
                                    A("pe", lambda e, sps=sps, head=head: e.matmul(sps, identb, blo[:, head, :], start=False, stop=False), r=["identb", "blo"], w=skey)
                                    A("pe", lambda e, sps=sps, b=b, prt=prt, bi=bi, qv=qv: e.matmul(sps[:, 0:128], kT[b][prt, 128 * (bi - 1):128 * bi], qv, start=False, stop=False), r=["kT%d" % b, "qT%d" % b], w=skey)
                                    A("pe", lambda e, sps=sps, b=b, prt=prt, bi=bi, qv=qv: e.matmul(sps[:, 128:256], kT[b][prt, 128 * bi:128 * (bi + 1)], qv, start=False, stop=True), r=["kT%d" % b, "qT%d" % b], w=skey)
                                    A("act", lambda e, sps=sps, sb_=sb_: e.activation(out=pT[sb_], in_=sps, func=AF.Exp, scale=0.125), r=skey, w=["pT%d" % sb_])
                                    ucol = slice(128 * bi4, 128 * (bi4 + 1))
                                    A("pe", lambda e, ub=ub, prt=prt, ucol=ucol, b=b, bi=bi, hh=hh, sb_=sb_: e.matmul(psb(ub)[prt, ucol], Vb[b][:, bi - 1, 64 * hh:64 * (hh + 1)], pT[sb_][:, 0:128], start=True, stop=False), r=["Vb%d" % b, "pT%d" % sb_], w=pk(ub))
                                    A("pe", lambda e, ub=ub, prt=prt, ucol=ucol, b=b, bi=bi, hh=hh, sb_=sb_: e.matmul(psb(ub)[prt, ucol], Vb[b][:, bi, 64 * hh:64 * (hh + 1)], pT[sb_][:, 128:256], start=False, stop=True), r=["Vb%d" % b, "pT%d" % sb_], w=pk(ub))
                                    A("pe", lambda e, lb=lb, prt=prt, ucol=ucol, sb_=sb_: e.matmul(psb(lb)[prt, ucol], onesv, pT[sb_][:, 0:128], start=True, stop=False), r=["onesv", "pT%d" % sb_], w=pk(lb))
                                    A("pe", lambda e, lb=lb, prt=prt, ucol=ucol, sb_=sb_: e.matmul(psb(lb)[prt, ucol], onesv, pT[sb_][:, 128:256], start=False, stop=True), r=["onesv", "pT%d" % sb_], w=pk(lb))
                                else:
                                    sc_ = sps[:, 128:256]
                                    A("pe", lambda e, sc_=sc_, head=head: e.matmul(sc_, identb, bhi[:, head, 128:256], start=True, stop=False), r=["identb", "bhi"], w=skey)
                                    A("pe", lambda e, sc_=sc_, head=head: e.matmul(sc_, identb, blo[:, head, 128:256], start=False, stop=False), r=["identb", "blo"], w=skey)
                                    A("pe", lambda e, sc_=sc_, b=b, prt=prt, bi=bi, qv=qv: e.matmul(sc_, kT[b][prt, 128 * bi:128 * (bi + 1)], qv, start=False, stop=True), r=["kT%d" % b, "qT%d" % b], w=skey)
                                    A("act", lambda e, sc_=sc_, sb_=sb_: e.activation(out=pT[sb_][:, 128:256], in_=sc_, func=AF.Exp, scale=0.125), r=skey, w=["pT%d" % sb_])
                                    ucol = slice(128 * bi4, 128 * (bi4 + 1))
                                    A("pe", lambda e, ub=ub, prt=prt, ucol=ucol, b=b, bi=bi, hh=hh, sb_=sb_: e.matmul(psb(ub)[prt, ucol], Vb[b][:, bi, 64 * hh:64 * (hh + 1)], pT[sb_][:, 128:256], start=True, stop=True), r=["Vb%d" % b, "pT%d" % sb_], w=pk(ub))
                                    A("pe", lambda e, lb=lb, prt=prt, ucol=ucol, sb_=sb_: e.matmul(psb(lb)[prt, ucol], onesv, pT[sb_][:, 128:256], start=True, stop=True), r=["onesv", "pT%d" % sb_], w=pk(lb))
                        if dil == 1:
                            uo = Uacc[:, 512 * q4:512 * (q4 + 1)]
                            lo_ = Lacc[:, 512 * q4:512 * (q4 + 1)]
                            ui = psb(ub)
                            li = psb(lb)
                        elif dil == 4:
                            uo = Uacc.rearrange("p (m r) -> p r m", r=4)[:, q4, :]
                            lo_ = Lacc.rearrange("p (m r) -> p r m", r=4)[:, q4, :]
                            ui = psb(ub)
                            li = psb(lb)
                        else:
                            uo = Uacc.rearrange("p (i r) -> p i r", r=16)[:, :, 4 * q4:4 * (q4 + 1)]
                            lo_ = Lacc.rearrange("p (i r) -> p i r", r=16)[:, :, 4 * q4:4 * (q4 + 1)]
                            ui = psb(ub).rearrange("p (rr i) -> p i rr", rr=4)
                            li = psb(lb).rearrange("p (rr i) -> p i rr", rr=4)
                        if g == 0:
                            A("dve", lambda e, uo=uo, ui=ui: e.tensor_copy(out=uo, in_=ui), r=pk(ub), w=["Uacc"])
                            A("dve", lambda e, lo_=lo_, li=li: e.tensor_copy(out=lo_, in_=li), r=pk(lb), w=["Lacc"])
                        else:
                            A("dve", lambda e, uo=uo, ui=ui: e.tensor_tensor(out=uo, in0=uo, in1=ui, op=ALU.add), r=pk(ub) + ["Uacc"], w=["Uacc"])
                            A("dve", lambda e, lo_=lo_, li=li: e.tensor_tensor(out=lo_, in0=lo_, in1=li, op=ALU.add), r=pk(lb) + ["Lacc"], w=["Lacc"])
                A("dve", lambda e: e.reciprocal(out=Lacc, in_=Lacc), r=["Lacc"], w=["Lacc"])
                A("dve", lambda e, hp=hp: e.tensor_tensor(out=BR[:, 4 + hp, :], in0=Uacc, in1=Lacc, op=ALU.mult), r=["Uacc", "Lacc"], w=["BR%d" % (4 + hp)])
            S.barrier()

        def phase_merge(l):
            S.mark("phase_merge")
            reset_arenas()
            BR = BRa[:, :].rearrange("p (k t) -> p k t", k=12)
            mT = aR1.bf(32768).rearrange("p (k t) -> p k t", k=16)
            Wb = []
            for br, wd in enumerate((w_ssm_out, w_attn_out, w_conv_out)):
                Wb.append(load_w(br, wd[l].rearrange("(k p) n -> p k n", p=128), 4, 2048, "WS%d" % br))
            gt = [[aT.bf(512) for _ in range(3)] for _ in range(2)]
            m0 = aT.f32(512)
            m1 = aT.f32(512)
            m2 = aT.f32(512)
            n = 0
            for fo in range(16):
                for c in range(NCH):
                    cs = slice(TC * c, TC * (c + 1))
                    b = n % 2
                    n += 1
                    pb = 3 * b
                    for br in range(3):
                        gtile = 52 + 16 * br + fo
                        dma_sp(gt[b][br], PROJ[128 * gtile:128 * (gtile + 1), cs], w=["gt%d%d" % (b, br)])
                        for kt in range(4):
                            A("pe", lambda e, br=br, kt=kt, fo=fo, cs=cs, pb=pb: e.matmul(psb(pb + br), Wb[br][:, kt, 128 * fo:128 * (fo + 1)], BR[:, 4 * br + kt, cs], start=(kt == 0), stop=(kt == 3)),
                              r=["WS%d" % br, "BR%d" % (4 * br + kt)], w=pk(pb + br))
                    A("dve", lambda e, b=b, pb=pb: e.tensor_tensor(out=m0, in0=psb(pb), in1=gt[b][0], op=ALU.mult), r=pk(pb) + ["gt%d0" % b], w=["m0"])
                    A("dve", lambda e, b=b, pb=pb: e.tensor_tensor(out=m1, in0=psb(pb + 1), in1=gt[b][1], op=ALU.mult), r=pk(pb + 1) + ["gt%d1" % b], w=["m1"])
                    A("dve", lambda e, b=b, pb=pb: e.tensor_tensor(out=m2, in0=psb(pb + 2), in1=gt[b][2], op=ALU.mult), r=pk(pb + 2) + ["gt%d2" % b], w=["m2"])
                    A("pool", lambda e: e.tensor_tensor(out=m0, in0=m0, in1=m1, op=ALU.add), r=["m0", "m1"], w=["m0"])
                    A("pool", lambda e, fo=fo, cs=cs: e.tensor_tensor(out=mT[:, fo, cs], in0=m0, in1=m2, op=ALU.add), r=["m0", "m2"], w=["mT%d" % fo])
            S.barrier()
            aT.reset()
            aBR.reset()
            yst = [aBR.f32(2048) for _ in range(2)]
            sqf = aBR.f32(2048)
            wv = w_o[l].rearrange("(kt p) n -> p kt n", p=128)
            mkeys = ["mT%d" % k for k in range(16)]
            n = 0
            for cg in range(4):
                slot = cg % 3
                wk = "WS%d" % slot
                W = load_w(slot, wv[:, :, 512 * cg:512 * (cg + 1)], 16, 512, wk)
                for j in range(4):
                    fo = 4 * cg + j
                    hb = 4 * (half_ctr[0] % 2)
                    half_ctr[0] += 1
                    for c in range(NCH):
                        for kt in range(16):
                            A("pe", lambda e, W=W, kt=kt, j=j, c=c, hb=hb: e.matmul(psb(hb + c), W[:, kt, 128 * j:128 * (j + 1)], mT[:, kt, TC * c:TC * (c + 1)], start=(kt == 0), stop=(kt == 15)),
                              r=[wk, "mT%d" % kt], w=pk(hb + c))
                    b = n % 2
                    n += 1
                    A("act", lambda e, b=b, hb=hb: e.activation(out=yst[b], in_=psb(hb, 4), func=AF.Copy), r=pk(hb, 4), w=["yst%d" % b])
                    dma_sp(YT[128 * fo:128 * (fo + 1), :], yst[b], r=["yst%d" % b])
                    if fo == 0:
                        A("act", lambda e, hb=hb: e.activation(out=ACC[:], in_=psb(hb, 4), func=AF.Square), r=pk(hb, 4), w=["ACC"])
                    else:
                        A("act", lambda e, hb=hb: e.activation(out=sqf, in_=psb(hb, 4), func=AF.Square), r=pk(hb, 4), w=["sqf"])
                        A("pool", lambda e: e.tensor_tensor(out=ACC[:], in0=ACC[:], in1=sqf, op=ALU.add), r=["sqf", "ACC"], w=["ACC"])
            finish_rstd()
            S.barrier()

        def finish_rstd():
            for c in range(NCH):
                cs = slice(TC * c, TC * (c + 1))
                A("pe", lambda e, c=c, cs=cs: e.matmul(psb(c), ones32[:], ACC[:, cs], start=True, stop=True), r=["ones32", "ACC"], w=pk(c))
            A("act", lambda e: e.activation(out=RSTDY[:], in_=psb(0, 4), func=AF.Sqrt, bias=epsT[:, 0:1], scale=1.0 / D), r=pk(0, 4) + ["epsT"], w=["RSTDY"])
            A("dve", lambda e: e.reciprocal(out=RSTDY[:], in_=RSTDY[:]), r=["RSTDY"], w=["RSTDY"])

        def phase_ffn(l):
            S.mark("phase_ffn")
            reset_arenas()
            hT = aR1.bf(32768).rearrange("p (k t) -> p k t", k=16)
            dma_sp(ffnwT[:], ffnwT_in[l], w=["ffnwT"])
            abuf = [aBR.f32(2064) for _ in range(2)]
            bbuf = [aBR.f32(2064) for _ in range(2)]
            cb = aBR.f32(2048)
            ca = aT.f32(2048)
            sa = aT.f32(2048)
            gst = [aT.bf(2048) for _ in range(2)]
            for b in range(2):
                A("dve", lambda e, b=b: e.memset(abuf[b][:, 0:16], 0.0), w=["abuf%d" % b])
                A("dve", lambda e, b=b: e.memset(bbuf[b][:, 0:16], 0.0), w=["bbuf%d" % b])
            wv = w_up[l].rearrange("(kt p) n -> p kt n", p=128)
            n = 0
            hn = 0

            def conv3(dst, src, w0, dkey, skey):
                A("dve", lambda e: e.tensor_scalar(out=dst, in0=src[:, 16:2064], scalar1=ffnwT[:, w0:w0 + 1], scalar2=None, op0=ALU.mult), r=[skey, "ffnwT"], w=[dkey])
                A("dve", lambda e: e.scalar_tensor_tensor(out=dst, in0=src[:, 15:2063], scalar=ffnwT[:, w0 + 1:w0 + 2], in1=dst, op0=ALU.mult, op1=ALU.add), r=[skey, "ffnwT", dkey], w=[dkey])
                A("dve", lambda e: e.scalar_tensor_tensor(out=dst, in0=src[:, 14:2062], scalar=ffnwT[:, w0 + 2:w0 + 3], in1=dst, op0=ALU.mult, op1=ALU.add), r=[skey, "ffnwT", dkey], w=[dkey])

            for pg in range(11):
                sla = (2 * pg) % 3
                slb = (2 * pg + 1) % 3
                Wa = load_w(sla, wv[:, :, 512 * pg:512 * (pg + 1)], 16, 512, "WS%d" % sla)
                Wb_ = load_w(slb, wv[:, :, DFF + 512 * pg:DFF + 512 * (pg + 1)], 16, 512, "WS%d" % slb)
                for j in range(4):
                    fa = 4 * pg + j
                    b = n % 2
                    n += 1
                    for th in range(2):
                        hb = 4 * (hn % 2)
                        hn += 1
                        for (W, wk, boff) in ((Wa, "WS%d" % sla, 0), (Wb_, "WS%d" % slb, 2)):
                            for kt in range(16):
                                for c2 in range(2):
                                    tok = slice(1024 * th + 512 * c2, 1024 * th + 512 * (c2 + 1))
                                    A("pe", lambda e, W=W, kt=kt, j=j, tok=tok, bank=hb + boff + c2: e.matmul(psb(bank), W[:, kt, 128 * j:128 * (j + 1)], hT[:, kt, tok], start=(kt == 0), stop=(kt == 15)),
                                      r=[wk, "hT%d" % kt], w=pk(hb + boff + c2))
                        A("act", lambda e, b=b, th=th, hb=hb: e.activation(out=abuf[b][:, 16 + 1024 * th:16 + 1024 * (th + 1)], in_=psb(hb, 2), func=AF.Copy), r=pk(hb, 2), w=["abuf%d" % b])
                        A("act", lambda e, b=b, th=th, hb=hb: e.activation(out=bbuf[b][:, 16 + 1024 * th:16 + 1024 * (th + 1)], in_=psb(hb + 2, 2), func=AF.Copy), r=pk(hb + 2, 2), w=["bbuf%d" % b])
                    conv3(ca, abuf[b], 3 * fa, "ca", "abuf%d" % b)
                    A("act", lambda e: e.activation(out=sa, in_=ca, func=AF.Silu), r=["ca"], w=["sa"])
                    conv3(cb, bbuf[b], 3 * (44 + fa), "cb", "bbuf%d" % b)
                    A("dve", lambda e, b=b: e.tensor_tensor(out=gst[b], in0=sa, in1=cb, op=ALU.mult), r=["sa", "cb"], w=["gst%d" % b])
                    dma_sp(GS[128 * fa:128 * (fa + 1), :], gst[b], r=["gst%d" % b], w=["GS%d" % fa])
            S.barrier()
            reset_arenas()
            g1 = aR1.bf(32768).rearrange("p (k t) -> p k t", k=32)
            g2 = aBR.bf(12288).rearrange("p (k t) -> p k t", k=12)
            yst = [aBR.f32(1024) for _ in range(2)]
            sqf = [aT.f32(1024) for _ in range(2)]
            gv = GS.rearrange("(k p) t -> p k t", p=128)
            wdv = w_down[l].rearrange("(k p) n -> p k n", p=128)
            WD = [WSall[:, 11264 * i:11264 * (i + 1)].rearrange("p (k n) -> p k n", k=22) for i in range(2)]
            n = 0
            nl_ = 0
            for th in range(2):
                ts_ = slice(1024 * th, 1024 * (th + 1))
                for q in range(4):
                    dma_sp(g1[:, 8 * q:8 * (q + 1), :], gv[:, 8 * q:8 * (q + 1), ts_], r=["GS%d" % i for i in range(8 * q, 8 * q + 8)], w=["g1_%d" % q])
                dma_sp(g2[:, 0:6, :], gv[:, 32:38, ts_], r=["GS%d" % i for i in range(32, 38)], w=["g2_0"])
                dma_sp(g2[:, 6:12, :], gv[:, 38:44, ts_], r=["GS%d" % i for i in range(38, 44)], w=["g2_1"])

                def gsrc(kt, c2):
                    if kt < 32:
                        return g1[:, kt, 512 * c2:512 * (c2 + 1)], "g1_%d" % (kt // 8)
                    return g2[:, kt - 32, 512 * c2:512 * (c2 + 1)], "g2_%d" % ((kt - 32) // 6)

                for cg in range(4):
                    for kh in range(2):
                        slot = nl_ % 2
                        nl_ += 1
                        wk = "WD%d" % slot
                        dma_pool(WD[slot], wdv[:, 22 * kh:22 * (kh + 1), 512 * cg:512 * (cg + 1)], w=[wk])
                        for j in range(4):
                            fo = 4 * cg + j
                            bank = 2 * j
                            for k2 in range(22):
                                kt = 22 * kh + k2
                                for c2 in range(2):
                                    gs_, gk = gsrc(kt, c2)
                                    A("pe", lambda e, slot=slot, k2=k2, j=j, gs_=gs_, bank=bank + c2, st_=(kt == 0), sp_=(kt == 43): e.matmul(psb(bank), WD[slot][:, k2, 128 * j:128 * (j + 1)], gs_, start=st_, stop=sp_), r=[wk, gk], w=pk(bank + c2))
                            if kh == 1:
                                b = n % 2
                                n += 1
                                A("act", lambda e, b=b, bank=bank: e.activation(out=yst[b], in_=psb(bank, 2), func=AF.Copy), r=pk(bank, 2), w=["yst%d" % b])
                                dma_sp(YT[128 * fo:128 * (fo + 1), ts_], yst[b], r=["yst%d" % b])
                                if fo == 0:
                                    A("act", lambda e, bank=bank, ts_=ts_: e.activation(out=ACC[:, ts_], in_=psb(bank, 2), func=AF.Square), r=pk(bank, 2), w=["ACC"])
                                else:
                                    A("act", lambda e, b=b, bank=bank: e.activation(out=sqf[b], in_=psb(bank, 2), func=AF.Square), r=pk(bank, 2), w=["sqf%d" % b])
                                    A("pool", lambda e, b=b, ts_=ts_: e.tensor_tensor(out=ACC[:, ts_], in0=ACC[:, ts_], in1=sqf[b], op=ALU.add), r=["sqf%d" % b, "ACC"], w=["ACC"])
            S.barrier()
            finish_rstd()
            S.barrier()

        ident_in = din("ident", [128, 128])

        for l in range(n_layers):
            p = l % 2
            q = (l + 1) % 2
            if l == 0:
                phase_mod(0)
                phase_norm(xT_in, False, None, AG2[0][:, 0:16], modT2[0][:, 0:16], XT, True, 0, 0)
            phase_inproj_ssm(l)
            phase_conv(l)
            phase_attn(l)
            phase_merge(l)
            nxt = gen_mod(l + 1) if l + 1 < n_layers else None
            phase_norm(XT, True, AG2[p][:, 16:32], AG2[p][:, 32:48], modT2[p][:, 48:64], XT, True, p, p, nxt)
            phase_ffn(l)
            if l + 1 < n_layers:
                phase_norm(XT, True, AG2[p][:, 48:64], AG2[q][:, 0:16], modT2[q][:, 0:16], XT, True, p, q)
            else:
                phase_norm(XT, True, AG2[p][:, 48:64], None, None, outT, False, p, p)

        sems = {e: st.enter_context(nc.semaphore("s_" + e)) for e in ENGS}
        dma_sems = {e: [st.enter_context(nc.semaphore("d_%s%d" % (e, i))) for i in range(DMA_K)] for e in ("sp", "pool")}
        S.mark("end")
        build_program.marks = S.marks
        S.prepare(sems, dma_sems)
        block = st.enter_context(nc.Block())

        @block.tensor
        def _(e):
            S.run("pe", e)

        @block.scalar
        def _(e):
            S.run("act", e)

        @block.vector
        def _(e):
            S.run("dve", e)

        @block.gpsimd
        def _(e):
            S.run("pool", e)

        @block.sync
        def _(e):
            S.run("sp", e)
    return nc


def prep_shared(inp):
    f = lambda a: np.ascontiguousarray(np.asarray(a, dtype=np.float32))
    sh = {}
    for k in ("w_mod", "w_in", "w_glu", "w_ssm_out", "w_attn_out", "w_conv_out", "w_o", "w_up", "w_down"):
        sh[k] = f(inp[k])
    sh["bmodT"] = f(np.asarray(inp["b_mod"]).reshape(NL, 96, 128).transpose(0, 2, 1))
    g = np.stack([np.asarray(inp[k]).reshape(NL, 16, 128).transpose(0, 2, 1) for k in ("g_pre_mix", "g_post_mix", "g_pre_ffn", "g_post_ffn")], axis=2)
    sh["gains"] = f(g.reshape(NL, 128, 64))
    sh["bgateT"] = f(np.asarray(inp["b_gate"]).reshape(NL, 48, 128).transpose(0, 2, 1))
    ld = np.repeat(np.asarray(inp["ssm_log_dt"]), 64, axis=1)
    ar = np.asarray(inp["ssm_a_re"]).reshape(NL, 2048)
    ai = np.asarray(inp["ssm_a_im"]).reshape(NL, 2048)
    sh["ssm_flat"] = f(np.stack([ld, ar, ai], axis=1))
    sm = np.stack([a.reshape(NL, 16, 128).transpose(0, 2, 1) for a in (ld, ar, ai)], axis=2)
    sh["ssm_sm"] = f(sm.reshape(NL, 128, 48))
    bre = np.asarray(inp["ssm_b_re"])
    bim = np.asarray(inp["ssm_b_im"])
    cre = np.asarray(inp["ssm_c_re"])
    cim = np.asarray(inp["ssm_c_im"])
    BTre = np.zeros((NL, 128, 16, 128), np.float32)
    BTim = np.zeros((NL, 128, 16, 128), np.float32)
    CTre = np.zeros((NL, 128, 16, 128), np.float32)
    CTim = np.zeros((NL, 128, 16, 128), np.float32)
    for g_ in range(32):
        cs = slice(16 * (g_ % 8), 16 * (g_ % 8) + 16)
        ss = slice(64 * (g_ % 2), 64 * (g_ % 2) + 64)
        BTre[:, cs, g_ // 2, ss] = bre[:, g_].transpose(0, 2, 1)
        BTim[:, cs, g_ // 2, ss] = bim[:, g_].transpose(0, 2, 1)
        CTre[:, ss, g_ // 2, cs] = cre[:, g_].transpose(0, 2, 1)
        CTim[:, ss, g_ // 2, cs] = cim[:, g_].transpose(0, 2, 1)
    sh["BTre"] = BTre.reshape(NL, 128, 2048)
    sh["BTim"] = BTim.reshape(NL, 128, 2048)
    sh["CTre"] = CTre.reshape(NL, 128, 2048)
    sh["CTim"] = CTim.reshape(NL, 128, 2048)
    sh["dT"] = f(np.asarray(inp["ssm_d"]).reshape(NL, 4, 128).transpose(0, 2, 1))
    sh["bgluT"] = f(np.asarray(inp["b_glu"]).reshape(NL, 4, 128).transpose(0, 2, 1))
    sh["convwT"] = f(np.asarray(inp["conv_mix_w"]).reshape(NL, 3, 4, 128).transpose(0, 3, 2, 1).reshape(NL, 128, 12))
    sh["ffnwT"] = f(np.asarray(inp["ffn_conv_w"]).reshape(NL, 3, 88, 128).transpose(0, 3, 2, 1).reshape(NL, 128, 264))
    sh["iota"] = f(np.tile(np.arange(512, dtype=np.float32)[None, :], (128, 1)))
    sh["ones"] = np.ones((128, 128), np.float32)
    sh["ident"] = np.eye(128, dtype=np.float32)
    hi, lo = make_bias_tables()
    sh["bias_hi"] = f(hi)
    sh["bias_lo"] = f(lo)
    return sh


def kernel(**inputs):
    n_layers = int(os.environ.get("K_NLAYERS", NL))
    debug = bool(int(os.environ.get("K_DEBUG", "0")))
    ncores = int(os.environ.get("K_NCORES", 8))
    sh = prep_shared(inputs)
    x = np.asarray(inputs["x"], dtype=np.float32)
    c = np.asarray(inputs["c"], dtype=np.float32)
    in_maps = []
    for b in range(ncores):
        m = dict(sh)
        m["xT"] = np.ascontiguousarray(x[b].T)
        m["cT"] = np.ascontiguousarray(c[b].reshape(16, 128).T)
        in_maps.append(m)
    nc = build_program(n_layers=n_layers, debug=debug)
    res = run_bass_kernel_spmd(nc, in_maps, core_ids=list(range(ncores)))
    if debug:
        kernel.last_results = res.results
    out = np.stack([np.ascontiguousarray(r["outT"].T) for r in res.results], axis=0)
    return out.astype(np.float32)
```

```python
import os
import numpy as np
import concourse.bass as bass
import concourse.mybir as mybir
from concourse.bass_utils import run_bass_kernel_spmd
from contextlib import ExitStack

F32 = mybir.dt.float32
BF16 = mybir.dt.bfloat16
AF = mybir.ActivationFunctionType
ALU = mybir.AluOpType

D = 2048
L = 2048
KT = 16
NL = 4
N_IN = 12800
DFF = 5632
NCH = 4
TC = 512
PI = float(np.pi)
USE_ACT_TABLES = 0

ENGS = ("pe", "act", "dve", "pool", "sp")
DMA_K = 8


class Op:
    __slots__ = ("eng", "fn", "deps", "is_dma", "signal", "sigval", "dsem", "dval", "dprev")

    def __init__(self, eng, fn, is_dma):
        self.eng = eng
        self.fn = fn
        self.is_dma = is_dma
        self.deps = ()
        self.signal = False
        self.sigval = 0
        self.dsem = None
        self.dval = 0
        self.dprev = None


class Sched:
    def __init__(self, same_engine_sync=True):
        self.ops = []
        self.lw = {}
        self.rd = {}
        self.same = same_engine_sync
        self.last_on = {e: None for e in ENGS}
        self.last_dmas = {e: [] for e in ENGS}
        self.barrier_deps = ()
        self._fresh = {e: False for e in ENGS}
        self.marks = []
        self.npe = 0

    def mark(self, name):
        self.marks.append((name, self.npe))

    def add(self, eng, fn, reads=(), writes=(), dma=False):
        op = Op(eng, fn, dma)
        if eng == "pe":
            self.npe += 1
        deps = set(self.barrier_deps) if self._fresh[eng] else set()
        self._fresh[eng] = False
        lw, rd = self.lw, self.rd
        for k in reads:
            w = lw.get(k)
            if w is not None:
                deps.add(w)
        for k in writes:
            w = lw.get(k)
            if w is not None:
                deps.add(w)
            r = rd.get(k)
            if r:
                deps.update(r)
        for k in reads:
            lst = rd.get(k)
            if lst is None:
                rd[k] = [op]
            elif (not dma) and lst and lst[-1].eng == eng and not lst[-1].is_dma:
                lst[-1] = op
            else:
                lst.append(op)
        for k in writes:
            lw[k] = op
            rd[k] = []
        deps.discard(op)
        op.deps = deps
        self.ops.append(op)
        if dma:
            ld = self.last_dmas[eng]
            ld.append(op)
            if len(ld) > DMA_K:
                ld.pop(0)
        else:
            self.last_on[eng] = op
        return op

    def barrier(self):
        deps = [o for o in self.last_on.values() if o is not None]
        for e in ENGS:
            deps.extend(self.last_dmas[e])
        self.barrier_deps = tuple(deps)
        self._fresh = {e: True for e in ENGS}
        self.lw = {}
        self.rd = {}

    def prepare(self, sems, dma_sems):
        ops = self.ops
        same = self.same
        for op in ops:
            for d in op.deps:
                if d.is_dma:
                    continue
                if d.eng == op.eng and not op.is_dma and (d.eng == "pe" or not same):
                    continue
                d.signal = True
        cnt = {e: 0 for e in ENGS}
        dcnt = {e: 0 for e in ENGS}
        hist = {e: [] for e in ENGS}
        for op in ops:
            if op.is_dma:
                n = dcnt[op.eng]
                dcnt[op.eng] = n + 1
                op.dsem = dma_sems[op.eng][n % DMA_K]
                op.dval = 16 * (n // DMA_K + 1)
                h = hist[op.eng]
                op.dprev = h[n - DMA_K] if n >= DMA_K else None
                h.append(op)
            elif op.signal:
                cnt[op.eng] += 1
                op.sigval = cnt[op.eng]
        self.by_eng = {e: [] for e in ENGS}
        for op in ops:
            self.by_eng[op.eng].append(op)
        self.sems = sems

    def run(self, eng_name, e):
        waited = {}
        sems = self.sems
        same = self.same
        for op in self.by_eng[eng_name]:
            need = {}
            deps = list(op.deps)
            if op.is_dma and op.dprev is not None:
                deps.append(op.dprev)
            for d in deps:
                if d.is_dma:
                    s, v = d.dsem, d.dval
                else:
                    if d.eng == eng_name and not op.is_dma and (eng_name == "pe" or not same):
                        continue
                    if not d.signal:
                        continue
                    s, v = sems[d.eng], d.sigval
                if waited.get(s, 0) >= v:
                    continue
                if need.get(s, 0) < v:
                    need[s] = v
            for s, v in need.items():
                e.wait_ge(s, v)
                waited[s] = v
            ins = op.fn(e)
            if op.is_dma:
                ins.then_inc(op.dsem, 16)
            elif op.signal:
                ins.then_inc(sems[eng_name], 1)
        last = {}
        for op in self.by_eng[eng_name]:
            if op.is_dma:
                last[op.dsem] = op.dval
        for s, v in last.items():
            e.wait_ge(s, v)


def alibi_slopes(n_heads):
    return np.array([2.0 ** (-8.0 * (h + 1) / n_heads) for h in range(n_heads)], dtype=np.float64)


def _bf16_round(x):
    u = np.asarray(x, np.float32).view(np.uint32).astype(np.uint64)
    r = ((u + 0x7FFF + ((u >> 16) & 1)) & 0xFFFF0000).astype(np.uint32)
    return r.view(np.float32)


def make_bias_tables():
    slopes = alibi_slopes(24)
    dil = [1, 4, 16]
    kj = np.arange(128)[:, None]
    qi = np.arange(128)[None, :]
    tab = np.zeros((128, 24, 256), np.float64)
    NEG = -240000.0
    for h in range(24):
        a = slopes[h] * dil[h // 8] * 8.0
        prev = np.where(qi <= kj, -a * (128 + qi - kj), NEG)
        cur = np.where(qi >= kj, -a * (qi - kj), NEG)
        tab[:, h, 0:128] = prev
        tab[:, h, 128:256] = cur
    hi = _bf16_round(tab.astype(np.float32))
    lo = _bf16_round((tab - hi.astype(np.float64)).astype(np.float32))
    return hi.reshape(128, 24 * 256), lo.reshape(128, 24 * 256)


def build_program(n_layers=NL, debug=False):
    nc = bass.Bass("TRN2", target_bir_lowering=False)
    S = Sched()

    def din(name, shape, dt=F32):
        return nc.dram_tensor(name, list(shape), dt, kind="ExternalInput").ap()

    def dscr(name, shape, dt):
        return nc.dram_tensor(name, list(shape), dt, kind=("ExternalOutput" if debug else "Internal")).ap()

    xT_in = din("xT", [D, L])
    cT_in = din("cT", [128, 16])
    w_mod = din("w_mod", [NL, D, 6 * D])
    w_in = din("w_in", [NL, D, N_IN])
    w_glu = din("w_glu", [NL, 512, 512])
    w_ssm_out = din("w_ssm_out", [NL, 512, D])
    w_attn_out = din("w_attn_out", [NL, 512, D])
    w_conv_out = din("w_conv_out", [NL, 512, D])
    w_o = din("w_o", [NL, D, D])
    w_up = din("w_up", [NL, D, 2 * DFF])
    w_down = din("w_down", [NL, DFF, D])
    bmodT_in = din("bmodT", [NL, 128, 96])
    gains_in = din("gains", [NL, 128, 64])
    bgateT_in = din("bgateT", [NL, 128, 48])
    ssm_sm_in = din("ssm_sm", [NL, 128, 48])
    ssm_flat_in = din("ssm_flat", [NL, 3, 2048])
    BTre_in = din("BTre", [NL, 128, 2048])
    BTim_in = din("BTim", [NL, 128, 2048])
    CTre_in = din("CTre", [NL, 128, 2048])
    CTim_in = din("CTim", [NL, 128, 2048])
    dT_in = din("dT", [NL, 128, 4])
    bgluT_in = din("bgluT", [NL, 128, 4])
    convwT_in = din("convwT", [NL, 128, 12])
    ffnwT_in = din("ffnwT", [NL, 128, 264])
    iota_in = din("iota", [128, 512])
    ones_in = din("ones", [128, 128])
    bias_hi_in = din("bias_hi", [128, 24 * 256])
    bias_lo_in = din("bias_lo", [128, 24 * 256])

    outT = nc.dram_tensor("outT", [D, L], F32, kind="ExternalOutput").ap()
    XT = dscr("XT", [D, L], F32)
    YT = dscr("YT", [D, L], F32)
    PROJ = dscr("PROJ", [N_IN, L], BF16)
    VTOK = dscr("VTOK", [L, 1536], BF16)
    GS = dscr("GS", [DFF, L], BF16)

    st = ExitStack()
    with st:
        def sbt(name, shape, dt):
            return st.enter_context(nc.sbuf_tensor("sb_" + name, list(shape), dt))

        R1 = sbt("R1", [128, 32768], BF16)
        WSall = sbt("WS", [128, 24576], BF16)
        WSa = [WSall[:, 8192 * i:8192 * (i + 1)] for i in range(3)]
        BRa = sbt("BR", [128, 24576], BF16)
        Ta = sbt("T", [128, 12288], BF16)
        ACC = sbt("ACC", [128, 2048], F32)
        RSTDY = sbt("RSTDY", [128, 2048], F32)
        PS = st.enter_context(nc.psum_tensor("PS", [128, 4096], F32))
        iota = sbt("iota", [128, 512], F32)
        ones32 = sbt("ones32", [128, 128], F32)
        onesbf = sbt("onesbf", [128, 128], BF16)
        epsT = sbt("epsT", [128, 1], F32)
        condT = sbt("condT", [128, 16], BF16)
        cTs = sbt("cTs", [128, 16], F32)
        modT2 = [sbt("modT%d" % i, [128, 96], F32) for i in range(2)]
        bmodT = sbt("bmodT", [128, 96], F32)
        gains = sbt("gains", [128, 64], F32)
        AG2 = [sbt("AG%d" % i, [128, 64], F32) for i in range(2)]
        bgateT = sbt("bgateT", [128, 48], F32)
        ssm_sm = sbt("ssm_sm", [128, 48], F32)
        sm_r = sbt("sm_r", [128, 16], F32)
        sm_th = sbt("sm_th", [128, 16], F32)
        sm_cT = sbt("sm_cT", [128, 16], F32)
        sm_sT = sbt("sm_sT", [128, 16], F32)
        sm_tmp = sbt("sm_tmp", [128, 96], F32)
        carry = sbt("carry", [128, 8], F32)
        cst = sbt("cst", [128, 4], F32)
        dTt = sbt("dTt", [128, 4], F32)
        bgluT = sbt("bgluT", [128, 4], F32)
        convwT = sbt("convwT", [128, 12], F32)
        ffnwT = sbt("ffnwT", [128, 264], F32)

        def psb(b, n=1):
            return PS[:, 512 * b:512 * (b + n)]

        def pk(b, n=1):
            return ["ps%d" % i for i in range(b, b + n)]

        class Arena:
            def __init__(self, t, size, name):
                self.t = t
                self.size = size
                self.off = 0
                self.name = name

            def reset(self):
                self.off = 0

            def bf(self, n):
                assert self.off + n <= self.size, (self.name, self.off, n, self.size)
                a = self.t[:, self.off:self.off + n]
                self.off += n
                return a

            def f32(self, n):
                return self.bf(2 * n).bitcast(F32)

        aR1 = Arena(R1, 32768, "R1")
        aBR = Arena(BRa, 24576, "BR")
        aT = Arena(Ta, 12288, "T")

        def reset_arenas():
            aR1.reset()
            aBR.reset()
            aT.reset()

        def A(eng, fn, r=(), w=(), dma=False):
            S.add(eng, fn, r, w, dma)

        def dma_sp(out, in_, r=(), w=()):
            A("sp", lambda e: e.dma_start(out=out, in_=in_), r, w, True)

        def dma_pool(out, in_, r=(), w=()):
            A("pool", lambda e: e.dma_start(out=out, in_=in_), r, w, True)

        dma_sp(iota[:], iota_in[:, :], w=["iota"])
        dma_sp(ones32[:], ones_in[:, :], w=["ones32"])
        dma_pool(onesbf[:], ones_in[:, :], w=["onesbf"])
        dma_sp(cTs[:], cT_in[:, :], w=["cTs"])
        A("dve", lambda e: e.memset(epsT[:], 1e-6), w=["epsT"])
        A("dve", lambda e: e.memset(cst[:, 0:1], 0.0), w=["cst"])
        A("dve", lambda e: e.memset(cst[:, 1:2], PI / 2), w=["cst"])
        A("dve", lambda e: e.memset(cst[:, 2:3], 12582912.0), w=["cst"])
        A("dve", lambda e: e.memset(cst[:, 3:4], -12582912.0), w=["cst"])
        A("act", lambda e: e.activation(out=condT[:], in_=cTs[:], func=AF.Silu), r=["cTs"], w=["condT"])
        S.barrier()

        def sincos(x_ap, n, out_sin, out_cos, tmpk, tmpp, keyp):
            M = 12582912.0
            C1 = 6.28125
            C2 = float(2 * np.pi - 6.28125)
            for which, outp, shift in (("s", out_sin, 0.0), ("c", out_cos, PI / 2)):
                kk = keyp + which
                A("dve", lambda e, shift=shift: e.tensor_scalar(out=tmpp, in0=x_ap, scalar1=shift, scalar2=None, op0=ALU.add), r=[keyp + "x"], w=[keyp + "p"])
                A("dve", lambda e: e.tensor_scalar(out=tmpk, in0=tmpp, scalar1=float(1 / (2 * np.pi)), scalar2=M, op0=ALU.mult, op1=ALU.add), r=[keyp + "p"], w=[keyp + "k"])
                A("dve", lambda e: e.tensor_scalar(out=tmpk, in0=tmpk, scalar1=-M, scalar2=None, op0=ALU.add), r=[keyp + "k"], w=[keyp + "k"])
                A("dve", lambda e: e.scalar_tensor_tensor(out=tmpp, in0=tmpk, scalar=-C1, in1=tmpp, op0=ALU.mult, op1=ALU.add), r=[keyp + "k", keyp + "p"], w=[keyp + "p"])
                A("dve", lambda e: e.scalar_tensor_tensor(out=tmpp, in0=tmpk, scalar=-C2, in1=tmpp, op0=ALU.mult, op1=ALU.add), r=[keyp + "k", keyp + "p"], w=[keyp + "p"])
                A("dve", lambda e: e.tensor_scalar(out=tmpp, in0=tmpp, scalar1=PI, scalar2=-PI, op0=ALU.min, op1=ALU.max), r=[keyp + "p"], w=[keyp + "p"])
                A("act", lambda e, outp=outp: e.activation(out=outp, in_=tmpp, func=AF.Sin), r=[keyp + "p"], w=[kk])

        def load_w(slot, dram_ap3, nkt, ncol, key):
            v = WSa[slot][:, 0:nkt * ncol].rearrange("p (k n) -> p k n", k=nkt)
            dma_pool(v, dram_ap3, w=[key])
            return v

        def gen_mod(l):
            par = l % 2
            modT = modT2[par]
            AG = AG2[par]
            mk, ak = "modT%d" % par, "AG%d" % par
            dma_sp(bmodT[:], bmodT_in[l], w=["bmodT"])
            dma_sp(gains[:], gains_in[l], w=["gains"])
            dma_sp(bgateT[:], bgateT_in[l], w=["bgateT"])
            wv = w_mod[l].rearrange("(kt p) n -> p kt n", p=128)
            for cg in range(24):
                slot = cg % 3
                W = load_w(slot, wv[:, :, 512 * cg:512 * (cg + 1)], 16, 512, "WS%d" % slot)
                for j in range(4):
                    ct = 4 * cg + j
                    for kt in range(16):
                        A("pe", lambda e, W=W, j=j, kt=kt, ct=ct: e.matmul(PS[:, 1024 + ct:1025 + ct], W[:, kt, 128 * j:128 * (j + 1)], condT[:, kt:kt + 1], start=(kt == 0), stop=(kt == 15)),
                          r=["WS%d" % slot, "condT"], w=["ps2"])
                yield
            A("dve", lambda e: e.tensor_tensor(out=modT[:], in0=PS[:, 1024:1120], in1=bmodT[:], op=ALU.add), r=["ps2", "bmodT"], w=[mk])
            A("dve", lambda e: e.scalar_tensor_tensor(out=AG[:, 0:16], in0=modT[:, 16:32], scalar=1.0, in1=gains[:, 0:16], op0=ALU.add, op1=ALU.mult), r=[mk, "gains"], w=[ak])
            A("dve", lambda e: e.tensor_tensor(out=AG[:, 16:32], in0=modT[:, 32:48], in1=gains[:, 16:32], op=ALU.mult), r=[mk, "gains"], w=[ak])
            A("dve", lambda e: e.scalar_tensor_tensor(out=AG[:, 32:48], in0=modT[:, 64:80], scalar=1.0, in1=gains[:, 32:48], op0=ALU.add, op1=ALU.mult), r=[mk, "gains"], w=[ak])
            A("dve", lambda e: e.tensor_tensor(out=AG[:, 48:64], in0=modT[:, 80:96], in1=gains[:, 48:64], op=ALU.mult), r=[mk, "gains"], w=[ak])
            yield

        def phase_mod(l):
            S.mark("phase_mod")
            for _ in gen_mod(l):
                pass
            S.barrier()

        def phase_norm(src_x, has_y, Gap, Aap, Bap, dst_x, make_h, gpar=0, apar=0, other=None):
            S.mark("phase_norm")
            g = gen_norm(src_x, has_y, Gap, Aap, Bap, dst_x, make_h, gpar, apar)
            done_o = other is None
            k = 0
            for _ in g:
                k += 1
                if not done_o and k % 2 == 0:
                    try:
                        next(other)
                    except StopIteration:
                        done_o = True
            if not done_o:
                for _ in other:
                    pass
            S.barrier()

        def gen_norm(src_x, has_y, Gap, Aap, Bap, dst_x, make_h, gpar, apar):
            reset_arenas()
            gkey = "AG%d" % gpar
            akey = "AG%d" % apar
            mkey = "modT%d" % apar
            hT = aR1.bf(32768).rearrange("p (k t) -> p k t", k=16)
            XN = aBR.f32(8192).rearrange("p (k t) -> p k t", k=16)
            NB = 3
            xt = [aT.f32(512) for _ in range(NB)]
            yt = [aT.f32(512) for _ in range(NB)]
            tmp = [aT.f32(512) for _ in range(2)]
            sq = [aT.bf(512) for _ in range(2)]
            rt = aT.f32(512)
            rstd = aT.f32(512)
            steps = [(c, ft) for c in range(NCH) for ft in range(16)]
            loaded = [0]

            def emit_loads(upto):
                while loaded[0] < min(upto, len(steps)):
                    i = loaded[0]
                    c, ft = steps[i]
                    bb = i % NB
                    rows = slice(128 * ft, 128 * (ft + 1))
                    cs = slice(TC * c, TC * (c + 1))
                    dma_sp(xt[bb], src_x[rows, cs], w=["xt%d" % bb])
                    dma_sp(yt[bb], YT[rows, cs], w=["yt%d" % bb])
                    loaded[0] += 1

            for i, (c, ft) in enumerate(steps):
                cs = slice(TC * c, TC * (c + 1))
                pb = c % 2
                b = ft % 2
                rows = slice(128 * ft, 128 * (ft + 1))
                kx = "XN%d" % ft
                if has_y:
                    emit_loads(i + NB)
                    bb = i % NB
                    A("dve", lambda e, b=b, bb=bb, cs=cs: e.tensor_tensor(out=tmp[b], in0=yt[bb], in1=RSTDY[:, cs], op=ALU.mult), r=["yt%d" % bb, "RSTDY"], w=["tmp%d" % b])
                    A("dve", lambda e, b=b, bb=bb, ft=ft: e.scalar_tensor_tensor(out=XN[:, ft, :], in0=tmp[b], scalar=Gap[:, ft:ft + 1], in1=xt[bb], op0=ALU.mult, op1=ALU.add),
                      r=["tmp%d" % b, "xt%d" % bb, gkey], w=[kx])
                else:
                    dma_sp(XN[:, ft, :], src_x[rows, cs], w=[kx])
                if dst_x is not None:
                    dma_pool(dst_x[rows, cs], XN[:, ft, :], r=[kx])
                if make_h:
                    A("act", lambda e, b=b, ft=ft: e.activation(out=sq[b], in_=XN[:, ft, :], func=AF.Square), r=[kx], w=["sq%d" % b])
                    A("pe", lambda e, b=b, ft=ft, pb=pb: e.matmul(psb(pb), onesbf[:], sq[b], start=(ft == 0), stop=(ft == 15)), r=["sq%d" % b, "onesbf"], w=pk(pb))
                if make_h and ft == 15:
                    A("act", lambda e, pb=pb: e.activation(out=rt, in_=psb(pb), func=AF.Sqrt, bias=epsT[:, 0:1], scale=1.0 / D), r=pk(pb) + ["epsT"], w=["rt"])
                    A("dve", lambda e: e.reciprocal(out=rstd, in_=rt), r=["rt"], w=["rstd"])
                    for f2 in range(16):
                        b2 = f2 % 2
                        A("dve", lambda e, b2=b2, f2=f2: e.tensor_tensor(out=tmp[b2], in0=XN[:, f2, :], in1=rstd, op=ALU.mult), r=["XN%d" % f2, "rstd"], w=["tmp%d" % b2])
                        A("act", lambda e, b2=b2, f2=f2, cs=cs: e.activation(out=hT[:, f2, cs], in_=tmp[b2], func=AF.Identity, bias=Bap[:, f2:f2 + 1], scale=Aap[:, f2:f2 + 1]),
                          r=["tmp%d" % b2, akey, mkey], w=["hT%d" % f2])
                yield

        half_ctr = [0]

        def gen_inproj(l, uT):
            hT = R1[:, :].rearrange("p (k t) -> p k t", k=16)
            stg = [aT.bf(2048) for _ in range(2)]
            vst = [aT.bf(512) for _ in range(2)]
            wv = w_in[l].rearrange("(kt p) n -> p kt n", p=128)
            nb_ = 0
            ev = 0
            for cg in range(25):
                slot = cg % 3
                wk = "WS%d" % slot
                W = load_w(slot, wv[:, :, 512 * cg:512 * (cg + 1)], 16, 512, wk)
                if 7 <= cg <= 9:
                    for tt in range(16):
                        bank = nb_ % 2
                        nb_ += 1
                        for kt in range(16):
                            A("pe", lambda e, W=W, kt=kt, tt=tt, bank=bank: e.matmul(psb(bank), hT[:, kt, 128 * tt:128 * (tt + 1)], W[:, kt, :], start=(kt == 0), stop=(kt == 15)),
                              r=[wk, "hT%d" % kt], w=pk(bank))
                        b = tt % 2
                        A("act", lambda e, b=b, bank=bank: e.activation(out=vst[b], in_=psb(bank), func=AF.Copy), r=pk(bank), w=["vst%d" % b])
                        dma_sp(VTOK[128 * tt:128 * (tt + 1), 512 * (cg - 7):512 * (cg - 6)], vst[b], r=["vst%d" % b])
                        if tt % 4 == 3:
                            yield
                    continue
                for j in range(4):
                    pt = 4 * cg + j
                    b = ev % 2
                    ev += 1
                    if cg == 0:
                        dst = uT[:, j, :]
                        sk = "uT%d" % j
                    else:
                        dst = stg[b]
                        sk = "stg%d" % b
                    for c_ in range(NCH):
                        bank = nb_ % 2
                        nb_ += 1
                        for kt in range(16):
                            A("pe", lambda e, W=W, kt=kt, j=j, c_=c_, bank=bank: e.matmul(psb(bank), W[:, kt, 128 * j:128 * (j + 1)], hT[:, kt, TC * c_:TC * (c_ + 1)], start=(kt == 0), stop=(kt == 15)),
                              r=[wk, "hT%d" % kt], w=pk(bank))
                        src = psb(bank)
                        if cg >= 13:
                            gi_ = pt - 52
                            A("act", lambda e, src=src, dst=dst, gi_=gi_, c_=c_: e.activation(out=dst[:, TC * c_:TC * (c_ + 1)], in_=src, func=AF.Sigmoid, bias=bgateT[:, gi_:gi_ + 1]), r=pk(bank) + ["bgateT"], w=[sk])
                        elif cg in (2, 5, 3, 6):
                            dil = 4 if cg in (2, 5) else 16
                            w_ = 512 // dil
                            A("act", lambda e, src=src, dst=dst, dil=dil, w_=w_, c_=c_: e.activation(out=dst.rearrange("p (r m) -> p r m", r=dil)[:, :, w_ * c_:w_ * (c_ + 1)], in_=src.rearrange("p (m r) -> p r m", r=dil), func=AF.Copy), r=pk(bank), w=[sk])
                        else:
                            A("act", lambda e, src=src, dst=dst, c_=c_: e.activation(out=dst[:, TC * c_:TC * (c_ + 1)], in_=src, func=AF.Copy), r=pk(bank), w=[sk])
                    if cg != 0:
                        dma_sp(PROJ[128 * pt:128 * (pt + 1), :], dst, r=[sk], w=["PROJ%d" % pt])
                    yield

        def ssm_prep(l, ctx):
            dma_sp(ssm_sm[:], ssm_sm_in[l], w=["ssm_sm"])
            dma_sp(dTt[:], dT_in[l], w=["dTt"])
            dma_sp(bgluT[:], bgluT_in[l], w=["bgluT"])
            dt_sm = sm_tmp[:, 0:16]
            A("act", lambda e: e.activation(out=dt_sm, in_=ssm_sm[:, 0:16], func=AF.Exp), r=["ssm_sm"], w=["dt_sm"])
            A("dve", lambda e: e.tensor_tensor(out=sm_tmp[:, 16:32], in0=ssm_sm[:, 16:32], in1=dt_sm, op=ALU.mult), r=["dt_sm", "ssm_sm"], w=["ardt"])
            A("act", lambda e: e.activation(out=sm_r[:], in_=sm_tmp[:, 16:32], func=AF.Exp), r=["ardt"], w=["sm_r"])
            A("dve", lambda e: e.tensor_tensor(out=sm_th[:], in0=ssm_sm[:, 32:48], in1=dt_sm, op=ALU.mult), r=["dt_sm", "ssm_sm"], w=["sm_th"])
            A("dve", lambda e: e.tensor_scalar(out=sm_tmp[:, 32:48], in0=sm_th[:], scalar1=float(TC), scalar2=None, op0=ALU.mult), r=["sm_th"], w=["smTx"])
            sincos(sm_tmp[:, 32:48], 16, sm_sT[:], sm_cT[:], sm_tmp[:, 48:64], sm_tmp[:, 64:80], "smT")
            BbrT = aBR.bf(2048).rearrange("p (s n) -> p s n", s=16)
            BbiT = aBR.bf(2048).rearrange("p (s n) -> p s n", s=16)
            CrT = aBR.bf(2048).rearrange("p (s n) -> p s n", s=16)
            CiT = aT.bf(2048).rearrange("p (s n) -> p s n", s=16)
            Wglu = aT.bf(2048).rearrange("p (k n) -> p k n", k=4)
            ctx.update(BbrT=BbrT, BbiT=BbiT, CrT=CrT, CiT=CiT, Wglu=Wglu)
            dma_pool(CrT.rearrange("p s n -> p (s n)"), CTre_in[l], w=["CrT"])
            dma_pool(CiT.rearrange("p s n -> p (s n)"), CTim_in[l], w=["CiT"])
            dma_pool(Wglu, w_glu[l].rearrange("(k p) n -> p k n", p=128), w=["Wglu"])
            mark_br = aBR.off
            fl = [aBR.f32(512) for _ in range(14)]
            (f_ld, f_ar, f_ai, f_dt, f_mag, f_th, f_sin, f_cos, f_k, f_p, f_a, f_b, f_fr, f_fi) = fl
            f_bre = aBR.f32(512)
            f_bim = aBR.f32(512)
            for q in range(4):
                qs = slice(512 * q, 512 * (q + 1))
                dma_sp(f_ld, ssm_flat_in[l, 0:1, qs].to_broadcast([128, 512]), w=["f_ld"])
                dma_sp(f_ar, ssm_flat_in[l, 1:2, qs].to_broadcast([128, 512]), w=["f_ar"])
                dma_sp(f_ai, ssm_flat_in[l, 2:3, qs].to_broadcast([128, 512]), w=["f_ai"])
                dma_sp(f_bre, BTre_in[l, :, qs], w=["f_bre"])
                dma_sp(f_bim, BTim_in[l, :, qs], w=["f_bim"])
                A("act", lambda e: e.activation(out=f_dt, in_=f_ld, func=AF.Exp), r=["f_ld"], w=["f_dt"])
                A("dve", lambda e: e.tensor_tensor(out=f_a, in0=f_ar, in1=f_dt, op=ALU.mult), r=["f_ar", "f_dt"], w=["f_a"])
                A("act", lambda e: e.activation(out=f_mag, in_=f_a, func=AF.Exp), r=["f_a"], w=["f_mag"])
                A("dve", lambda e: e.tensor_tensor(out=f_th, in0=f_ai, in1=f_dt, op=ALU.mult), r=["f_ai", "f_dt"], w=["flx"])
                sincos(f_th, 512, f_sin, f_cos, f_k, f_p, "fl")
                A("dve", lambda e: e.tensor_tensor(out=f_cos, in0=f_cos, in1=f_mag, op=ALU.mult), r=["flc", "f_mag"], w=["flc"])
                A("dve", lambda e: e.tensor_tensor(out=f_sin, in0=f_sin, in1=f_mag, op=ALU.mult), r=["fls", "f_mag"], w=["fls"])
                A("dve", lambda e: e.tensor_scalar(out=f_cos, in0=f_cos, scalar1=-1.0, scalar2=None, op0=ALU.add), r=["flc"], w=["flc"])
                A("dve", lambda e: e.tensor_tensor(out=f_a, in0=f_ar, in1=f_ar, op=ALU.mult), r=["f_ar"], w=["f_a"])
                A("dve", lambda e: e.tensor_tensor(out=f_b, in0=f_ai, in1=f_ai, op=ALU.mult), r=["f_ai"], w=["f_b"])
                A("dve", lambda e: e.tensor_tensor(out=f_a, in0=f_a, in1=f_b, op=ALU.add), r=["f_a", "f_b"], w=["f_a"])
                A("dve", lambda e: e.reciprocal(out=f_a, in_=f_a), r=["f_a"], w=["f_a"])
                A("dve", lambda e: e.tensor_tensor(out=f_fr, in0=f_cos, in1=f_ar, op=ALU.mult), r=["flc", "f_ar"], w=["f_fr"])
                A("dve", lambda e: e.tensor_tensor(out=f_b, in0=f_sin, in1=f_ai, op=ALU.mult), r=["fls", "f_ai"], w=["f_b"])
                A("dve", lambda e: e.tensor_tensor(out=f_fr, in0=f_fr, in1=f_b, op=ALU.add), r=["f_fr", "f_b"], w=["f_fr"])
                A("dve", lambda e: e.tensor_tensor(out=f_fr, in0=f_fr, in1=f_a, op=ALU.mult), r=["f_fr", "f_a"], w=["f_fr"])
                A("dve", lambda e: e.tensor_tensor(out=f_fi, in0=f_sin, in1=f_ar, op=ALU.mult), r=["fls", "f_ar"], w=["f_fi"])
                A("dve", lambda e: e.tensor_tensor(out=f_b, in0=f_cos, in1=f_ai, op=ALU.mult), r=["flc", "f_ai"], w=["f_b"])
                A("dve", lambda e: e.tensor_tensor(out=f_fi, in0=f_fi, in1=f_b, op=ALU.subtract), r=["f_fi", "f_b"], w=["f_fi"])
                A("dve", lambda e: e.tensor_tensor(out=f_fi, in0=f_fi, in1=f_a, op=ALU.mult), r=["f_fi", "f_a"], w=["f_fi"])
                brv = BbrT.rearrange("p s n -> p (s n)")[:, qs]
                biv = BbiT.rearrange("p s n -> p (s n)")[:, qs]
                A("dve", lambda e: e.tensor_tensor(out=f_k, in0=f_fr, in1=f_bre, op=ALU.mult), r=["f_fr", "f_bre"], w=["flk"])
                A("dve", lambda e: e.tensor_tensor(out=f_p, in0=f_fi, in1=f_bim, op=ALU.mult), r=["f_fi", "f_bim"], w=["flp"])
                A("dve", lambda e, brv=brv: e.tensor_tensor(out=brv, in0=f_k, in1=f_p, op=ALU.subtract), r=["flk", "flp"], w=["BbrT"])
                A("dve", lambda e: e.tensor_tensor(out=f_k, in0=f_fr, in1=f_bim, op=ALU.mult), r=["f_fr", "f_bim"], w=["flk"])
                A("dve", lambda e: e.tensor_tensor(out=f_p, in0=f_fi, in1=f_bre, op=ALU.mult), r=["f_fi", "f_bre"], w=["flp"])
                A("dve", lambda e, biv=biv: e.tensor_tensor(out=biv, in0=f_k, in1=f_p, op=ALU.add), r=["flk", "flp"], w=["BbiT"])
            aBR.off = mark_br
            S.barrier()

        def gen_ssm_main(l, ctx):
            BbrT, BbiT, CrT, CiT = ctx["BbrT"], ctx["BbiT"], ctx["CrT"], ctx["CiT"]
            uT = ctx["uT"]
            GL = uT
            cosT2 = [aBR.f32(512), aBR.f32(512)]
            sinT2 = [aBR.f32(512), aT.f32(512)]
            tk = aBR.f32(512)
            tp = aBR.f32(512)
            xr_s = aBR.f32(512)
            xi_s = aBR.f32(512)
            t1 = aBR.f32(512)
            t2 = aBR.f32(512)
            t3 = aBR.f32(512)
            t4 = aBR.f32(512)
            zr = aBR.f32(512)
            zi = aBR.f32(512)
            gr = aBR.f32(512)
            gi = aBR.f32(512)
            hr = [aBR.bf(512) for _ in range(2)]
            nhi = [aBR.bf(512) for _ in range(2)]
            yv = t1
            y2 = t2
            ctr = 0
            pending = []

            def flush():
                for f in pending:
                    f()
                del pending[:]

            M_ = 12582912.0
            C1_ = 6.28125
            C2_ = float(2 * np.pi - 6.28125)

            def table_dve(st_, shift):
                A("dve", lambda e: e.tensor_scalar(out=tp, in0=iota[:], scalar1=sm_th[:, st_:st_ + 1], scalar2=None, op0=ALU.mult), r=["iota", "sm_th"], w=["tbp"])
                if shift != 0.0:
                    A("dve", lambda e: e.tensor_scalar(out=tp, in0=tp, scalar1=shift, scalar2=None, op0=ALU.add), r=["tbp"], w=["tbp"])
                A("dve", lambda e: e.tensor_scalar(out=tk, in0=tp, scalar1=float(1 / (2 * np.pi)), scalar2=M_, op0=ALU.mult, op1=ALU.add), r=["tbp"], w=["tbk"])
                A("dve", lambda e: e.tensor_scalar(out=tk, in0=tk, scalar1=-M_, scalar2=None, op0=ALU.add), r=["tbk"], w=["tbk"])
                A("dve", lambda e: e.scalar_tensor_tensor(out=tp, in0=tk, scalar=-C1_, in1=tp, op0=ALU.mult, op1=ALU.add), r=["tbk", "tbp"], w=["tbp"])
                A("dve", lambda e: e.scalar_tensor_tensor(out=tp, in0=tk, scalar=-C2_, in1=tp, op0=ALU.mult, op1=ALU.add), r=["tbk", "tbp"], w=["tbp"])
                A("dve", lambda e: e.tensor_scalar(out=tp, in0=tp, scalar1=PI, scalar2=-PI, op0=ALU.min, op1=ALU.max), r=["tbp"], w=["tbp"])

            def table_act(outp, okey):
                A("act", lambda e: e.activation(out=outp, in_=tp, func=AF.Sin), r=["tbp"], w=[okey])

            for stt in range(16):
                ut = stt // 4
                tpar = stt % 2
                cosT = cosT2[tpar]
                sinT = sinT2[tpar]
                kc = "tbc%d" % tpar
                ks = "tbs%d" % tpar
                if stt == 0:
                    table_dve(0, 0.0)
                    table_act(sinT2[0], "tbs0")
                    table_dve(0, PI / 2)
                    table_act(cosT2[0], "tbc0")
                for c in range(NCH):
                    cs = slice(TC * c, TC * (c + 1))
                    b = ctr % 2
                    ctr += 1
                    flush()
                    A("pe", lambda e, stt=stt, ut=ut, cs=cs: e.matmul(psb(6), BbrT[:, stt, :], uT[:, ut, cs], start=True, stop=True), r=["BbrT", "uT%d" % ut], w=pk(6))
                    A("pe", lambda e, stt=stt, ut=ut, cs=cs: e.matmul(psb(7), BbiT[:, stt, :], uT[:, ut, cs], start=True, stop=True), r=["BbiT", "uT%d" % ut], w=pk(7))
                    A("act", lambda e: e.activation(out=xr_s, in_=psb(6), func=AF.Copy), r=pk(6), w=["xr"])
                    A("act", lambda e: e.activation(out=xi_s, in_=psb(7), func=AF.Copy), r=pk(7), w=["xi"])
                    A("dve", lambda e, cosT=cosT, sinT=sinT: e.tensor_tensor(out=t1, in0=xr_s, in1=cosT, op=ALU.mult), r=["xr", kc], w=["t1"])
                    A("dve", lambda e, cosT=cosT, sinT=sinT: e.tensor_tensor(out=t2, in0=xi_s, in1=sinT, op=ALU.mult), r=["xi", ks], w=["t2"])
                    A("dve", lambda e: e.tensor_tensor(out=zr, in0=t1, in1=t2, op=ALU.add), r=["t1", "t2"], w=["zr"])
                    A("dve", lambda e, cosT=cosT, sinT=sinT: e.tensor_tensor(out=t3, in0=xi_s, in1=cosT, op=ALU.mult), r=["xi", kc], w=["t3"])
                    A("dve", lambda e, cosT=cosT, sinT=sinT: e.tensor_tensor(out=t4, in0=xr_s, in1=sinT, op=ALU.mult), r=["xr", ks], w=["t4"])
                    A("dve", lambda e: e.tensor_tensor(out=zi, in0=t3, in1=t4, op=ALU.subtract), r=["t3", "t4"], w=["zi"])
                    rbc = sm_r[:, stt:stt + 1].to_broadcast([128, 512])
                    if c == 0:
                        A("dve", lambda e, rbc=rbc: e.tensor_tensor_scan(out=gr, data0=rbc, data1=zr, initial=0.0, op0=ALU.mult, op1=ALU.add), r=["zr", "sm_r"], w=["gr"])
                        A("dve", lambda e, rbc=rbc: e.tensor_tensor_scan(out=gi, data0=rbc, data1=zi, initial=0.0, op0=ALU.mult, op1=ALU.add), r=["zi", "sm_r"], w=["gi"])
                    else:
                        A("dve", lambda e, rbc=rbc: e.tensor_tensor_scan(out=gr, data0=rbc, data1=zr, initial=carry[:, 0:1], op0=ALU.mult, op1=ALU.add), r=["zr", "sm_r", "carry"], w=["gr"])
                        A("dve", lambda e, rbc=rbc: e.tensor_tensor_scan(out=gi, data0=rbc, data1=zi, initial=carry[:, 1:2], op0=ALU.mult, op1=ALU.add), r=["zi", "sm_r", "carry"], w=["gi"])
                    if c < NCH - 1:
                        cT_ = sm_cT[:, stt:stt + 1]
                        sT_ = sm_sT[:, stt:stt + 1]
                        A("dve", lambda e, cT_=cT_: e.tensor_tensor(out=carry[:, 2:3], in0=gr[:, 511:512], in1=cT_, op=ALU.mult), r=["gr", "smTc"], w=["cy2"])
                        A("dve", lambda e, sT_=sT_: e.tensor_tensor(out=carry[:, 3:4], in0=gi[:, 511:512], in1=sT_, op=ALU.mult), r=["gi", "smTs"], w=["cy3"])
                        A("dve", lambda e, sT_=sT_: e.tensor_tensor(out=carry[:, 4:5], in0=gr[:, 511:512], in1=sT_, op=ALU.mult), r=["gr", "smTs"], w=["cy4"])
                        A("dve", lambda e, cT_=cT_: e.tensor_tensor(out=carry[:, 5:6], in0=gi[:, 511:512], in1=cT_, op=ALU.mult), r=["gi", "smTc"], w=["cy5"])
                        A("dve", lambda e: e.tensor_tensor(out=carry[:, 0:1], in0=carry[:, 2:3], in1=carry[:, 3:4], op=ALU.subtract), r=["cy2", "cy3"], w=["carry"])
                        A("dve", lambda e: e.tensor_tensor(out=carry[:, 1:2], in0=carry[:, 4:5], in1=carry[:, 5:6], op=ALU.add), r=["cy4", "cy5", "carry"], w=["carry"])
                    A("dve", lambda e, cosT=cosT, sinT=sinT: e.tensor_tensor(out=t1, in0=gr, in1=cosT, op=ALU.mult), r=["gr", kc], w=["t1"])
                    A("dve", lambda e, cosT=cosT, sinT=sinT: e.tensor_tensor(out=t2, in0=gi, in1=sinT, op=ALU.mult), r=["gi", ks], w=["t2"])
                    A("dve", lambda e, b=b: e.tensor_tensor(out=hr[b], in0=t1, in1=t2, op=ALU.subtract), r=["t1", "t2"], w=["hr%d" % b])
                    A("dve", lambda e, cosT=cosT, sinT=sinT: e.tensor_tensor(out=t3, in0=gr, in1=sinT, op=ALU.mult), r=["gr", ks], w=["t3"])
                    A("dve", lambda e, cosT=cosT, sinT=sinT: e.tensor_tensor(out=t4, in0=gi, in1=cosT, op=ALU.mult), r=["gi", kc], w=["t4"])
                    A("dve", lambda e, b=b: e.scalar_tensor_tensor(out=nhi[b], in0=t3, scalar=-1.0, in1=t4, op0=ALU.mult, op1=ALU.subtract), r=["t3", "t4"], w=["nhi%d" % b])
                    first = (stt % 4 == 0)
                    last = (stt % 4 == 3)

                    def cmm(stt=stt, b=b, c=c, first=first, last=last):
                        A("pe", lambda e: e.matmul(psb(2 + c), CrT[:, stt, :], hr[b], start=first, stop=False), r=["CrT", "hr%d" % b], w=pk(2 + c))
                        A("pe", lambda e: e.matmul(psb(2 + c), CiT[:, stt, :], nhi[b], start=False, stop=last), r=["CiT", "nhi%d" % b], w=pk(2 + c))
                    pending.append(cmm)
                    if stt < 15:
                        npar = (stt + 1) % 2
                        if c == 0:
                            table_dve(stt + 1, 0.0)
                        elif c == 1:
                            table_act(sinT2[npar], "tbs%d" % npar)
                            table_dve(stt + 1, PI / 2)
                        elif c == 2:
                            table_act(cosT2[npar], "tbc%d" % npar)
                    yield
                if stt % 4 == 3:
                    flush()
                    for c in range(NCH):
                        cs = slice(TC * c, TC * (c + 1))
                        A("dve", lambda e, ut=ut, cs=cs, c=c: e.scalar_tensor_tensor(out=yv, in0=uT[:, ut, cs], scalar=dTt[:, ut:ut + 1], in1=psb(2 + c), op0=ALU.mult, op1=ALU.add), r=["uT%d" % ut, "dTt"] + pk(2 + c), w=["t1"])
                        A("dve", lambda e: e.tensor_tensor(out=y2, in0=yv, in1=yv, op=ALU.mult), r=["t1"], w=["t2"])
                        A("dve", lambda e: e.tensor_scalar(out=y2, in0=y2, scalar1=0.044715, scalar2=1.0, op0=ALU.mult, op1=ALU.add), r=["t2"], w=["t2"])
                        A("dve", lambda e: e.tensor_tensor(out=y2, in0=y2, in1=yv, op=ALU.mult), r=["t2", "t1"], w=["t2"])
                        A("act", lambda e: e.activation(out=y2, in_=y2, func=AF.Sigmoid, scale=float(2.0 * np.sqrt(2.0 / np.pi))), r=["t2"], w=["t2"])
                        A("dve", lambda e, ut=ut, cs=cs: e.tensor_tensor(out=GL[:, ut, cs], in0=y2, in1=yv, op=ALU.mult), r=["t2", "t1"], w=["uT%d" % ut])
                    yield

        def ssm_glu(l, ctx):
            Wglu = ctx["Wglu"]
            GL = ctx["uT"]
            BR = BRa[:, :].rearrange("p (k t) -> p k t", k=12)
            sg = aT.f32(512)
            n = 0
            for fo in range(4):
                for c in range(NCH):
                    cs = slice(TC * c, TC * (c + 1))
                    bank = 4 + (n % 4)
                    n += 1
                    for kt in range(4):
                        A("pe", lambda e, kt=kt, fo=fo, cs=cs, bank=bank: e.matmul(psb(bank), Wglu[:, kt, 128 * fo:128 * (fo + 1)], GL[:, kt, cs], start=(kt == 0), stop=(kt == 3)), r=["Wglu", "uT%d" % kt], w=pk(bank))
                    A("act", lambda e, fo=fo, bank=bank: e.activation(out=sg, in_=psb(bank), func=AF.Sigmoid, bias=bgluT[:, fo:fo + 1]), r=pk(bank) + ["bgluT"], w=["sg"])
                    A("dve", lambda e, fo=fo, cs=cs: e.tensor_tensor(out=BR[:, fo, cs], in0=GL[:, fo, cs], in1=sg, op=ALU.mult), r=["sg", "uT%d" % fo], w=["BR%d" % fo])
            S.barrier()

        def phase_inproj_ssm(l):
            S.mark("phase_inproj")
            reset_arenas()
            ctx = {}
            class _U:
                def __getitem__(self, key):
                    p, k, t = key
                    base = ACC if k < 2 else RSTDY
                    v = base[:, :].bitcast(BF16).rearrange("p (k t) -> p k t", k=2)
                    return v[p, k % 2, t]
            ctx["uT"] = _U()
            gi_ = gen_inproj(l, ctx["uT"])
            for _ in range(4):
                next(gi_)
            ssm_prep(l, ctx)
            S.mark("phase_ssm")
            gs_ = gen_ssm_main(l, ctx)
            done_i = done_s = False
            while not (done_i and done_s):
                if not done_s:
                    try:
                        next(gs_)
                    except StopIteration:
                        done_s = True
                if not done_i:
                    try:
                        next(gi_)
                    except StopIteration:
                        done_i = True
            S.barrier()
            ssm_glu(l, ctx)

        def phase_conv(l):
            S.mark("phase_conv")
            reset_arenas()
            BR = BRa[:, :].rearrange("p (k t) -> p k t", k=12)
            dma_sp(convwT[:], convwT_in[l], w=["convwT"])
            cb = [aR1.bf(2048) for _ in range(2)]
            cc = [aR1.bf(2048) for _ in range(2)]
            ch = [aR1.bf(2048) for _ in range(2)]
            zb = aR1.f32(2064)
            acc = aR1.f32(2048)
            A("dve", lambda e: e.memset(zb[:, 0:16], 0.0), w=["zb"])
            for j in range(4):
                b = j % 2
                dma_sp(cb[b], PROJ[128 * (40 + j):128 * (41 + j), :], w=["cb%d" % b])
                dma_sp(cc[b], PROJ[128 * (44 + j):128 * (45 + j), :], w=["cc%d" % b])
                dma_sp(ch[b], PROJ[128 * (48 + j):128 * (49 + j), :], w=["ch%d" % b])
                A("dve", lambda e, b=b: e.tensor_tensor(out=zb[:, 16:2064], in0=cc[b], in1=ch[b], op=ALU.mult), r=["cc%d" % b, "ch%d" % b], w=["zb"])
                A("dve", lambda e, j=j: e.tensor_scalar(out=acc, in0=zb[:, 16:2064], scalar1=convwT[:, 3 * j:3 * j + 1], scalar2=None, op0=ALU.mult), r=["zb", "convwT"], w=["cacc"])
                A("dve", lambda e, j=j: e.scalar_tensor_tensor(out=acc, in0=zb[:, 15:2063], scalar=convwT[:, 3 * j + 1:3 * j + 2], in1=acc, op0=ALU.mult, op1=ALU.add), r=["zb", "convwT", "cacc"], w=["cacc"])
                A("dve", lambda e, j=j: e.scalar_tensor_tensor(out=acc, in0=zb[:, 14:2062], scalar=convwT[:, 3 * j + 2:3 * j + 3], in1=acc, op0=ALU.mult, op1=ALU.add), r=["zb", "convwT", "cacc"], w=["cacc"])
                A("dve", lambda e, j=j, b=b: e.tensor_tensor(out=BR[:, 8 + j, :], in0=acc, in1=cb[b], op=ALU.mult), r=["cacc", "cb%d" % b], w=["BR%d" % (8 + j)])
            S.barrier()

        def phase_attn(l):
            S.mark("phase_attn")
            reset_arenas()
            BR = BRa[:, :].rearrange("p (k t) -> p k t", k=12)
            bhi = aR1.bf(24 * 256).rearrange("p (h n) -> p h n", h=24)
            blo = aR1.bf(24 * 256).rearrange("p (h n) -> p h n", h=24)
            qT = [aR1.bf(2048) for _ in range(2)]
            kT = [aR1.bf(2048) for _ in range(2)]
            Vb = [aR1.bf(2048).rearrange("p (b f) -> p b f", b=16) for _ in range(2)]
            identb = aR1.bf(128)
            onesv = aR1.bf(64)
            Uacc = aT.f32(2048)
            Lacc = aT.f32(2048)
            pT = [aT.bf(256) for _ in range(2)]
            dma_pool(bhi.rearrange("p h n -> p (h n)"), bias_hi_in[:, :], w=["bhi"])
            dma_pool(blo.rearrange("p h n -> p (h n)"), bias_lo_in[:, :], w=["blo"])
            dma_pool(identb, ident_in[:, :], w=["identb"])
            A("dve", lambda e: e.memset(onesv, 1.0), w=["onesv"])
            dils = [1, 4, 16]
            n = 0
            sctr = 0
            uctr = 0
            for hp in range(4):
                for g in range(3):
                    dil = dils[g]
                    nb = 16 // dil
                    b = n % 2
                    n += 1
                    qt = 4 + 4 * g + hp
                    kt_ = 16 + 4 * g + hp
                    dma_sp(qT[b], PROJ[128 * qt:128 * (qt + 1), :], w=["qT%d" % b])
                    dma_sp(kT[b], PROJ[128 * kt_:128 * (kt_ + 1), :], w=["kT%d" % b])
                    vsrc = VTOK.rearrange("(bb i r) f -> i r bb f", i=128, r=dil)
                    for r in range(dil):
                        dma_sp(Vb[b][:, r * nb:(r + 1) * nb, :], vsrc[:, r, :, 512 * g + 128 * hp:512 * g + 128 * (hp + 1)], w=["Vb%d" % b])
                    for q4 in range(4):
                        ub = 2 + (uctr % 2)
                        lb = 4 + (uctr % 2)
                        uctr += 1
                        for bi4 in range(4):
                            bi = 4 * q4 + bi4
                            bblk = bi % nb
                            for hh in range(2):
                                head = 8 * g + 2 * hp + hh
                                prt = slice(64 * hh, 64 * (hh + 1))
                                sb_ = sctr % 2
                                sctr += 1
                                sps = PS[:, 256 * sb_:256 * (sb_ + 1)]
                                skey = ["pss%d" % sb_]
                                qv = qT[b][prt, 128 * bi:128 * (bi + 1)]
                                if bblk > 0:
                                    A("pe", lambda e, sps=sps, head=head: e.matmul(sps, identb, bhi[:, head, :], start=True, stop=False), r=["identb", "bhi"], w=skey)
                                    A("pe", lambda e, sps=sps, head=head: e.matmul(sps, identb, blo[:, head, :], start=False, stop=False), r=["identb", "blo"], w=skey)
                                    A("pe", lambda e, sps=sps, b=b, prt=prt, bi=bi, qv=qv: e.matmul(sps[:, 0:128], kT[b][prt, 128 * (bi - 1):128 * bi], qv, start=False, stop=False), r=["kT%d" % b, "qT%d" % b], w=skey)
                                    A("pe", lambda e, sps=sps, b=b, prt=prt, bi=bi, qv=qv: e.matmul(sps[:, 128:256], kT[b][prt, 128 * bi:128 * (bi + 1)], qv, start=False, stop=True), r=["kT%d" % b, "qT%d" % b], w=skey)
                                    A("act", lambda e, sps=sps, sb_=sb_: e.activation(out=pT[sb_], in_=sps, func=AF.Exp, scale=0.125), r=skey, w=["pT%d" % sb_])
                                    ucol = slice(128 * bi4, 128 * (bi4 + 1))
                                    A("pe", lambda e, ub=ub, prt=prt, ucol=ucol, b=b, bi=bi, hh=hh, sb_=sb_: e.matmul(psb(ub)[prt, ucol], Vb[b][:, bi - 1, 64 * hh:64 * (hh + 1)], pT[sb_][:, 0:128], start=True, stop=False), r=["Vb%d" % b, "pT%d" % sb_], w=pk(ub))
                                    A("pe", lambda e, ub=ub, prt=prt, ucol=ucol, b=b, bi=bi, hh=hh, sb_=sb_: e.matmul(psb(ub)[prt, ucol], Vb[b][:, bi, 64 * hh:64 * (hh + 1)], pT[sb_][:, 128:256], start=False, stop=True), r=["Vb%d" % b, "pT%d" % sb_], w=pk(ub))
                                    A("pe", lambda e, lb=lb, prt=prt, ucol=ucol, sb_=sb_: e.matmul(psb(lb)[prt, ucol], onesv, pT[sb_][:, 0:128], start=True, stop=False), r=["onesv", "pT%d" % sb_], w=pk(lb))
                                    A("pe", lambda e, lb=lb, prt=prt, ucol=ucol, sb_=sb_: e.matmul(psb(lb)[prt, ucol], onesv, pT[sb_][:, 128:256], start=False, stop=True), r=["onesv", "pT%d" % sb_], w=pk(lb))
                                else:
                                    sc_ = sps[:, 128:256]
                                    A("pe", lambda e, sc_=sc_, head=head: e.matmul(sc_, identb, bhi[:, head, 128:256], start=True, stop=False), r=["identb", "bhi"], w=skey)
                                    A("pe", lambda e, sc_=sc_, head=head: e.matmul(sc_, identb, blo[:, head, 128:256], start=False, stop=False), r=["identb", "blo"], w=skey)
                                    A("pe", lambda e, sc_=sc_, b=b, prt=prt, bi=bi, qv=qv: e.matmul(sc_, kT[b][prt, 128 * bi:128 * (bi + 1)], qv, start=False, stop=True), r=["kT%d" % b, "qT%d" % b], w=skey)
                                    A("act", lambda e, sc_=sc_, sb_=sb_: e.activation(out=pT[sb_][:, 128:256], in_=sc_, func=AF.Exp, scale=0.125), r=skey, w=["pT%d" % sb_])
                                    ucol = slice(128 * bi4, 128 * (bi4 + 1))
                                    A("pe", lambda e, ub=ub, prt=prt, ucol=ucol, b=b, bi=bi, hh=hh, sb_=sb_: e.matmul(psb(ub)[prt, ucol], Vb[b][:, bi, 64 * hh:64 * (hh + 1)], pT[sb_][:, 128:256], start=True, stop=True), r=["Vb%d" % b, "pT%d" % sb_], w=pk(ub))
                                    A("pe", lambda e, lb=lb, prt=prt, ucol=ucol, sb_=sb_: e.matmul(psb(lb)[prt, ucol], onesv, pT[sb_][:, 128:256], start=True, stop=True), r=["onesv", "pT%d" % sb_], w=pk(lb))
                        if dil == 1:
                            uo = Uacc[:, 512 * q4:512 * (q4 + 1)]
                            lo_ = Lacc[:, 512 * q4:512 * (q4 + 1)]
                            ui = psb(ub)
                            li = psb(lb)
                        elif dil == 4:
                            uo = Uacc.rearrange("p (m r) -> p r m", r=4)[:, q4, :]
                            lo_ = Lacc.rearrange("p (m r) -> p r m", r=4)[:, q4, :]
                            ui = psb(ub)
                            li = psb(lb)
                        else:
                            uo = Uacc.rearrange("p (i r) -> p i r", r=16)[:, :, 4 * q4:4 * (q4 + 1)]
                            lo_ = Lacc.rearrange("p (i r) -> p i r", r=16)[:, :, 4 * q4:4 * (q4 + 1)]
                            ui = psb(ub).rearrange("p (rr i) -> p i rr", rr=4)
                            li = psb(lb).rearrange("p (rr i) -> p i rr", rr=4)
                        if g == 0:
                            A("dve", lambda e, uo=uo, ui=ui: e.tensor_copy(out=uo, in_=ui), r=pk(ub), w=["Uacc"])
                            A("dve", lambda e, lo_=lo_, li=li: e.tensor_copy(out=lo_, in_=li), r=pk(lb), w=["Lacc"])
                        else:
                            A("dve", lambda e, uo=uo, ui=ui: e.tensor_tensor(out=uo, in0=uo, in1=ui, op=ALU.add), r=pk(ub) + ["Uacc"], w=["Uacc"])
                            A("dve", lambda e, lo_=lo_, li=li: e.tensor_tensor(out=lo_, in0=lo_, in1=li, op=ALU.add), r=pk(lb) + ["Lacc"], w=["Lacc"])
                A("dve", lambda e: e.reciprocal(out=Lacc, in_=Lacc), r=["Lacc"], w=["Lacc"])
                A("dve", lambda e, hp=hp: e.tensor_tensor(out=BR[:, 4 + hp, :], in0=Uacc, in1=Lacc, op=ALU.mult), r=["Uacc", "Lacc"], w=["BR%d" % (4 + hp)])
            S.barrier()

        def phase_merge(l):
            S.mark("phase_merge")
            reset_arenas()
            BR = BRa[:, :].rearrange("p (k t) -> p k t", k=12)
            mT = aR1.bf(32768).rearrange("p (k t) -> p k t", k=16)
            Wb = []
            for br, wd in enumerate((w_ssm_out, w_attn_out, w_conv_out)):
                Wb.append(load_w(br, wd[l].rearrange("(k p) n -> p k n", p=128), 4, 2048, "WS%d" % br))
            gt = [[aT.bf(512) for _ in range(3)] for _ in range(2)]
            m0_ = [aT.f32(512) for _ in range(2)]
            m1_ = [aT.f32(512) for _ in range(2)]
            m2_ = [aT.f32(512) for _ in range(2)]
            n = 0
            for fo in range(16):
                for c in range(NCH):
                    cs = slice(TC * c, TC * (c + 1))
                    b = n % 2
                    n += 1
                    pb = 3 * b
                    for br in range(3):
                        gtile = 52 + 16 * br + fo
                        dma_sp(gt[b][br], PROJ[128 * gtile:128 * (gtile + 1), cs], w=["gt%d%d" % (b, br)])
                        for kt in range(4):
                            A("pe", lambda e, br=br, kt=kt, fo=fo, cs=cs, pb=pb: e.matmul(psb(pb + br), Wb[br][:, kt, 128 * fo:128 * (fo + 1)], BR[:, 4 * br + kt, cs], start=(kt == 0), stop=(kt == 3)),
                              r=["WS%d" % br, "BR%d" % (4 * br + kt)], w=pk(pb + br))
                    m0, m1, m2 = m0_[b], m1_[b], m2_[b]
                    A("dve", lambda e, b=b, pb=pb, m0=m0: e.tensor_tensor(out=m0, in0=psb(pb), in1=gt[b][0], op=ALU.mult), r=pk(pb) + ["gt%d0" % b], w=["m0%d" % b])
                    A("dve", lambda e, b=b, pb=pb, m1=m1: e.tensor_tensor(out=m1, in0=psb(pb + 1), in1=gt[b][1], op=ALU.mult), r=pk(pb + 1) + ["gt%d1" % b], w=["m1%d" % b])
                    A("dve", lambda e, b=b, pb=pb, m2=m2: e.tensor_tensor(out=m2, in0=psb(pb + 2), in1=gt[b][2], op=ALU.mult), r=pk(pb + 2) + ["gt%d2" % b], w=["m2%d" % b])
                    A("pool", lambda e, m0=m0, m1=m1: e.tensor_tensor(out=m0, in0=m0, in1=m1, op=ALU.add), r=["m0%d" % b, "m1%d" % b], w=["m0%d" % b])
                    A("pool", lambda e, fo=fo, cs=cs, m0=m0, m2=m2: e.tensor_tensor(out=mT[:, fo, cs], in0=m0, in1=m2, op=ALU.add), r=["m0%d" % b, "m2%d" % b], w=["mT%d" % fo])
            S.barrier()
            aT.reset()
            aBR.reset()
            yst = [aBR.f32(2048) for _ in range(2)]
            sqf = aBR.f32(2048)
            wv = w_o[l].rearrange("(kt p) n -> p kt n", p=128)
            mkeys = ["mT%d" % k for k in range(16)]
            n = 0
            for cg in range(4):
                slot = cg % 3
                wk = "WS%d" % slot
                W = load_w(slot, wv[:, :, 512 * cg:512 * (cg + 1)], 16, 512, wk)
                for j in range(4):
                    fo = 4 * cg + j
                    hb = 4 * (half_ctr[0] % 2)
                    half_ctr[0] += 1
                    for c in range(NCH):
                        for kt in range(16):
                            A("pe", lambda e, W=W, kt=kt, j=j, c=c, hb=hb: e.matmul(psb(hb + c), W[:, kt, 128 * j:128 * (j + 1)], mT[:, kt, TC * c:TC * (c + 1)], start=(kt == 0), stop=(kt == 15)),
                              r=[wk, "mT%d" % kt], w=pk(hb + c))
                    b = n % 2
                    n += 1
                    A("act", lambda e, b=b, hb=hb: e.activation(out=yst[b], in_=psb(hb, 4), func=AF.Copy), r=pk(hb, 4), w=["yst%d" % b])
                    dma_sp(YT[128 * fo:128 * (fo + 1), :], yst[b], r=["yst%d" % b])
                    if fo == 0:
                        A("act", lambda e, hb=hb: e.activation(out=ACC[:], in_=psb(hb, 4), func=AF.Square), r=pk(hb, 4), w=["ACC"])
                    else:
                        A("act", lambda e, hb=hb: e.activation(out=sqf, in_=psb(hb, 4), func=AF.Square), r=pk(hb, 4), w=["sqf"])
                        A("pool", lambda e: e.tensor_tensor(out=ACC[:], in0=ACC[:], in1=sqf, op=ALU.add), r=["sqf", "ACC"], w=["ACC"])
            finish_rstd()
            S.barrier()

        def finish_rstd():
            for c in range(NCH):
                cs = slice(TC * c, TC * (c + 1))
                A("pe", lambda e, c=c, cs=cs: e.matmul(psb(c), ones32[:], ACC[:, cs], start=True, stop=True), r=["ones32", "ACC"], w=pk(c))
            A("act", lambda e: e.activation(out=RSTDY[:], in_=psb(0, 4), func=AF.Sqrt, bias=epsT[:, 0:1], scale=1.0 / D), r=pk(0, 4) + ["epsT"], w=["RSTDY"])
            A("dve", lambda e: e.reciprocal(out=RSTDY[:], in_=RSTDY[:]), r=["RSTDY"], w=["RSTDY"])

        def phase_ffn(l):
            S.mark("phase_ffn")
            reset_arenas()
            hT = aR1.bf(32768).rearrange("p (k t) -> p k t", k=16)
            dma_sp(ffnwT[:], ffnwT_in[l], w=["ffnwT"])
            abuf = [aBR.f32(2064) for _ in range(2)]
            bbuf = [aBR.f32(2064) for _ in range(2)]
            cb = aBR.f32(2048)
            ca = aT.f32(2048)
            sa = aT.f32(2048)
            gst = [aT.bf(2048) for _ in range(2)]
            for b in range(2):
                A("dve", lambda e, b=b: e.memset(abuf[b][:, 0:16], 0.0), w=["abuf%d" % b])
                A("dve", lambda e, b=b: e.memset(bbuf[b][:, 0:16], 0.0), w=["bbuf%d" % b])
            wv = w_up[l].rearrange("(kt p) n -> p kt n", p=128)
            n = 0
            hn = 0

            def conv3(dst, src, w0, dkey, skey):
                A("dve", lambda e: e.tensor_scalar(out=dst, in0=src[:, 16:2064], scalar1=ffnwT[:, w0:w0 + 1], scalar2=None, op0=ALU.mult), r=[skey, "ffnwT"], w=[dkey])
                A("dve", lambda e: e.scalar_tensor_tensor(out=dst, in0=src[:, 15:2063], scalar=ffnwT[:, w0 + 1:w0 + 2], in1=dst, op0=ALU.mult, op1=ALU.add), r=[skey, "ffnwT", dkey], w=[dkey])
                A("dve", lambda e: e.scalar_tensor_tensor(out=dst, in0=src[:, 14:2062], scalar=ffnwT[:, w0 + 2:w0 + 3], in1=dst, op0=ALU.mult, op1=ALU.add), r=[skey, "ffnwT", dkey], w=[dkey])

            for pg in range(11):
                sla = (2 * pg) % 3
                slb = (2 * pg + 1) % 3
                Wa = load_w(sla, wv[:, :, 512 * pg:512 * (pg + 1)], 16, 512, "WS%d" % sla)
                Wb_ = load_w(slb, wv[:, :, DFF + 512 * pg:DFF + 512 * (pg + 1)], 16, 512, "WS%d" % slb)
                for j in range(4):
                    fa = 4 * pg + j
                    b = n % 2
                    n += 1
                    for th in range(2):
                        hb = 4 * (hn % 2)
                        hn += 1
                        for (W, wk, boff) in ((Wa, "WS%d" % sla, 0), (Wb_, "WS%d" % slb, 2)):
                            for kt in range(16):
                                for c2 in range(2):
                                    tok = slice(1024 * th + 512 * c2, 1024 * th + 512 * (c2 + 1))
                                    A("pe", lambda e, W=W, kt=kt, j=j, tok=tok, bank=hb + boff + c2: e.matmul(psb(bank), W[:, kt, 128 * j:128 * (j + 1)], hT[:, kt, tok], start=(kt == 0), stop=(kt == 15)),
                                      r=[wk, "hT%d" % kt], w=pk(hb + boff + c2))
                        A("act", lambda e, b=b, th=th, hb=hb: e.activation(out=abuf[b][:, 16 + 1024 * th:16 + 1024 * (th + 1)], in_=psb(hb, 2), func=AF.Copy), r=pk(hb, 2), w=["abuf%d" % b])
                        A("act", lambda e, b=b, th=th, hb=hb: e.activation(out=bbuf[b][:, 16 + 1024 * th:16 + 1024 * (th + 1)], in_=psb(hb + 2, 2), func=AF.Copy), r=pk(hb + 2, 2), w=["bbuf%d" % b])
                    conv3(ca, abuf[b], 3 * fa, "ca", "abuf%d" % b)
                    A("act", lambda e: e.activation(out=sa, in_=ca, func=AF.Silu), r=["ca"], w=["sa"])
                    conv3(cb, bbuf[b], 3 * (44 + fa), "cb", "bbuf%d" % b)
                    A("dve", lambda e, b=b: e.tensor_tensor(out=gst[b], in0=sa, in1=cb, op=ALU.mult), r=["sa", "cb"], w=["gst%d" % b])
                    dma_sp(GS[128 * fa:128 * (fa + 1), :], gst[b], r=["gst%d" % b], w=["GS%d" % fa])
            S.barrier()
            reset_arenas()
            g1 = aR1.bf(32768).rearrange("p (k t) -> p k t", k=32)
            g2 = aBR.bf(12288).rearrange("p (k t) -> p k t", k=12)
            yst = [aBR.f32(1024) for _ in range(2)]
            sqf = [aT.f32(1024) for _ in range(2)]
            gv = GS.rearrange("(k p) t -> p k t", p=128)
            wdv = w_down[l].rearrange("(k p) n -> p k n", p=128)
            WD = [WSall[:, 11264 * i:11264 * (i + 1)].rearrange("p (k n) -> p k n", k=22) for i in range(2)]
            n = 0
            nl_ = 0
            for th in range(2):
                ts_ = slice(1024 * th, 1024 * (th + 1))
                for q in range(4):
                    dma_sp(g1[:, 8 * q:8 * (q + 1), :], gv[:, 8 * q:8 * (q + 1), ts_], r=["GS%d" % i for i in range(8 * q, 8 * q + 8)], w=["g1_%d" % q])
                dma_sp(g2[:, 0:6, :], gv[:, 32:38, ts_], r=["GS%d" % i for i in range(32, 38)], w=["g2_0"])
                dma_sp(g2[:, 6:12, :], gv[:, 38:44, ts_], r=["GS%d" % i for i in range(38, 44)], w=["g2_1"])

                def gsrc(kt, c2):
                    if kt < 32:
                        return g1[:, kt, 512 * c2:512 * (c2 + 1)], "g1_%d" % (kt // 8)
                    return g2[:, kt - 32, 512 * c2:512 * (c2 + 1)], "g2_%d" % ((kt - 32) // 6)

                for cg in range(4):
                    for kh in range(2):
                        slot = nl_ % 2
                        nl_ += 1
                        wk = "WD%d" % slot
                        dma_pool(WD[slot], wdv[:, 22 * kh:22 * (kh + 1), 512 * cg:512 * (cg + 1)], w=[wk])
                        for j in range(4):
                            fo = 4 * cg + j
                            bank = 2 * j
                            for k2 in range(22):
                                kt = 22 * kh + k2
                                for c2 in range(2):
                                    gs_, gk = gsrc(kt, c2)
                                    A("pe", lambda e, slot=slot, k2=k2, j=j, gs_=gs_, bank=bank + c2, st_=(kt == 0), sp_=(kt == 43): e.matmul(psb(bank), WD[slot][:, k2, 128 * j:128 * (j + 1)], gs_, start=st_, stop=sp_), r=[wk, gk], w=pk(bank + c2))
                            if kh == 1:
                                b = n % 2
                                n += 1
                                A("act", lambda e, b=b, bank=bank: e.activation(out=yst[b], in_=psb(bank, 2), func=AF.Copy), r=pk(bank, 2), w=["yst%d" % b])
                                dma_sp(YT[128 * fo:128 * (fo + 1), ts_], yst[b], r=["yst%d" % b])
                                if fo == 0:
                                    A("act", lambda e, bank=bank, ts_=ts_: e.activation(out=ACC[:, ts_], in_=psb(bank, 2), func=AF.Square), r=pk(bank, 2), w=["ACC"])
                                else:
                                    A("act", lambda e, b=b, bank=bank: e.activation(out=sqf[b], in_=psb(bank, 2), func=AF.Square), r=pk(bank, 2), w=["sqf%d" % b])
                                    A("pool", lambda e, b=b, ts_=ts_: e.tensor_tensor(out=ACC[:, ts_], in0=ACC[:, ts_], in1=sqf[b], op=ALU.add), r=["sqf%d" % b, "ACC"], w=["ACC"])
            S.barrier()
            finish_rstd()
            S.barrier()

        ident_in = din("ident", [128, 128])

        for l in range(n_layers):
            p = l % 2
            q = (l + 1) % 2
            if l == 0:
                phase_mod(0)
                phase_norm(xT_in, False, None, AG2[0][:, 0:16], modT2[0][:, 0:16], XT, True, 0, 0)
            phase_inproj_ssm(l)
            phase_conv(l)
            phase_attn(l)
            phase_merge(l)
            nxt = gen_mod(l + 1) if l + 1 < n_layers else None
            phase_norm(XT, True, AG2[p][:, 16:32], AG2[p][:, 32:48], modT2[p][:, 48:64], XT, True, p, p, nxt)
            phase_ffn(l)
            if l + 1 < n_layers:
                phase_norm(XT, True, AG2[p][:, 48:64], AG2[q][:, 0:16], modT2[q][:, 0:16], XT, True, p, q)
            else:
                phase_norm(XT, True, AG2[p][:, 48:64], None, None, outT, False, p, p)

        sems = {e: st.enter_context(nc.semaphore("s_" + e)) for e in ENGS}
        dma_sems = {e: [st.enter_context(nc.semaphore("d_%s%d" % (e, i))) for i in range(DMA_K)] for e in ("sp", "pool")}
        S.mark("end")
        build_program.marks = S.marks
        S.prepare(sems, dma_sems)
        block = st.enter_context(nc.Block())

        @block.tensor
        def _(e):
            S.run("pe", e)

        @block.scalar
        def _(e):
            S.run("act", e)

        @block.vector
        def _(e):
            S.run("dve", e)

        @block.gpsimd
        def _(e):
            S.run("pool", e)

        @block.sync
        def _(e):
            S.run("sp", e)
    return nc


def prep_shared(inp):
    f = lambda a: np.ascontiguousarray(np.asarray(a, dtype=np.float32))
    sh = {}
    for k in ("w_mod", "w_in", "w_glu", "w_ssm_out", "w_attn_out", "w_conv_out", "w_o", "w_up", "w_down"):
        sh[k] = f(inp[k])
    sh["bmodT"] = f(np.asarray(inp["b_mod"]).reshape(NL, 96, 128).transpose(0, 2, 1))
    g = np.stack([np.asarray(inp[k]).reshape(NL, 16, 128).transpose(0, 2, 1) for k in ("g_pre_mix", "g_post_mix", "g_pre_ffn", "g_post_ffn")], axis=2)
    sh["gains"] = f(g.reshape(NL, 128, 64))
    sh["bgateT"] = f(np.asarray(inp["b_gate"]).reshape(NL, 48, 128).transpose(0, 2, 1))
    ld = np.repeat(np.asarray(inp["ssm_log_dt"]), 64, axis=1)
    ar = np.asarray(inp["ssm_a_re"]).reshape(NL, 2048)
    ai = np.asarray(inp["ssm_a_im"]).reshape(NL, 2048)
    sh["ssm_flat"] = f(np.stack([ld, ar, ai], axis=1))
    sm = np.stack([a.reshape(NL, 16, 128).transpose(0, 2, 1) for a in (ld, ar, ai)], axis=2)
    sh["ssm_sm"] = f(sm.reshape(NL, 128, 48))
    bre = np.asarray(inp["ssm_b_re"])
    bim = np.asarray(inp["ssm_b_im"])
    cre = np.asarray(inp["ssm_c_re"])
    cim = np.asarray(inp["ssm_c_im"])
    BTre = np.zeros((NL, 128, 16, 128), np.float32)
    BTim = np.zeros((NL, 128, 16, 128), np.float32)
    CTre = np.zeros((NL, 128, 16, 128), np.float32)
    CTim = np.zeros((NL, 128, 16, 128), np.float32)
    for g_ in range(32):
        cs = slice(16 * (g_ % 8), 16 * (g_ % 8) + 16)
        ss = slice(64 * (g_ % 2), 64 * (g_ % 2) + 64)
        BTre[:, cs, g_ // 2, ss] = bre[:, g_].transpose(0, 2, 1)
        BTim[:, cs, g_ // 2, ss] = bim[:, g_].transpose(0, 2, 1)
        CTre[:, ss, g_ // 2, cs] = cre[:, g_].transpose(0, 2, 1)
        CTim[:, ss, g_ // 2, cs] = cim[:, g_].transpose(0, 2, 1)
    sh["BTre"] = BTre.reshape(NL, 128, 2048)
    sh["BTim"] = BTim.reshape(NL, 128, 2048)
    sh["CTre"] = CTre.reshape(NL, 128, 2048)
    sh["CTim"] = CTim.reshape(NL, 128, 2048)
    sh["dT"] = f(np.asarray(inp["ssm_d"]).reshape(NL, 4, 128).transpose(0, 2, 1))
    sh["bgluT"] = f(np.asarray(inp["b_glu"]).reshape(NL, 4, 128).transpose(0, 2, 1))
    sh["convwT"] = f(np.asarray(inp["conv_mix_w"]).reshape(NL, 3, 4, 128).transpose(0, 3, 2, 1).reshape(NL, 128, 12))
    sh["ffnwT"] = f(np.asarray(inp["ffn_conv_w"]).reshape(NL, 3, 88, 128).transpose(0, 3, 2, 1).reshape(NL, 128, 264))
    sh["iota"] = f(np.tile(np.arange(512, dtype=np.float32)[None, :], (128, 1)))
    sh["ones"] = np.ones((128, 128), np.float32)
    sh["ident"] = np.eye(128, dtype=np.float32)
    hi, lo = make_bias_tables()
    sh["bias_hi"] = f(hi)
    sh["bias_lo"] = f(lo)
    return sh


def kernel(**inputs):
    n_layers = int(os.environ.get("K_NLAYERS", NL))
    debug = bool(int(os.environ.get("K_DEBUG", "0")))
    ncores = int(os.environ.get("K_NCORES", 8))
    sh = prep_shared(inputs)
    x = np.asarray(inputs["x"], dtype=np.float32)
    c = np.asarray(inputs["c"], dtype=np.float32)
    in_maps = []
    for b in range(ncores):
        m = dict(sh)
        m["xT"] = np.ascontiguousarray(x[b].T)
        m["cT"] = np.ascontiguousarray(c[b].reshape(16, 128).T)
        in_maps.append(m)
    nc = build_program(n_layers=n_layers, debug=debug)
    res = run_bass_kernel_spmd(nc, in_maps, core_ids=list(range(ncores)))
    if debug:
        kernel.last_results = res.results
    out = np.stack([np.ascontiguousarray(r["outT"].T) for r in res.results], axis=0)
    return out.astype(np.float32)
```
